# Optimizing a Trainium2 kernel written in Bass

```python
import math
import jax, jax.numpy as jnp
from jax import lax
import numpy as np

D_MODEL = 1024
BATCH = 32
SEQ = 2048
DEPTH = 1

D_MIX = D_MODEL
DIFF_WIDTH = D_MIX // 2
DIFF_HEADS = 4
DIFF_QK_DIM = 64
DIFF_V_DIM = DIFF_WIDTH // DIFF_HEADS
NSA_WIDTH = D_MIX - DIFF_WIDTH
NSA_HEADS = 8
NSA_HEAD_DIM = NSA_WIDTH // NSA_HEADS
NSA_KV_HEADS = 2
NSA_GROUP = NSA_HEADS // NSA_KV_HEADS
CMP_BLOCK = 32
CMP_STRIDE = 16
CMP_HIDDEN = 256
SEL_BLOCK = 64
SEL_TOPK = 8
N_LOCAL_BLOCKS = 2
FORCE_BONUS = 1000.0
WINDOW = 512
Q_BLOCK = 128
SEL_Q_BLOCK = 32
RMS_EPS = 1e-6
NEG_BIG = -1e30

COL_DIFF_Q = DIFF_HEADS * 2 * DIFF_QK_DIM
COL_DIFF_K = DIFF_HEADS * 2 * DIFF_QK_DIM
COL_DIFF_V = DIFF_WIDTH
COL_DIFF_Z = DIFF_WIDTH
COL_NSA_Q = NSA_WIDTH
COL_NSA_KV = 6 * NSA_KV_HEADS * NSA_HEAD_DIM
COL_NSA_Z = NSA_WIDTH
COL_NSA_GATE = 3 * NSA_HEADS
IN_COLS = COL_DIFF_Q + COL_DIFF_K + COL_DIFF_V + COL_DIFF_Z + COL_NSA_Q + COL_NSA_KV + COL_NSA_Z + COL_NSA_GATE
IN_SPLITS = [int(v) for v in np.cumsum([COL_DIFF_Q, COL_DIFF_K, COL_DIFF_V, COL_DIFF_Z, COL_NSA_Q, COL_NSA_KV, COL_NSA_Z])]

kernel_name = 'hymba_diffattn_nsa_block'


def rms_norm(x, g):
    xf = x.astype(jnp.float32)
    r = lax.rsqrt(jnp.mean(xf * xf, axis=-1, keepdims=True) + RMS_EPS)
    return (xf * r).astype(x.dtype) * g


def alibi_slopes(n):
    return jnp.asarray([2.0 ** (-8.0 * (h + 1) / n) for h in range(n)], dtype=jnp.float32)


def diff_attention(q, k, v, lam, slopes):
    B, H, _, T, _ = q.shape
    dv = v.shape[-1]
    scale = DIFF_QK_DIM ** -0.5
    kpos = jnp.arange(T)

    def block(start):
        qb = lax.dynamic_slice_in_dim(q, start, Q_BLOCK, axis=3)
        s = jnp.einsum('bhcqd,bhckd->bhcqk', qb, k).astype(jnp.float32) * scale
        qpos = start + jnp.arange(Q_BLOCK)
        dist = (qpos[:, None] - kpos[None, :]).astype(jnp.float32)
        s = s - slopes[None, :, None, None, None] * dist
        s = jnp.where(dist >= 0, s, -jnp.inf)
        p = jax.nn.softmax(s, axis=-1)
        p = p[:, :, 0] - lam * p[:, :, 1]
        return jnp.einsum('bhqk,bhkd->bhqd', p.astype(v.dtype), v)

    out = lax.map(block, jnp.arange(T // Q_BLOCK) * Q_BLOCK)
    return out.transpose(1, 2, 0, 3, 4).reshape(B, H, T, dv)


def compress(kv, pos, w1, b1, w2):
    B, KVH, T, d = kv.shape
    n_cmp = (T - CMP_BLOCK) // CMP_STRIDE + 1
    idx = jnp.arange(n_cmp)[:, None] * CMP_STRIDE + jnp.arange(CMP_BLOCK)[None, :]
    blocks = kv[:, :, idx] + pos
    flat = blocks.reshape(B, KVH, n_cmp, CMP_BLOCK * d)
    return jax.nn.silu(flat @ w1 + b1) @ w2


def compressed_attention(q, kc, vc):
    T = q.shape[3]
    n_cmp = kc.shape[2]
    scale = NSA_HEAD_DIM ** -0.5
    s = jnp.einsum('bhgtd,bhcd->bhgtc', q, kc).astype(jnp.float32) * scale
    cend = jnp.arange(n_cmp) * CMP_STRIDE + CMP_BLOCK - 1
    valid = cend[None, :] <= jnp.arange(T)[:, None]
    p = jax.nn.softmax(jnp.where(valid, s, NEG_BIG), axis=-1)
    p = jnp.where(valid, p, 0.0)
    o = jnp.einsum('bhgtc,bhcd->bhgtd', p.astype(vc.dtype), vc)
    return o, p


def select_blocks(p_cmp):
    T, n_cmp = p_cmp.shape[3], p_cmp.shape[4]
    n_sb = T // SEL_BLOCK
    cstart = jnp.arange(n_cmp) * CMP_STRIDE
    sstart = jnp.arange(n_sb) * SEL_BLOCK
    overlap = ((cstart[:, None] < sstart[None, :] + SEL_BLOCK)
               & (cstart[:, None] + CMP_BLOCK > sstart[None, :])).astype(jnp.float32)
    imp = jnp.einsum('bhgtc,cs->bhts', p_cmp, overlap)
    blk_t = jnp.arange(T)[:, None] // SEL_BLOCK
    j = jnp.arange(n_sb)[None, :]
    valid = j <= blk_t
    forced = valid & ((j == 0) | (j >= blk_t - (N_LOCAL_BLOCKS - 1)))
    score = jnp.where(valid, imp, -1.0) + jnp.where(forced, FORCE_BONUS, 0.0)
    _, idx = lax.top_k(score, min(SEL_TOPK, n_sb))
    return idx


def selected_attention(q, k, v, sel_idx, slopes):
    B, KVH, G, T, d = q.shape
    n = sel_idx.shape[-1]
    scale = NSA_HEAD_DIM ** -0.5
    kb = k.reshape(B, KVH, T // SEL_BLOCK, SEL_BLOCK, d)
    vb = v.reshape(B, KVH, T // SEL_BLOCK, SEL_BLOCK, d)
    bi = jnp.arange(B)[:, None, None, None]
    hi = jnp.arange(KVH)[None, :, None, None]
    offs = jnp.arange(SEL_BLOCK)
    n_keys = n * SEL_BLOCK

    def chunk(start):
        qc = lax.dynamic_slice_in_dim(q, start, SEL_Q_BLOCK, axis=3)
        ic = lax.dynamic_slice_in_dim(sel_idx, start, SEL_Q_BLOCK, axis=2)
        kg = kb[bi, hi, ic].reshape(B, KVH, SEL_Q_BLOCK, n_keys, d)
        vg = vb[bi, hi, ic].reshape(B, KVH, SEL_Q_BLOCK, n_keys, d)
        kpos = (ic[..., None] * SEL_BLOCK + offs).reshape(B, KVH, SEL_Q_BLOCK, n_keys)
        qpos = start + jnp.arange(SEL_Q_BLOCK)
        dist = (qpos[None, None, :, None] - kpos).astype(jnp.float32)[:, :, None]
        s = jnp.einsum('bhgqd,bhqkd->bhgqk', qc, kg).astype(jnp.float32) * scale
        s = s - slopes[None, :, :, None, None] * dist
        s = jnp.where(dist >= 0, s, -jnp.inf)
        p = jax.nn.softmax(s, axis=-1)
        return jnp.einsum('bhgqk,bhqkd->bhgqd', p.astype(vg.dtype), vg)

    out = lax.map(chunk, jnp.arange(T // SEL_Q_BLOCK) * SEL_Q_BLOCK)
    return out.transpose(1, 2, 3, 0, 4, 5).reshape(B, KVH, G, T, d)


def window_attention(q, k, v, slopes):
    B, KVH, G, T, d = q.shape
    scale = NSA_HEAD_DIM ** -0.5
    span = WINDOW + Q_BLOCK
    kp = jnp.pad(k, ((0, 0), (0, 0), (WINDOW, 0), (0, 0)))
    vp = jnp.pad(v, ((0, 0), (0, 0), (WINDOW, 0), (0, 0)))

    def block(start):
        qb = lax.dynamic_slice_in_dim(q, start, Q_BLOCK, axis=3)
        kbk = lax.dynamic_slice_in_dim(kp, start, span, axis=2)
        vbk = lax.dynamic_slice_in_dim(vp, start, span, axis=2)
        qpos = start + jnp.arange(Q_BLOCK)
        kpos = start - WINDOW + jnp.arange(span)
        dist = (qpos[:, None] - kpos[None, :]).astype(jnp.float32)
        valid = (dist >= 0) & (dist < WINDOW) & (kpos[None, :] >= 0)
        s = jnp.einsum('bhgqd,bhkd->bhgqk', qb, kbk).astype(jnp.float32) * scale
        s = s - slopes[None, :, :, None, None] * dist
        s = jnp.where(valid, s, -jnp.inf)
        p = jax.nn.softmax(s, axis=-1)
        return jnp.einsum('bhgqk,bhkd->bhgqd', p.astype(vbk.dtype), vbk)

    out = lax.map(block, jnp.arange(T // Q_BLOCK) * Q_BLOCK)
    return out.transpose(1, 2, 3, 0, 4, 5).reshape(B, KVH, G, T, d)


def setup_inputs(seed: int = 0) -> dict:
    key = jax.random.key(seed)
    ks = jax.random.split(key, 20)
    f32 = jnp.float32
    nrm = lambda k, shape, s: jax.random.normal(k, shape, f32) * s
    return {
        'x': nrm(ks[0], (BATCH, SEQ, D_MODEL), 1.0),
        'norm_g': 1.0 + nrm(ks[1], (DEPTH, D_MODEL), 0.02),
        'w_in': nrm(ks[2], (DEPTH, D_MODEL, IN_COLS), D_MODEL ** -0.5),
        'diff_q_norm_g': 1.0 + nrm(ks[3], (DEPTH, DIFF_QK_DIM), 0.02),
        'diff_k_norm_g': 1.0 + nrm(ks[4], (DEPTH, DIFF_QK_DIM), 0.02),
        'diff_lambda_q1': nrm(ks[5], (DEPTH, DIFF_QK_DIM), 0.1),
        'diff_lambda_k1': nrm(ks[6], (DEPTH, DIFF_QK_DIM), 0.1),
        'diff_lambda_q2': nrm(ks[7], (DEPTH, DIFF_QK_DIM), 0.1),
        'diff_lambda_k2': nrm(ks[8], (DEPTH, DIFF_QK_DIM), 0.1),
        'diff_subln_g': 1.0 + nrm(ks[9], (DEPTH, DIFF_V_DIM), 0.02),
        'nsa_q_norm_g': 1.0 + nrm(ks[10], (DEPTH, NSA_HEAD_DIM), 0.02),
        'nsa_k_norm_g': 1.0 + nrm(ks[11], (DEPTH, 3, NSA_HEAD_DIM), 0.02),
        'cmp_pos': nrm(ks[12], (DEPTH, 2, CMP_BLOCK, NSA_HEAD_DIM), 0.1),
        'cmp_w1': nrm(ks[13], (DEPTH, 2, CMP_BLOCK * NSA_HEAD_DIM, CMP_HIDDEN), (CMP_BLOCK * NSA_HEAD_DIM) ** -0.5),
        'cmp_b1': nrm(ks[14], (DEPTH, 2, CMP_HIDDEN), 0.01),
        'cmp_w2': nrm(ks[15], (DEPTH, 2, CMP_HIDDEN, NSA_HEAD_DIM), CMP_HIDDEN ** -0.5),
        'w_out': nrm(ks[16], (DEPTH, D_MIX, D_MODEL), D_MIX ** -0.5),
    }


def reference(x, norm_g, w_in, diff_q_norm_g, diff_k_norm_g, diff_lambda_q1, diff_lambda_k1,
              diff_lambda_q2, diff_lambda_k2, diff_subln_g, nsa_q_norm_g, nsa_k_norm_g,
              cmp_pos, cmp_w1, cmp_b1, cmp_w2, w_out):
    B, T, _ = x.shape
    f32 = jnp.float32
    diff_slopes = alibi_slopes(DIFF_HEADS)
    nsa_slopes = alibi_slopes(NSA_HEADS).reshape(NSA_KV_HEADS, NSA_GROUP)
    for layer in range(DEPTH):
        lam_init = 0.8 - 0.6 * math.exp(-0.3 * layer)
        h = rms_norm(x, norm_g[layer])
        proj = h @ w_in[layer]
        dq, dk, dv, dz, nq, nkv, nz, ng = jnp.split(proj, IN_SPLITS, axis=-1)

        dq = rms_norm(dq.reshape(B, T, DIFF_HEADS, 2, DIFF_QK_DIM).transpose(0, 2, 3, 1, 4), diff_q_norm_g[layer])
        dk = rms_norm(dk.reshape(B, T, DIFF_HEADS, 2, DIFF_QK_DIM).transpose(0, 2, 3, 1, 4), diff_k_norm_g[layer])
        dv = dv.reshape(B, T, DIFF_HEADS, DIFF_V_DIM).transpose(0, 2, 1, 3)
        lam = (jnp.exp(jnp.sum(diff_lambda_q1[layer].astype(f32) * diff_lambda_k1[layer].astype(f32)))
               - jnp.exp(jnp.sum(diff_lambda_q2[layer].astype(f32) * diff_lambda_k2[layer].astype(f32)))
               + lam_init)
        o_diff = diff_attention(dq, dk, dv, lam, diff_slopes)
        o_diff = rms_norm(o_diff, diff_subln_g[layer]) * (1.0 - lam_init)
        o_diff = o_diff.transpose(0, 2, 1, 3).reshape(B, T, DIFF_WIDTH) * jax.nn.silu(dz)

        nq = rms_norm(nq.reshape(B, T, NSA_KV_HEADS, NSA_GROUP, NSA_HEAD_DIM).transpose(0, 2, 3, 1, 4),
                      nsa_q_norm_g[layer])
        nkv = nkv.reshape(B, T, 6, NSA_KV_HEADS, NSA_HEAD_DIM).transpose(2, 0, 3, 1, 4)
        kc = compress(nkv[0], cmp_pos[layer, 0], cmp_w1[layer, 0], cmp_b1[layer, 0], cmp_w2[layer, 0])
        vc = compress(nkv[1], cmp_pos[layer, 1], cmp_w1[layer, 1], cmp_b1[layer, 1], cmp_w2[layer, 1])
        kc = rms_norm(kc, nsa_k_norm_g[layer, 0])
        k_slc = rms_norm(nkv[2], nsa_k_norm_g[layer, 1])
        k_win = rms_norm(nkv[4], nsa_k_norm_g[layer, 2])
        o_cmp, p_cmp = compressed_attention(nq, kc, vc)
        sel_idx = select_blocks(p_cmp)
        o_slc = selected_attention(nq, k_slc, nkv[3], sel_idx, nsa_slopes)
        o_win = window_attention(nq, k_win, nkv[5], nsa_slopes)
        g = jax.nn.sigmoid(ng.astype(f32)).astype(x.dtype)
        g = g.reshape(B, T, 3, NSA_KV_HEADS, NSA_GROUP).transpose(2, 0, 3, 4, 1)[..., None]
        o_nsa = g[0] * o_cmp + g[1] * o_slc + g[2] * o_win
        o_nsa = o_nsa.transpose(0, 3, 1, 2, 4).reshape(B, T, NSA_WIDTH) * jax.nn.silu(nz)

        x = x + jnp.concatenate([o_diff, o_nsa], axis=-1) @ w_out[layer]
    return x
```

```python
import numpy as np
import ml_dtypes
import concourse.bass as bass
import concourse.mybir as mybir
from concourse.bass_utils import run_bass_kernel_spmd
from contextlib import ExitStack
from collections import deque

F32 = mybir.dt.float32
BF16 = mybir.dt.bfloat16
ALU = mybir.AluOpType
AF = mybir.ActivationFunctionType
AX = mybir.AxisListType
BF = ml_dtypes.bfloat16

NCORES = 8
NB = 4
T = 2048
D = 1024
NEG = -30000.0
EPS = 1e-6
NGRP = 9
NCOL = NGRP * 512
DSL = [2.0 ** (-2.0 * (h + 1)) for h in range(4)]
NSL = [2.0 ** (-1.0 * (h + 1)) for h in range(8)]


class Buf:
    __slots__ = ("name", "last_w", "readers", "psum")

    def __init__(self, name, psum=False):
        self.name = name
        self.last_w = None
        self.readers = {}
        self.psum = psum


class Op:
    __slots__ = ("eng", "fn", "deps", "need_sig", "sem", "val", "is_dma", "idx", "inc")

    def __init__(self, eng, fn, is_dma):
        self.eng = eng
        self.fn = fn
        self.deps = {}
        self.need_sig = is_dma
        self.sem = None
        self.val = 0
        self.is_dma = is_dma
        self.inc = 16 if is_dma else 1


class Prog:
    NDMA = 8

    def __init__(self, nc):
        self.nc = nc
        self.engs = {"pe": nc.tensor, "act": nc.scalar, "dve": nc.vector,
                     "pool": nc.gpsimd, "sp": nc.sync}
        self.ops = {k: [] for k in self.engs}
        self.dma_ops = {k: [] for k in self.engs}
        self.nops = 0

    def _key(self, op):
        return ("d", id(op)) if op.is_dma else op.eng

    def _add_dep(self, op, dep):
        if dep is None or dep is op:
            return
        if (not dep.is_dma) and dep.eng == op.eng and op.eng == "pe" and not op.is_dma:
            return
        k = self._key(dep)
        old = op.deps.get(k)
        if old is None or old.idx < dep.idx:
            op.deps[k] = dep

    def op(self, eng, fn, reads=(), writes=(), is_dma=False):
        o = Op(eng, fn, is_dma)
        o.idx = self.nops
        self.nops += 1
        for b in reads:
            self._add_dep(o, b.last_w)
            if b.psum:
                for r in b.readers.values():
                    if r.eng != eng:
                        self._add_dep(o, r)
        for b in writes:
            self._add_dep(o, b.last_w)
            for r in b.readers.values():
                self._add_dep(o, r)
        for b in reads:
            b.readers[self._key(o)] = o
        for b in writes:
            b.last_w = o
            b.readers = {}
        if is_dma:
            lst = self.dma_ops[eng]
            if len(lst) >= self.NDMA:
                self._add_dep(o, lst[len(lst) - self.NDMA])
            lst.append(o)
        self.ops[eng].append(o)
        return o

    def dma(self, eng, out, in_, reads=(), writes=(), **kw):
        return self.op(eng, lambda e: e.dma_start(out=out, in_=in_, **kw), reads, writes, is_dma=True)

    def emit(self, sems):
        for k, lst in self.ops.items():
            for o in lst:
                for d in o.deps.values():
                    d.need_sig = True
        for k, lst in self.ops.items():
            cnt = 0
            dcnt = 0
            for o in lst:
                if o.is_dma:
                    o.sem = sems["dma_%s_%d" % (k, dcnt % self.NDMA)]
                    o.val = 16 * (dcnt // self.NDMA + 1)
                    dcnt += 1
                elif o.need_sig:
                    cnt += 1
                    o.sem = sems[k]
                    o.val = cnt
        nwait = 0
        for k, lst in self.ops.items():
            e = self.engs[k]
            waited = {}
            for o in lst:
                for d in o.deps.values():
                    sid = id(d.sem)
                    if waited.get(sid, 0) >= d.val:
                        continue
                    waited[sid] = d.val
                    e.wait_ge(d.sem, d.val)
                    nwait += 1
                ins = o.fn(e)
                if o.need_sig:
                    ins.then_inc(o.sem, o.inc)
        return nwait


def make_consts():
    c = {}
    c["ident"] = np.eye(128, dtype=np.float32).astype(BF)
    blk = np.zeros((128, 128), np.float32)
    blk[:64, :64] = 1.0 / 64
    blk[64:, 64:] = 1.0 / 64
    c["blkones"] = blk.astype(BF)
    kk = np.arange(128)[:, None]
    qq = np.arange(128)[None, :]
    c["tri_c"] = np.where(qq >= kk, 0.0, NEG).astype(np.float32).astype(BF)
    c["tri_w"] = np.where(qq < kk, 0.0, NEG).astype(np.float32).astype(BF)
    cc = np.arange(128)[:, None]
    tt = np.arange(T)[None, :]
    c["cmask"] = np.where((16 * cc + 31 <= tt) & (cc < 127), 0.0, NEG).astype(np.float32).astype(BF)
    cstart = np.arange(127) * 16
    sstart = np.arange(32) * 64
    ovl = ((cstart[:, None] < sstart[None, :] + 64) & (cstart[:, None] + 32 > sstart[None, :]))
    oa = np.zeros((128, 33), np.float32)
    oa[:127, :32] = ovl
    oa[:127, 32] = 1.0
    c["ovl"] = oa.astype(BF)
    t = np.arange(T)[:, None]
    j = np.arange(32)[None, :]
    blk_t = t // 64
    valid = j <= blk_t
    forced = valid & ((j == 0) | (j >= blk_t - 1))
    cvalid = valid.astype(np.float32)
    cbias = (cvalid - 1.0) + 1000.0 * forced.astype(np.float32)
    c["cvalid"] = np.ascontiguousarray(cvalid.reshape(16, 128, 32).transpose(1, 0, 2))
    c["cbias"] = np.ascontiguousarray(cbias.reshape(16, 128, 32).transpose(1, 0, 2))
    kpos = np.arange(T)
    kaux = np.stack([128.0 * (kpos // 128), (kpos % 128).astype(np.float64),
                     np.ones(T), np.ones(T)]).astype(np.float32)
    c["kaux"] = kaux.astype(BF)
    c["dkaux"] = np.stack([DSL[h] * kaux for h in range(4)]).astype(np.float32).astype(BF)
    qrel = np.arange(512)
    qa = (qrel // 128).astype(np.float32)
    qb = (qrel % 128).astype(np.float32)
    c["qaux_d"] = np.stack([np.ones(512), np.ones(512), -128.0 * qa, -qb]).astype(np.float32).astype(BF)
    c["qaux_n"] = np.stack([np.stack([np.full(512, s), np.full(512, s), -s * 128.0 * qa, -s * qb])
                            for s in NSL]).astype(np.float32).astype(BF)
    c["expand"] = (np.arange(T)[None, :] // 64 == np.arange(32)[:, None]).astype(np.float32).astype(BF)
    return c


def permute_w_in(w):
    NQ0, NKV, NZ, NG = 2048, 2560, 3328, 3840
    cols = []
    for h in range(4):
        cols += [w[:, h * 128:(h + 1) * 128], w[:, 512 + h * 128:512 + (h + 1) * 128],
                 w[:, 1024 + h * 128:1024 + (h + 1) * 128], w[:, 1536 + h * 128:1536 + (h + 1) * 128]]
    kv = lambda s, k: w[:, NKV + s * 128 + k * 64:NKV + s * 128 + (k + 1) * 64]
    cols += [kv(2, 0), kv(2, 1), kv(4, 0), kv(4, 1), kv(0, 0), kv(0, 0), kv(0, 1), kv(0, 1)]
    cols += [kv(1, 0), kv(1, 0), kv(1, 1), kv(1, 1), kv(3, 0), kv(3, 1), kv(5, 0), kv(5, 1)]
    cols += [w[:, NZ:NZ + 512]]
    for jj in range(4):
        cols += [w[:, NQ0 + jj * 64:NQ0 + (jj + 1) * 64], w[:, NQ0 + (4 + jj) * 64:NQ0 + (5 + jj) * 64]]
    cols += [w[:, NG:NG + 24], np.zeros((w.shape[0], 512 - 24), w.dtype)]
    out = np.ascontiguousarray(np.concatenate(cols, axis=1))
    assert out.shape[1] == NCOL
    return out


def build(nc, es, nb=NB, dbg=False):
    P = Prog(nc)
    total_sb = [0]

    def sb(name, shape, dt):
        n = 1
        for s in shape[1:]:
            n *= s
        total_sb[0] += n * (4 if dt == F32 else 2)
        return es.enter_context(nc.sbuf_tensor(name, shape, dt))

    def din(name, shape, dt):
        return nc.dram_tensor(name, shape, dt, kind="ExternalInput").ap()

    x = din("x", [nb, T, D], F32)
    y = nc.dram_tensor("y", [nb, T, D], F32, kind="ExternalOutput").ap()
    w_in = din("w_in", [D, NCOL], F32)
    w_out = din("w_out", [D, D], F32)
    w1_d = din("cmp_w1", [2, 2048, 256], F32)
    w2_d = din("cmp_w2", [2, 256, 64], F32)
    b1_d = din("cmp_b1", [2, 256], F32)
    pos_d = din("cmp_pos", [2, 32, 64], F32)
    normg_d = din("norm_g", [D], F32)
    dqg_d = din("diff_q_norm_g", [64], F32)
    dkg_d = din("diff_k_norm_g", [64], F32)
    lq1_d = din("diff_lambda_q1", [64], F32)
    lk1_d = din("diff_lambda_k1", [64], F32)
    lq2_d = din("diff_lambda_q2", [64], F32)
    lk2_d = din("diff_lambda_k2", [64], F32)
    subg_d = din("diff_subln_g", [128], F32)
    nqg_d = din("nsa_q_norm_g", [64], F32)
    nkg_d = din("nsa_k_norm_g", [3, 64], F32)
    c_ident = din("c_ident", [128, 128], BF16)
    c_blk = din("c_blkones", [128, 128], BF16)
    c_tric = din("c_tri_c", [128, 128], BF16)
    c_triw = din("c_tri_w", [128, 128], BF16)
    c_cmask = din("c_cmask", [128, T], BF16)
    c_ovl = din("c_ovl", [128, 33], BF16)
    c_cvalid = din("c_cvalid", [128, 16, 32], F32)
    c_cbias = din("c_cbias", [128, 16, 32], F32)
    c_kaux = din("c_kaux", [4, T], BF16)
    c_dkaux = din("c_dkaux", [4, 4, T], BF16)
    c_qauxd = din("c_qaux_d", [4, 512], BF16)
    c_qauxn = din("c_qaux_n", [8, 4, 512], BF16)
    c_expand = din("c_expand", [32, T], BF16)
    if dbg:
        dbg_mix = nc.dram_tensor("dbg_mix", [128, 16, 1024], BF16, kind="ExternalOutput").ap()

    banks = [es.enter_context(nc.psum_tensor("bank%d" % i, [128, 512], F32)) for i in range(8)]
    BK = [Buf("bank%d" % i, psum=True) for i in range(8)]
    pools = {"st": [0, 1, 2], "acc": [3, 4, 5, 6], "misc": [7]}
    pptr = {"st": 0, "acc": 0, "misc": 0}

    def nxt(pool):
        ids = pools[pool]
        i = ids[pptr[pool] % len(ids)]
        pptr[pool] += 1
        return i

    def bbf(i):
        return banks[i][:, :].bitcast(BF16)

    hT = sb("hT", [128, 8, T], BF16)
    HT = Buf("hT")
    mix = sb("mix", [128, 16, 1024], BF16)
    MIX = Buf("mix")
    wbuf = [sb("wbuf%d" % i, [128, 8, 512], BF16) for i in range(2)]
    WB = [Buf("wb%d" % i) for i in range(2)]
    wst = [sb("wst%d" % i, [128, 512], F32) for i in range(2)]
    WS = [Buf("ws%d" % i) for i in range(2)]
    xs = [sb("xs%d" % i, [128, 1024], F32) for i in range(2)]
    XS = [Buf("xs%d" % i) for i in range(2)]
    xn = sb("xn", [128, 1024], BF16)
    XN = Buf("xn")
    junk = sb("junk", [128, 1024], BF16)
    KSA = sb("KSA", [128, T], BF16)
    KSB = sb("KSB", [128, T], BF16)
    KWA = sb("KWA", [128, T], BF16)
    KWB = sb("KWB", [128, T], BF16)
    B_KSA, B_KSB, B_KWA, B_KWB = Buf("KSA"), Buf("KSB"), Buf("KWA"), Buf("KWB")
    QD = [[sb("QD%d%d" % (s, c), [128, 512], BF16) for c in range(2)] for s in range(2)]
    B_QD = [[Buf("QD%d%d" % (s, c)) for c in range(2)] for s in range(2)]
    Vd = sb("Vd", [128, 16, 129], BF16)
    B_Vd = Buf("Vd")
    Zd = sb("Zd", [128, 16, 128], BF16)
    B_Zd = Buf("Zd")
    QN = [sb("QN%d" % h, [128, 512], BF16) for h in range(8)]
    B_QN = [Buf("QN%d" % h) for h in range(8)]
    Vn = sb("Vn", [128, 16, 4, 65], BF16)
    B_Vn = Buf("Vn")
    Zq = sb("Zq", [128, 4, 512], BF16)
    B_Zq = Buf("Zq")
    Gt = sb("Gt", [128, 16, 24], F32)
    B_Gt = Buf("Gt")
    KV2 = sb("KV2", [128, T], BF16)
    B_KV2 = Buf("KV2")
    kc = [sb("kc%d" % k, [128, 128], BF16) for k in range(2)]
    B_kc = [Buf("kc%d" % k) for k in range(2)]
    vc = [sb("vc%d" % k, [128, 65], BF16) for k in range(2)]
    B_vc = [Buf("vc%d" % k) for k in range(2)]
    hid = sb("hid", [128, 2, 128], BF16)
    B_hid = Buf("hid")
    ON = sb("ON", [128, 4, 512], F32)
    B_ON = Buf("ON")
    IA = [sb("IA%d" % k, [128, 4, 32], F32) for k in range(2)]
    B_IA = [Buf("IA%d" % k) for k in range(2)]
    sc32 = sb("sc32", [128, 4, 32], F32)
    B_sc32 = Buf("sc32")
    m8 = sb("m8", [128, 4, 8], F32)
    B_m8 = Buf("m8")
    SBT = sb("SBT", [128, 4, 128], BF16)
    B_SBT = Buf("SBT")
    tmp64 = sb("tmp64", [128, 4, 64], F32)
    B_tmp64 = Buf("tmp64")
    tmp32 = sb("tmp32", [128, 4, 32], F32)
    B_tmp32 = Buf("tmp32")
    pt = [sb("pt%d" % i, [128, 512], BF16) for i in range(4)]
    PTB = [Buf("pt%d" % i) for i in range(4)]
    sqb = sb("sqb", [128, 512], BF16)
    B_sqb = Buf("sqb")
    lnb = sb("lnb", [128, 512], F32)
    B_lnb = Buf("lnb")
    rsb = sb("rsb", [128, 512], F32)
    B_rsb = Buf("rsb")
    t0 = sb("t0", [128, 4, 128], F32)
    t1 = sb("t1", [128, 4, 128], F32)
    B_t0, B_t1 = Buf("t0"), Buf("t1")
    small = sb("small", [128, 64], F32)
    B_small = Buf("small")
    mT = [sb("mT%d" % i, [128, 8, 128], BF16) for i in range(2)]
    B_mT = [Buf("mT%d" % i) for i in range(2)]
    ident = sb("ident", [128, 128], BF16)
    blkones = sb("blkones", [128, 128], BF16)
    tri_c = sb("tri_c", [128, 128], BF16)
    tri_w = sb("tri_w", [128, 128], BF16)
    cmask = sb("cmask", [128, T], BF16)
    ovl = sb("ovl", [128, 33], BF16)
    cvalid = sb("cvalid", [128, 16, 32], F32)
    cbias = sb("cbias", [128, 16, 32], F32)
    normg = sb("normg", [128, 8], F32)
    gcols = sb("gcols", [128, 8], F32)
    gsub = sb("gsub", [128, 128], F32)
    lamt = sb("lamt", [128, 4, 64], F32)
    lamc = sb("lamc", [128, 8], F32)
    w2k = [sb("w2k%d" % k, [128, 2, 128], BF16) for k in range(2)]
    w2v = sb("w2v", [128, 2, 64], BF16)
    w2st = sb("w2st", [128, 2, 2, 64], F32)
    b1c = sb("b1c", [128, 2, 2], F32)
    bias_h = sb("bias_h", [128, 2, 2], F32)
    posst = sb("posst", [128, 2, 16], F32)
    pos2 = sb("pos2", [128, 2, 16], BF16)
    CONST = Buf("const")

    sems_names = ["pe", "act", "dve", "pool", "sp"] + ["dma_sp_%d" % i for i in range(8)]
    sems = {n: es.enter_context(nc.semaphore(n)) for n in sems_names}

    def cload(dst, src, **kw):
        P.dma("sp", dst, src, writes=[CONST], **kw)

    cload(ident[:, :], c_ident)
    cload(blkones[:, :], c_blk)
    cload(tri_c[:, :], c_tric)
    cload(tri_w[:, :], c_triw)
    cload(cmask[:, :], c_cmask)
    cload(ovl[:, :], c_ovl)
    cload(cvalid[:, :, :], c_cvalid)
    cload(cbias[:, :, :], c_cbias)
    cload(normg[:, :], normg_d.rearrange("(m p) -> p m", p=128), allow_slow_non_contiguous=True)

    def colload(col, src):
        for half in range(2):
            cload(gcols[half * 64:(half + 1) * 64, col:col + 1], src.rearrange("(p o) -> p o", o=1))

    colload(0, dqg_d)
    colload(1, dkg_d)
    colload(2, nqg_d)
    for i in range(3):
        colload(3 + i, nkg_d[i, :])

    def bcast_rows(src1d, n):
        return bass.AP(tensor=src1d.tensor, offset=src1d.offset, ap=[[0, 128], [1, n]])

    cload(gsub[:, :], bcast_rows(subg_d, 128))
    for i, l in enumerate([lq1_d, lk1_d, lq2_d, lk2_d]):
        cload(lamt[:, i, :], bcast_rows(l, 64))
    for kv in range(2):
        cload(b1c[:, kv, :], b1_d[kv, :].rearrange("(jt j) -> j jt", j=128), allow_slow_non_contiguous=True)
        cload(w2st[:, kv, :, :], w2_d[kv, :, :].rearrange("(jt j) d -> j jt d", j=128))
        cload(posst[:, kv, :], pos_d[kv, :, :].rearrange("(l2 two) d -> (two d) l2", two=2),
              allow_slow_non_contiguous=True)

    def mz(eng, ap):
        P.op(eng, lambda e: e.memset(ap, 0.0), [], [CONST])

    def m1(eng, ap):
        P.op(eng, lambda e: e.memset(ap, 1.0), [], [CONST])

    for tl in (KSA, KSB, KWA, KWB):
        mz("pool", tl[:, :])
    for s in range(2):
        for c in range(2):
            mz("pool", QD[s][c][:, :])
    for h in range(8):
        mz("pool", QN[h][:, :])
    mz("pool", SBT[:, :, :])
    for k in range(2):
        mz("pool", kc[k][:, :])
        mz("pool", vc[k][:, :])
        mz("pool", w2k[k][:, :, :])
        m1("pool", vc[k][:, 64:65])
    mz("pool", hid[:, :, :])
    m1("pool", Vd[:, :, 128:129])
    m1("pool", Vn[:, :, :, 64:65])
    cload(KSA[96:100, :], c_kaux)
    cload(KSB[32:36, :], c_kaux)
    cload(KSA[64:96, :], c_expand)
    cload(KSB[0:32, :], c_expand)
    for s in range(2):
        cload(QD[s][0][96:100, :], c_qauxd)
        cload(QD[s][1][32:36, :], c_qauxd)
    for h in range(8):
        if h < 4:
            cload(QN[h][96:100, :], c_qauxn[h])
        else:
            cload(QN[h][32:36, :], c_qauxn[h])
    P.op("dve", lambda e: e.tensor_scalar(out=gcols[:, 0:1], in0=gcols[:, 0:1], scalar1=0.125, scalar2=None,
                                          op0=ALU.mult), [CONST], [CONST])
    P.op("dve", lambda e: e.tensor_scalar(out=gcols[:, 2:3], in0=gcols[:, 2:3], scalar1=0.125, scalar2=None,
                                          op0=ALU.mult), [CONST], [CONST])
    P.op("dve", lambda e: e.tensor_scalar(out=gsub[:, :], in0=gsub[:, :], scalar1=0.8, scalar2=None,
                                          op0=ALU.mult), [CONST], [CONST])
    P.op("dve", lambda e: e.tensor_tensor(out=lamt[:, 0, :], in0=lamt[:, 0, :], in1=lamt[:, 1, :], op=ALU.mult),
         [CONST], [CONST])
    P.op("dve", lambda e: e.tensor_tensor(out=lamt[:, 2, :], in0=lamt[:, 2, :], in1=lamt[:, 3, :], op=ALU.mult),
         [CONST], [CONST])
    P.op("dve", lambda e: e.reduce_sum(out=lamc[:, 0:1], in_=lamt[:, 0, :], axis=AX.X), [CONST], [CONST])
    P.op("dve", lambda e: e.reduce_sum(out=lamc[:, 1:2], in_=lamt[:, 2, :], axis=AX.X), [CONST], [CONST])
    P.op("act", lambda e: e.activation(out=lamc[:, 2:4], in_=lamc[:, 0:2], func=AF.Exp), [CONST], [CONST])
    P.op("dve", lambda e: e.scalar_tensor_tensor(out=lamc[:, 4:5], in0=lamc[:, 3:4], scalar=-0.2, in1=lamc[:, 2:3],
                                                 op0=ALU.add, op1=ALU.subtract), [CONST], [CONST])
    for k in range(2):
        P.op("dve", lambda e, k=k: e.tensor_copy(out=w2k[k][:, :, k * 64:(k + 1) * 64], in_=w2st[:, 0, :, :]),
             [CONST], [CONST])
    P.op("dve", lambda e: e.tensor_copy(out=w2v[:, :, :], in_=w2st[:, 1, :, :]), [CONST], [CONST])
    P.op("dve", lambda e: e.tensor_copy(out=pos2[:, :, :], in_=posst[:, :, :]), [CONST], [CONST])

    wctr = [0]
    stc = [0]

    def stage_cast(k, dst_ap, src_ap, scale_ap):
        s = stc[0] % 2
        stc[0] += 1
        P.dma("sp", wst[s][:, :], src_ap, writes=[WS[s]])
        if scale_ap is not None:
            P.op("pool", lambda e: e.tensor_scalar(out=dst_ap, in0=wst[s][:, :], scalar1=scale_ap, scalar2=None,
                                                   op0=ALU.mult), [WS[s], CONST], [WB[k]])
        else:
            P.op("pool", lambda e: e.tensor_copy(out=dst_ap, in_=wst[s][:, :]), [WS[s]], [WB[k]])

    def load_group(g):
        k = wctr[0] % 2
        wctr[0] += 1
        for m in range(8):
            stage_cast(k, wbuf[k][:, m, :], w_in[m * 128:(m + 1) * 128, g * 512:(g + 1) * 512], normg[:, m:m + 1])
        return k

    def load_wout(n):
        k = wctr[0] % 2
        wctr[0] += 1
        for m in range(8):
            stage_cast(k, wbuf[k][:, m, :], w_out[m * 128:(m + 1) * 128, n * 512:(n + 1) * 512], None)
        return k

    def load_w1(kv):
        k = wctr[0] % 2
        wctr[0] += 1
        src = w1_d[kv, :, :].rearrange("(c p) j -> p c j", p=128)
        wv = wbuf[k][:, :, :].rearrange("p m n -> p (m n)").rearrange("p (c j) -> p c j", j=256)
        for i in range(8):
            s = stc[0] % 2
            stc[0] += 1
            P.dma("sp", wst[s][:, :].rearrange("p (c j) -> p c j", j=256), src[:, 2 * i:2 * i + 2, :], writes=[WS[s]])
            P.op("pool", lambda e, s=s, i=i: e.tensor_copy(out=wv[:, 2 * i:2 * i + 2, :],
                                                          in_=wst[s][:, :].rearrange("p (c j) -> p c j", j=256)),
                 [WS[s]], [WB[k]])
        return k, wv

    for kv in range(2):
        k, wv = load_w1(kv)
        bk = nxt("misc")
        for jt in range(2):
            for l2 in range(16):
                P.op("pe", lambda e, jt=jt, l2=l2, wv=wv, kv=kv, bk=bk: e.matmul(
                    banks[bk][:, jt:jt + 1], lhsT=wv[:, l2, jt * 128:(jt + 1) * 128], rhs=pos2[:, kv, l2:l2 + 1],
                    start=(l2 == 0 and jt == 0), stop=True, skip_group_check=True), [WB[k], CONST], [BK[bk]])
        P.op("dve", lambda e, kv=kv, bk=bk: e.tensor_tensor(out=bias_h[:, kv, :], in0=banks[bk][:, 0:2],
                                                            in1=b1c[:, kv, :], op=ALU.add), [BK[bk], CONST], [CONST])

    def qknorm(pb, n, gcol, dsts):
        P.op("act", lambda e: e.activation(out=sqb[:, :n], in_=banks[pb][:, :n], func=AF.Square), [BK[pb]], [B_sqb])
        mb = nxt("misc")
        P.op("pe", lambda e: e.matmul(banks[mb][:, :n], lhsT=blkones[:, :], rhs=sqb[:, :n], start=True, stop=True),
             [B_sqb, CONST], [BK[mb]])
        P.op("act", lambda e: e.activation(out=lnb[:, :n], in_=banks[mb][:, :n], func=AF.Ln, bias=EPS),
             [BK[mb]], [B_lnb])
        P.op("act", lambda e: e.activation(out=rsb[:, :n], in_=lnb[:, :n], func=AF.Exp, scale=-0.5),
             [B_lnb], [B_rsb])
        for (dst, lo, hi, db) in dsts:
            P.op("dve", lambda e, dst=dst, lo=lo, hi=hi: e.scalar_tensor_tensor(
                out=dst, in0=banks[pb][lo:hi, :n], scalar=gcol[lo:hi, 0:1], in1=rsb[lo:hi, :n],
                op0=ALU.mult, op1=ALU.mult), [BK[pb], B_rsb, CONST], [db])

    def proj_fm(k, c0, tq, pb, ncols=512):
        for m in range(8):
            P.op("pe", lambda e, m=m: e.matmul(banks[pb][:, :ncols], lhsT=wbuf[k][:, m, c0:c0 + 128],
                                               rhs=hT[:, m, tq * 512:tq * 512 + ncols], start=(m == 0), stop=(m == 7)),
                 [WB[k], HT], [BK[pb]])

    def proj_tm(k, c0, n, tt, out_ap, pb, first):
        for m in range(8):
            P.op("pe", lambda e, m=m: e.matmul(out_ap, lhsT=hT[:, m, tt * 128:(tt + 1) * 128],
                                               rhs=wbuf[k][:, m, c0:c0 + n], start=(first and m == 0), stop=True,
                                               skip_group_check=True), [WB[k], HT], [BK[pb]])

    pend = deque()
    npv = [0]
    DEPTH = 2
    ptc = [0]

    def drain(limit):
        while pend and npv[0] > limit:
            kind, fn = pend.popleft()
            if kind == "pv":
                npv[0] -= 1
            fn()
        while pend and pend[0][0] == "fin":
            pend.popleft()[1]()

    def flush():
        while pend:
            kind, fn = pend.popleft()
            if kind == "pv":
                npv[0] -= 1
            fn()

    def attention(tiles, Qt, QB, KB, VB, acc_ap, acc_bank, bias, fin):
        started = set()
        for d in tiles:
            sbk = nxt("st")
            c0, c1 = d["c0"], d["c1"]
            ex = d.get("extra")
            P.op("pe", lambda e, d=d, sbk=sbk, c0=c0, c1=c1, ex=ex: e.matmul(
                banks[sbk][:, c0:c1], lhsT=d["k_ap"], rhs=Qt[:, c0:c1], start=True, stop=(ex is None)),
                [KB, QB], [BK[sbk]])
            if ex is not None:
                P.op("pe", lambda e, sbk=sbk, ex=ex: e.matmul(
                    banks[sbk][:, ex[1]:ex[1] + ex[2]], lhsT=ident[:, :], rhs=ex[0], start=False, stop=True),
                    [CONST], [BK[sbk]])
            pi = ptc[0] % 4
            ptc[0] += 1
            P.op("act", lambda e, sbk=sbk, pi=pi, c0=c0, c1=c1: e.activation(
                out=pt[pi][:, c0:c1], in_=banks[sbk][:, c0:c1], func=AF.Exp, bias=float(bias)),
                [BK[sbk]], [PTB[pi]])

            def pv(d=d, pi=pi, c0=c0, c1=c1):
                for s in range(c0 // 128, c1 // 128):
                    bk = acc_bank(s)
                    first = bk not in started
                    started.add(bk)
                    P.op("pe", lambda e, s=s, first=first: e.matmul(
                        acc_ap(s), lhsT=pt[pi][:, s * 128:(s + 1) * 128], rhs=d["v_ap"], start=first, stop=True,
                        skip_group_check=True), [PTB[pi], VB], [BK[bk]])
                if d.get("more") is not None:
                    d["more"](pi)
            pend.append(("pv", pv))
            npv[0] += 1
            drain(DEPTH)
        pend.append(("fin", fin))

    def causal_tiles(Kt, qi, vfn):
        tiles = []
        for kt in range(4 * qi):
            tiles.append(dict(k_ap=Kt[:, kt * 128:(kt + 1) * 128], c0=0, c1=512, v_ap=vfn(kt)))
        for j in range(4):
            kt = 4 * qi + j
            tiles.append(dict(k_ap=Kt[:, kt * 128:(kt + 1) * 128], c0=128 * j, c1=512,
                              extra=(tri_c[:, :], 128 * j, 128), v_ap=vfn(kt)))
        return tiles

    def window_tiles(Kt, qi, vfn):
        tiles = []
        if qi > 0:
            for j in range(4):
                kt = 4 * (qi - 1) + j
                tiles.append(dict(k_ap=Kt[:, kt * 128:(kt + 1) * 128], c0=0, c1=128 * (j + 1),
                                  extra=(tri_w[:, :], 128 * j, 128), v_ap=vfn(kt)))
        for j in range(4):
            kt = 4 * qi + j
            tiles.append(dict(k_ap=Kt[:, kt * 128:(kt + 1) * 128], c0=128 * j, c1=512,
                              extra=(tri_c[:, :], 128 * j, 128), v_ap=vfn(kt)))
        return tiles

    for b in range(nb):
        for tt in range(16):
            xi = tt % 2
            P.dma("sp", xs[xi][:, :], x[b, tt * 128:(tt + 1) * 128, :], writes=[XS[xi]])
            P.op("act", lambda e, xi=xi: e.activation(out=junk[:, :], in_=xs[xi][:, :], func=AF.Square,
                                                      accum_out=small[:, 0:1]), [XS[xi]], [B_small])
            P.op("act", lambda e: e.activation(out=small[:, 1:2], in_=small[:, 0:1], func=AF.Ln, scale=1.0 / D,
                                               bias=EPS), [B_small], [B_small])
            P.op("act", lambda e: e.activation(out=small[:, 2:3], in_=small[:, 1:2], func=AF.Exp, scale=-0.5),
                 [B_small], [B_small])
            P.op("dve", lambda e, xi=xi: e.tensor_scalar(out=xn[:, :], in0=xs[xi][:, :], scalar1=small[:, 2:3],
                                                         scalar2=None, op0=ALU.mult), [XS[xi], B_small], [XN])
            bk = nxt("st")
            for m in range(8):
                P.op("pe", lambda e, m=m, bk=bk: e.transpose(out=bbf(bk)[:, m * 128:(m + 1) * 128],
                                                             in_=xn[:, m * 128:(m + 1) * 128], identity=ident[:, :]),
                     [XN, CONST], [BK[bk]])
            P.op("dve", lambda e, tt=tt, bk=bk: e.tensor_copy(
                out=hT[:, :, tt * 128:(tt + 1) * 128],
                in_=bbf(bk)[:, 0:1024].rearrange("p (m t) -> p m t", t=128)), [BK[bk]], [HT])

        KA, KB_, B_KA, B_KB = KWA, KWB, B_KWA, B_KWB
        for h in range(4):
            flush()
            k = load_group(h)
            P.dma("sp", KA[96:100, :], c_dkaux[h], writes=[B_KA])
            P.dma("sp", KB_[32:36, :], c_dkaux[h], writes=[B_KB])
            for tq in range(4):
                pb = nxt("st")
                proj_fm(k, 128, tq, pb)
                qknorm(pb, 512, gcols[:, 1:2],
                       [(KA[0:64, tq * 512:(tq + 1) * 512], 0, 64, B_KA),
                        (KB_[64:128, tq * 512:(tq + 1) * 512], 64, 128, B_KB)])
            for t4 in range(4):
                pb = nxt("st")
                for i in range(4):
                    proj_tm(k, 256, 128, t4 * 4 + i, banks[pb][:, i * 128:(i + 1) * 128], pb, first=(i == 0))
                P.op("dve", lambda e, t4=t4, pb=pb: e.tensor_copy(
                    out=Vd[:, t4 * 4:(t4 + 1) * 4, 0:128],
                    in_=banks[pb][:, :].rearrange("p (i c) -> p i c", c=128)), [BK[pb]], [B_Vd])
            for t4 in range(4):
                pb = nxt("st")
                for i in range(4):
                    proj_tm(k, 384, 128, t4 * 4 + i, banks[pb][:, i * 128:(i + 1) * 128], pb, first=(i == 0))
                P.op("act", lambda e, t4=t4, pb=pb: e.activation(
                    out=Zd[:, t4 * 4:(t4 + 1) * 4, :], in_=banks[pb][:, :].rearrange("p (i c) -> p i c", c=128),
                    func=AF.Silu), [BK[pb]], [B_Zd])
                P.op("dve", lambda e, t4=t4: e.tensor_tensor(
                    out=Zd[:, t4 * 4:(t4 + 1) * 4, :], in0=Zd[:, t4 * 4:(t4 + 1) * 4, :],
                    in1=gsub[:, :].unsqueeze(1).broadcast_to([128, 4, 128]), op=ALU.mult), [B_Zd, CONST], [B_Zd])
            for qi in range(4):
                qs = qi % 2
                pb = nxt("st")
                proj_fm(k, 0, qi, pb)
                qknorm(pb, 512, gcols[:, 0:1],
                       [(QD[qs][0][0:64, :], 0, 64, B_QD[qs][0]), (QD[qs][1][64:128, :], 64, 128, B_QD[qs][1])])
                for c in range(2):
                    Kt, KBf = (KA, B_KA) if c == 0 else (KB_, B_KB)
                    bx = nxt("acc")
                    by = nxt("acc")
                    tt_ = t0 if c == 0 else t1
                    Bt_ = B_t0 if c == 0 else B_t1

                    def acc_ap(s, bx=bx, by=by):
                        return banks[bx][:, s * 129:(s + 1) * 129] if s < 3 else banks[by][:, 0:129]

                    def acc_bank(s, bx=bx, by=by):
                        return bx if s < 3 else by

                    def fin(c=c, bx=bx, by=by, tt_=tt_, Bt_=Bt_, qi=qi, h=h):
                        xv = banks[bx][:, 0:387].rearrange("p (s c) -> p s c", c=129)
                        P.op("dve", lambda e: e.tensor_copy(out=small[:, 8:11], in_=xv[:, :, 128]),
                             [BK[bx]], [B_small])
                        P.op("dve", lambda e: e.tensor_copy(out=small[:, 11:12], in_=banks[by][:, 128:129]),
                             [BK[by]], [B_small])
                        P.op("dve", lambda e: e.reciprocal(out=small[:, 12:16], in_=small[:, 8:12]),
                             [B_small], [B_small])
                        P.op("dve", lambda e: e.tensor_tensor(
                            out=tt_[:, 0:3, :], in0=xv[:, :, 0:128],
                            in1=small[:, 12:15].unsqueeze(2).broadcast_to([128, 3, 128]), op=ALU.mult),
                            [BK[bx], B_small], [Bt_])
                        P.op("dve", lambda e: e.tensor_scalar(
                            out=tt_[:, 3, :], in0=banks[by][:, 0:128], scalar1=small[:, 15:16], scalar2=None,
                            op0=ALU.mult), [BK[by], B_small], [Bt_])
                        if c == 1:
                            P.op("dve", lambda e: e.scalar_tensor_tensor(
                                out=t0[:, :, :], in0=t1[:, :, :], scalar=lamc[:, 4:5], in1=t0[:, :, :],
                                op0=ALU.mult, op1=ALU.add), [B_t0, B_t1, CONST], [B_t0])
                            P.op("dve", lambda e: e.tensor_tensor(out=t1[:, :, :], in0=t0[:, :, :], in1=t0[:, :, :],
                                                                  op=ALU.mult), [B_t0], [B_t1])
                            P.op("dve", lambda e: e.reduce_sum(out=small[:, 16:20], in_=t1[:, :, :], axis=AX.X),
                                 [B_t1], [B_small])
                            P.op("act", lambda e: e.activation(out=small[:, 20:24], in_=small[:, 16:20], func=AF.Ln,
                                                               scale=1.0 / 128, bias=EPS), [B_small], [B_small])
                            P.op("act", lambda e: e.activation(out=small[:, 24:28], in_=small[:, 20:24], func=AF.Exp,
                                                               scale=-0.5), [B_small], [B_small])
                            P.op("dve", lambda e: e.tensor_tensor(
                                out=t0[:, :, :], in0=t0[:, :, :],
                                in1=small[:, 24:28].unsqueeze(2).broadcast_to([128, 4, 128]), op=ALU.mult),
                                [B_t0, B_small], [B_t0])
                            P.op("dve", lambda e: e.tensor_tensor(
                                out=mix[:, qi * 4:(qi + 1) * 4, h * 128:(h + 1) * 128], in0=t0[:, :, :],
                                in1=Zd[:, qi * 4:(qi + 1) * 4, :], op=ALU.mult), [B_t0, B_Zd], [MIX])

                    attention(causal_tiles(Kt, qi, lambda kt: Vd[:, kt, 0:129]), QD[qs][c], B_QD[qs][c], KBf, B_Vd,
                              acc_ap, acc_bank, -DSL[h] * 512.0 * qi, fin)
        flush()

        P.dma("sp", KWA[96:100, :], c_kaux, writes=[B_KWA])
        P.dma("sp", KWB[32:36, :], c_kaux, writes=[B_KWB])
        k = load_group(4)
        for tq in range(4):
            pb = nxt("st")
            proj_fm(k, 0, tq, pb)
            qknorm(pb, 512, gcols[:, 4:5], [(KSA[0:64, tq * 512:(tq + 1) * 512], 0, 64, B_KSA),
                                            (KSB[64:128, tq * 512:(tq + 1) * 512], 64, 128, B_KSB)])
            pb = nxt("st")
            proj_fm(k, 128, tq, pb)
            qknorm(pb, 512, gcols[:, 5:6], [(KWA[0:64, tq * 512:(tq + 1) * 512], 0, 64, B_KWA),
                                            (KWB[64:128, tq * 512:(tq + 1) * 512], 64, 128, B_KWB)])

        def kv2_fill(kw, c0):
            for tq in range(4):
                pb = nxt("st")
                proj_fm(kw, c0, tq, pb)
                P.op("dve", lambda e, tq=tq, pb=pb: e.tensor_copy(out=KV2[0:64, tq * 512:(tq + 1) * 512],
                                                                  in_=banks[pb][0:64, :]), [BK[pb]], [B_KV2])
                if tq == 0:
                    P.op("dve", lambda e, pb=pb: e.tensor_copy(out=KV2[64:128, 0:511], in_=banks[pb][64:128, 1:512]),
                         [BK[pb]], [B_KV2])
                else:
                    P.op("dve", lambda e, tq=tq, pb=pb: e.tensor_copy(
                        out=KV2[64:128, tq * 512 - 1:tq * 512 + 511], in_=banks[pb][64:128, :]), [BK[pb]], [B_KV2])

        def compress_hidden(kw1, wv, kv):
            for jt in range(2):
                pb = nxt("st")
                for l2 in range(16):
                    P.op("pe", lambda e, l2=l2, jt=jt, pb=pb: e.matmul(
                        banks[pb][:, 0:127], lhsT=wv[:, l2, jt * 128:(jt + 1) * 128],
                        rhs=KV2[:, 2 * l2:2 * l2 + 16 * 126 + 1:16], start=(l2 == 0), stop=(l2 == 15)),
                        [WB[kw1], B_KV2], [BK[pb]])
                P.op("act", lambda e, jt=jt, pb=pb: e.activation(out=hid[:, jt, 0:127], in_=banks[pb][:, 0:127],
                                                                 func=AF.Silu, bias=bias_h[:, kv, jt:jt + 1]),
                     [BK[pb], CONST], [B_hid])

        kw1, wv1 = load_w1(0)
        for kvh in range(2):
            kv2_fill(k, 256 + kvh * 128)
            compress_hidden(kw1, wv1, 0)
            pb = nxt("st")
            for jt in range(2):
                P.op("pe", lambda e, jt=jt, pb=pb, kvh=kvh: e.matmul(
                    banks[pb][:, 0:127], lhsT=w2k[kvh][:, jt, :], rhs=hid[:, jt, 0:127], start=(jt == 0),
                    stop=(jt == 1)), [B_hid, CONST], [BK[pb]])
            qknorm(pb, 127, gcols[:, 3:4], [(kc[kvh][0:64, 0:127], 0, 64, B_kc[kvh]),
                                            (kc[kvh][64:128, 0:127], 64, 128, B_kc[kvh])])
        k = load_group(5)
        kw1, wv1 = load_w1(1)
        for kvh in range(2):
            kv2_fill(k, kvh * 128)
            compress_hidden(kw1, wv1, 1)
            pb = nxt("st")
            for jt in range(2):
                P.op("pe", lambda e, jt=jt, pb=pb: e.matmul(
                    banks[pb][0:127, 0:64], lhsT=hid[:, jt, 0:127], rhs=w2v[:, jt, :], start=(jt == 0),
                    stop=(jt == 1)), [B_hid, CONST], [BK[pb]])
            P.op("dve", lambda e, pb=pb, kvh=kvh: e.tensor_copy(out=vc[kvh][0:127, 0:64], in_=banks[pb][0:127, 0:64]),
                 [BK[pb]], [B_vc[kvh]])
        for t2 in range(8):
            pb = nxt("st")
            for i in range(2):
                proj_tm(k, 256, 256, t2 * 2 + i, banks[pb][:, i * 256:(i + 1) * 256], pb, first=(i == 0))
            P.op("dve", lambda e, t2=t2, pb=pb: e.tensor_copy(
                out=Vn[:, t2 * 2:t2 * 2 + 2, :, 0:64],
                in_=banks[pb][:, :].rearrange("p (i g c) -> p i g c", g=4, c=64)), [BK[pb]], [B_Vn])
        k = load_group(8)
        pb = nxt("st")
        for tt in range(16):
            proj_tm(k, 0, 24, tt, banks[pb][:, tt * 24:(tt + 1) * 24], pb, first=(tt == 0))
        P.op("act", lambda e, pb=pb: e.activation(out=Gt[:, :, :],
                                                  in_=banks[pb][:, 0:384].rearrange("p (t g) -> p t g", g=24),
                                                  func=AF.Sigmoid), [BK[pb]], [B_Gt])
        kq = load_group(7)
        kz = load_group(6)
        for qi in range(4):
            for i in range(4):
                pb = nxt("st")
                proj_tm(kz, 0, 512, qi * 4 + i, banks[pb][:, :], pb, first=True)
                P.op("act", lambda e, i=i, pb=pb: e.activation(out=Zq[:, i, :], in_=banks[pb][:, :], func=AF.Silu),
                     [BK[pb]], [B_Zq])
            for j in range(4):
                pb = nxt("st")
                proj_fm(kq, j * 128, qi, pb)
                qknorm(pb, 512, gcols[:, 2:3], [(QN[j][0:64, :], 0, 64, B_QN[j]),
                                                (QN[4 + j][64:128, :], 64, 128, B_QN[4 + j])])

            def nsa_fin_factory(hd, br, ab, qi=qi):
                def fin():
                    av = banks[ab][:, 0:260].rearrange("p (s c) -> p s c", c=65)
                    o = 32 + (hd % 2) * 16
                    P.op("dve", lambda e: e.tensor_scalar(out=small[:, o:o + 4], in0=av[:, :, 64], scalar1=1e-30,
                                                          scalar2=None, op0=ALU.max), [BK[ab]], [B_small])
                    P.op("dve", lambda e: e.reciprocal(out=small[:, o + 4:o + 8], in_=small[:, o:o + 4]),
                         [B_small], [B_small])
                    P.op("dve", lambda e: e.tensor_tensor(out=small[:, o + 8:o + 12], in0=small[:, o + 4:o + 8],
                                                          in1=Gt[:, qi * 4:(qi + 1) * 4, br * 8 + hd], op=ALU.mult),
                         [B_small, B_Gt], [B_small])
                    cf = small[:, o + 8:o + 12].unsqueeze(2).broadcast_to([128, 4, 64])
                    if br == 0:
                        P.op("dve", lambda e: e.tensor_tensor(out=ON[:, :, hd * 64:(hd + 1) * 64],
                                                              in0=av[:, :, 0:64], in1=cf, op=ALU.mult),
                             [BK[ab], B_small], [B_ON])
                    else:
                        P.op("dve", lambda e: e.tensor_tensor(out=tmp64[:, :, :], in0=av[:, :, 0:64], in1=cf,
                                                              op=ALU.mult), [BK[ab], B_small], [B_tmp64])
                        P.op("dve", lambda e: e.tensor_tensor(out=ON[:, :, hd * 64:(hd + 1) * 64],
                                                              in0=ON[:, :, hd * 64:(hd + 1) * 64], in1=tmp64[:, :, :],
                                                              op=ALU.add), [B_tmp64, B_ON], [B_ON])
                return fin

            for hd in range(8):
                kvh = hd // 4
                ab = nxt("acc")
                ib = nxt("acc")

                def more(pi, ib=ib):
                    for s in range(4):
                        P.op("pe", lambda e, s=s: e.matmul(
                            banks[ib][:, s * 33:(s + 1) * 33], lhsT=pt[pi][:, s * 128:(s + 1) * 128], rhs=ovl[:, :],
                            start=(s == 0), stop=True, skip_group_check=True), [PTB[pi], CONST], [BK[ib]])

                base_fin = nsa_fin_factory(hd, 0, ab)

                def fin(base_fin=base_fin, ib=ib, hd=hd, kvh=kvh):
                    base_fin()
                    iv = banks[ib][:, 0:132].rearrange("p (s c) -> p s c", c=33)
                    P.op("dve", lambda e: e.tensor_scalar(out=small[:, 28:32], in0=iv[:, :, 32], scalar1=1e-30,
                                                          scalar2=None, op0=ALU.max), [BK[ib]], [B_small])
                    P.op("dve", lambda e: e.reciprocal(out=small[:, 4:8], in_=small[:, 28:32]), [B_small], [B_small])
                    rb = small[:, 4:8].unsqueeze(2).broadcast_to([128, 4, 32])
                    if hd % 4 == 0:
                        P.op("dve", lambda e: e.tensor_tensor(out=IA[kvh][:, :, :], in0=iv[:, :, 0:32], in1=rb,
                                                              op=ALU.mult), [BK[ib], B_small], [B_IA[kvh]])
                    else:
                        P.op("dve", lambda e: e.tensor_tensor(out=tmp32[:, :, :], in0=iv[:, :, 0:32], in1=rb,
                                                              op=ALU.mult), [BK[ib], B_small], [B_tmp32])
                        P.op("dve", lambda e: e.tensor_tensor(out=IA[kvh][:, :, :], in0=IA[kvh][:, :, :],
                                                              in1=tmp32[:, :, :], op=ALU.add),
                             [B_tmp32, B_IA[kvh]], [B_IA[kvh]])

                tiles = [dict(k_ap=kc[kvh][:, :], c0=0, c1=512, extra=(cmask[:, qi * 512:(qi + 1) * 512], 0, 512),
                              v_ap=vc[kvh][:, 0:65], more=more)]
                attention(tiles, QN[hd], B_QN[hd], B_kc[kvh], B_vc[kvh],
                          lambda s, ab=ab: banks[ab][:, s * 65:(s + 1) * 65], lambda s, ab=ab: ab, 0.0, fin)
            flush()
            for kvh in range(2):
                off = 64 if kvh == 0 else 0
                P.op("dve", lambda e, kvh=kvh, qi=qi: e.tensor_tensor(out=sc32[:, :, :], in0=IA[kvh][:, :, :],
                                                                      in1=cvalid[:, qi * 4:(qi + 1) * 4, :], op=ALU.mult),
                     [B_IA[kvh], CONST], [B_sc32])
                P.op("dve", lambda e, qi=qi: e.tensor_tensor(out=sc32[:, :, :], in0=sc32[:, :, :],
                                                             in1=cbias[:, qi * 4:(qi + 1) * 4, :], op=ALU.add),
                     [B_sc32, CONST], [B_sc32])
                for s in range(4):
                    P.op("dve", lambda e, s=s: e.max(out=m8[:, s, :], in_=sc32[:, s, :]), [B_sc32], [B_m8])
                for s in range(4):
                    P.op("dve", lambda e, s=s, off=off: e.tensor_scalar(
                        out=SBT[:, s, off:off + 32], in0=sc32[:, s, :], scalar1=m8[:, s, 7:8], scalar2=NEG,
                        op0=ALU.is_lt, op1=ALU.mult), [B_sc32, B_m8], [B_SBT])
            tb = nxt("st")
            for s in range(4):
                P.op("pe", lambda e, s=s, tb=tb: e.transpose(out=bbf(tb)[:, s * 128:(s + 1) * 128], in_=SBT[:, s, :],
                                                             identity=ident[:, :]), [B_SBT, CONST], [BK[tb]])
            for j in range(4):
                P.op("dve", lambda e, j=j, tb=tb: e.tensor_copy(out=QN[j][64:96, :], in_=bbf(tb)[64:96, 0:512]),
                     [BK[tb]], [B_QN[j]])
                P.op("dve", lambda e, j=j, tb=tb: e.tensor_copy(out=QN[4 + j][0:32, :], in_=bbf(tb)[0:32, 0:512]),
                     [BK[tb]], [B_QN[4 + j]])
            for hd in range(8):
                kvh = hd // 4
                Ks, BKs = (KSA, B_KSA) if kvh == 0 else (KSB, B_KSB)
                Kw, BKw = (KWA, B_KWA) if kvh == 0 else (KWB, B_KWB)
                ab = nxt("acc")
                attention(causal_tiles(Ks, qi, lambda kt, kvh=kvh: Vn[:, kt, kvh, :]), QN[hd], B_QN[hd], BKs, B_Vn,
                          lambda s, ab=ab: banks[ab][:, s * 65:(s + 1) * 65], lambda s, ab=ab: ab,
                          -NSL[hd] * 512.0 * qi, nsa_fin_factory(hd, 1, ab))
                ab = nxt("acc")
                attention(window_tiles(Kw, qi, lambda kt, kvh=kvh: Vn[:, kt, 2 + kvh, :]), QN[hd], B_QN[hd], BKw, B_Vn,
                          lambda s, ab=ab: banks[ab][:, s * 65:(s + 1) * 65], lambda s, ab=ab: ab,
                          -NSL[hd] * 512.0 * qi, nsa_fin_factory(hd, 2, ab))
            flush()
            P.op("dve", lambda e, qi=qi: e.tensor_tensor(out=mix[:, qi * 4:(qi + 1) * 4, 512:1024], in0=ON[:, :, :],
                                                         in1=Zq[:, :, :], op=ALU.mult), [B_ON, B_Zq], [MIX])

        if dbg and b == 0:
            P.dma("sp", dbg_mix, mix[:, :, :], reads=[MIX])

        ko = [load_wout(0), load_wout(1)]
        for tt in range(16):
            mi = tt % 2
            tb = nxt("st")
            for c in range(8):
                P.op("pe", lambda e, c=c, tb=tb, tt=tt: e.transpose(
                    out=bbf(tb)[:, c * 128:(c + 1) * 128], in_=mix[:, tt, c * 128:(c + 1) * 128],
                    identity=ident[:, :]), [MIX, CONST], [BK[tb]])
            P.op("dve", lambda e, mi=mi, tb=tb: e.tensor_copy(
                out=mT[mi][:, :, :], in_=bbf(tb)[:, 0:1024].rearrange("p (c t) -> p c t", t=128)),
                [BK[tb]], [B_mT[mi]])
            xi = tt % 2
            P.dma("sp", xs[xi][:, :], x[b, tt * 128:(tt + 1) * 128, :], writes=[XS[xi]])
            for n in range(2):
                ob = nxt("acc")
                for c in range(8):
                    P.op("pe", lambda e, c=c, n=n, ob=ob, mi=mi, ko=ko: e.matmul(
                        banks[ob][:, :], lhsT=mT[mi][:, c, :], rhs=wbuf[ko[n]][:, c, :], start=(c == 0),
                        stop=(c == 7)), [B_mT[mi], WB[ko[n]]], [BK[ob]])
                P.op("dve", lambda e, n=n, ob=ob, xi=xi: e.tensor_tensor(
                    out=xs[xi][:, n * 512:(n + 1) * 512], in0=banks[ob][:, :], in1=xs[xi][:, n * 512:(n + 1) * 512],
                    op=ALU.add), [BK[ob], XS[xi]], [XS[xi]])
            P.dma("sp", y[b, tt * 128:(tt + 1) * 128, :], xs[xi][:, :], reads=[XS[xi]])

    fin_op = P.op("sp", lambda e: e.nop(), [], [])
    for d in P.dma_ops["sp"][-16:]:
        P._add_dep(fin_op, d)
    nw = P.emit(sems)
    return dict(nops=P.nops, nwaits=nw, sbuf_bytes=total_sb[0])


_CACHE = {}


def kernel(x, norm_g, w_in, diff_q_norm_g, diff_k_norm_g, diff_lambda_q1, diff_lambda_k1, diff_lambda_q2,
           diff_lambda_k2, diff_subln_g, nsa_q_norm_g, nsa_k_norm_g, cmp_pos, cmp_w1, cmp_b1, cmp_w2, w_out,
           _dbg=False, _nb=NB, _ncores=NCORES):
    f = lambda a: np.ascontiguousarray(np.asarray(a, dtype=np.float32))
    x = f(x)
    consts = make_consts()
    shared = {
        "w_in": permute_w_in(f(w_in)[0]), "w_out": f(w_out)[0], "cmp_w1": f(cmp_w1)[0], "cmp_w2": f(cmp_w2)[0],
        "cmp_b1": f(cmp_b1)[0], "cmp_pos": f(cmp_pos)[0], "norm_g": f(norm_g)[0],
        "diff_q_norm_g": f(diff_q_norm_g)[0], "diff_k_norm_g": f(diff_k_norm_g)[0],
        "diff_lambda_q1": f(diff_lambda_q1)[0], "diff_lambda_k1": f(diff_lambda_k1)[0],
        "diff_lambda_q2": f(diff_lambda_q2)[0], "diff_lambda_k2": f(diff_lambda_k2)[0],
        "diff_subln_g": f(diff_subln_g)[0], "nsa_q_norm_g": f(nsa_q_norm_g)[0], "nsa_k_norm_g": f(nsa_k_norm_g)[0],
    }
    for k_, v in consts.items():
        shared["c_" + k_] = v
    nc = bass.Bass("TRN2", target_bir_lowering=False)
    with ExitStack() as es:
        info = build(nc, es, nb=_nb, dbg=_dbg)
    in_maps = []
    for c in range(_ncores):
        m = dict(shared)
        m["x"] = np.ascontiguousarray(x[c * _nb:(c + 1) * _nb])
        in_maps.append(m)
    res = run_bass_kernel_spmd(nc, in_maps, core_ids=list(range(_ncores)))
    out = np.concatenate([np.asarray(r["y"], dtype=np.float32) for r in res.results], axis=0)
    if _dbg:
        return out, res.results[0]["dbg_mix"], info
    return out
```

```python
import numpy as np
import ml_dtypes
import concourse.bass as bass
import concourse.mybir as mybir
from concourse.bass_utils import run_bass_kernel_spmd
from contextlib import ExitStack
from collections import deque

F32 = mybir.dt.float32
BF16 = mybir.dt.bfloat16
ALU = mybir.AluOpType
AF = mybir.ActivationFunctionType
AX = mybir.AxisListType
BF = ml_dtypes.bfloat16

NCORES = 8
NB = 4
T = 2048
D = 1024
NEG = -30000.0
EPS = 1e-6
NGRP = 9
NCOL = NGRP * 512
DSL = [2.0 ** (-2.0 * (h + 1)) for h in range(4)]
NSL = [2.0 ** (-1.0 * (h + 1)) for h in range(8)]


class Buf:
    __slots__ = ("name", "last_w", "readers", "psum")

    def __init__(self, name, psum=False):
        self.name = name
        self.last_w = None
        self.readers = {}
        self.psum = psum


class Op:
    __slots__ = ("eng", "fn", "deps", "need_sig", "sem", "val", "is_dma", "idx", "inc")

    def __init__(self, eng, fn, is_dma):
        self.eng = eng
        self.fn = fn
        self.deps = {}
        self.need_sig = is_dma
        self.sem = None
        self.val = 0
        self.is_dma = is_dma
        self.inc = 16 if is_dma else 1


class Prog:
    NDMA = 8

    def __init__(self, nc):
        self.nc = nc
        self.engs = {"pe": nc.tensor, "act": nc.scalar, "dve": nc.vector,
                     "pool": nc.gpsimd, "sp": nc.sync}
        self.ops = {k: [] for k in self.engs}
        self.dma_ops = {k: [] for k in self.engs}
        self.nops = 0

    def _key(self, op):
        return ("d", id(op)) if op.is_dma else op.eng

    def _add_dep(self, op, dep):
        if dep is None or dep is op:
            return
        if (not dep.is_dma) and dep.eng == op.eng and op.eng == "pe" and not op.is_dma:
            return
        k = self._key(dep)
        old = op.deps.get(k)
        if old is None or old.idx < dep.idx:
            op.deps[k] = dep

    def op(self, eng, fn, reads=(), writes=(), is_dma=False):
        o = Op(eng, fn, is_dma)
        o.idx = self.nops
        self.nops += 1
        for b in reads:
            self._add_dep(o, b.last_w)
            if b.psum:
                for r in b.readers.values():
                    if r.eng != eng:
                        self._add_dep(o, r)
        for b in writes:
            self._add_dep(o, b.last_w)
            for r in b.readers.values():
                self._add_dep(o, r)
        for b in reads:
            b.readers[self._key(o)] = o
        for b in writes:
            b.last_w = o
            b.readers = {}
        if is_dma:
            lst = self.dma_ops[eng]
            if len(lst) >= self.NDMA:
                self._add_dep(o, lst[len(lst) - self.NDMA])
            lst.append(o)
        self.ops[eng].append(o)
        return o

    def dma(self, eng, out, in_, reads=(), writes=(), **kw):
        return self.op(eng, lambda e: e.dma_start(out=out, in_=in_, **kw), reads, writes, is_dma=True)

    def emit(self, sems):
        for k, lst in self.ops.items():
            for o in lst:
                for d in o.deps.values():
                    d.need_sig = True
        for k, lst in self.ops.items():
            cnt = 0
            dcnt = 0
            for o in lst:
                if o.is_dma:
                    o.sem = sems["dma_%s_%d" % (k, dcnt % self.NDMA)]
                    o.val = 16 * (dcnt // self.NDMA + 1)
                    dcnt += 1
                elif o.need_sig:
                    cnt += 1
                    o.sem = sems[k]
                    o.val = cnt
        nwait = 0
        for k, lst in self.ops.items():
            e = self.engs[k]
            waited = {}
            for o in lst:
                for d in o.deps.values():
                    sid = id(d.sem)
                    if waited.get(sid, 0) >= d.val:
                        continue
                    waited[sid] = d.val
                    e.wait_ge(d.sem, d.val)
                    nwait += 1
                ins = o.fn(e)
                if o.need_sig:
                    ins.then_inc(o.sem, o.inc)
        return nwait


def make_consts():
    c = {}
    c["ident"] = np.eye(128, dtype=np.float32).astype(BF)
    blk = np.zeros((128, 128), np.float32)
    blk[:64, :64] = 1.0 / 64
    blk[64:, 64:] = 1.0 / 64
    c["blkones"] = blk.astype(BF)
    kk = np.arange(128)[:, None]
    qq = np.arange(128)[None, :]
    c["tri_c"] = np.where(qq >= kk, 0.0, NEG).astype(np.float32).astype(BF)
    c["tri_w"] = np.where(qq < kk, 0.0, NEG).astype(np.float32).astype(BF)
    cc = np.arange(128)[:, None]
    tt = np.arange(T)[None, :]
    c["cmask"] = np.where((16 * cc + 31 <= tt) & (cc < 127), 0.0, NEG).astype(np.float32).astype(BF)
    cstart = np.arange(127) * 16
    sstart = np.arange(32) * 64
    ovl = ((cstart[:, None] < sstart[None, :] + 64) & (cstart[:, None] + 32 > sstart[None, :]))
    oa = np.zeros((128, 33), np.float32)
    oa[:127, :32] = ovl
    oa[:127, 32] = 1.0
    c["ovl"] = oa.astype(BF)
    t = np.arange(T)[:, None]
    j = np.arange(32)[None, :]
    blk_t = t // 64
    valid = j <= blk_t
    forced = valid & ((j == 0) | (j >= blk_t - 1))
    cvalid = valid.astype(np.float32)
    cbias = (cvalid - 1.0) + 1000.0 * forced.astype(np.float32)
    c["cvalid"] = np.ascontiguousarray(cvalid.reshape(16, 128, 32).transpose(1, 0, 2))
    c["cbias"] = np.ascontiguousarray(cbias.reshape(16, 128, 32).transpose(1, 0, 2))
    kpos = np.arange(T)
    kaux = np.stack([128.0 * (kpos // 128), (kpos % 128).astype(np.float64),
                     np.ones(T), np.ones(T)]).astype(np.float32)
    c["kaux"] = kaux.astype(BF)
    c["dkaux"] = np.stack([DSL[h] * kaux for h in range(4)]).astype(np.float32).astype(BF)
    qrel = np.arange(512)
    qa = (qrel // 128).astype(np.float32)
    qb = (qrel % 128).astype(np.float32)
    c["qaux_d"] = np.stack([np.ones(512), np.ones(512), -128.0 * qa, -qb]).astype(np.float32).astype(BF)
    c["qaux_n"] = np.stack([np.stack([np.full(512, s), np.full(512, s), -s * 128.0 * qa, -s * qb])
                            for s in NSL]).astype(np.float32).astype(BF)
    c["expand"] = (np.arange(T)[None, :] // 64 == np.arange(32)[:, None]).astype(np.float32).astype(BF)
    return c


def permute_w_in(w):
    NQ0, NKV, NZ, NG = 2048, 2560, 3328, 3840
    cols = []
    for h in range(4):
        cols += [w[:, h * 128:(h + 1) * 128], w[:, 512 + h * 128:512 + (h + 1) * 128],
                 w[:, 1024 + h * 128:1024 + (h + 1) * 128], w[:, 1536 + h * 128:1536 + (h + 1) * 128]]
    kv = lambda s, k: w[:, NKV + s * 128 + k * 64:NKV + s * 128 + (k + 1) * 64]
    cols += [kv(2, 0), kv(2, 1), kv(4, 0), kv(4, 1), kv(0, 0), kv(0, 0), kv(0, 1), kv(0, 1)]
    cols += [kv(1, 0), kv(1, 0), kv(1, 1), kv(1, 1), kv(3, 0), kv(3, 1), kv(5, 0), kv(5, 1)]
    cols += [w[:, NZ:NZ + 512]]
    for jj in range(4):
        cols += [w[:, NQ0 + jj * 64:NQ0 + (jj + 1) * 64], w[:, NQ0 + (4 + jj) * 64:NQ0 + (5 + jj) * 64]]
    cols += [w[:, NG:NG + 24], np.zeros((w.shape[0], 512 - 24), w.dtype)]
    out = np.ascontiguousarray(np.concatenate(cols, axis=1))
    assert out.shape[1] == NCOL
    return out


def build(nc, es, nb=NB, dbg=False):
    P = Prog(nc)
    total_sb = [0]

    def sb(name, shape, dt):
        n = 1
        for s in shape[1:]:
            n *= s
        total_sb[0] += n * (4 if dt == F32 else 2)
        return es.enter_context(nc.sbuf_tensor(name, shape, dt))

    def din(name, shape, dt):
        return nc.dram_tensor(name, shape, dt, kind="ExternalInput").ap()

    x = din("x", [nb, T, D], F32)
    y = nc.dram_tensor("y", [nb, T, D], F32, kind="ExternalOutput").ap()
    w_in = din("w_in", [D, NCOL], F32)
    w_out = din("w_out", [D, D], F32)
    w1_d = din("cmp_w1", [2, 2048, 256], F32)
    w2_d = din("cmp_w2", [2, 256, 64], F32)
    b1_d = din("cmp_b1", [2, 256], F32)
    pos_d = din("cmp_pos", [2, 32, 64], F32)
    normg_d = din("norm_g", [D], F32)
    dqg_d = din("diff_q_norm_g", [64], F32)
    dkg_d = din("diff_k_norm_g", [64], F32)
    lq1_d = din("diff_lambda_q1", [64], F32)
    lk1_d = din("diff_lambda_k1", [64], F32)
    lq2_d = din("diff_lambda_q2", [64], F32)
    lk2_d = din("diff_lambda_k2", [64], F32)
    subg_d = din("diff_subln_g", [128], F32)
    nqg_d = din("nsa_q_norm_g", [64], F32)
    nkg_d = din("nsa_k_norm_g", [3, 64], F32)
    c_ident = din("c_ident", [128, 128], BF16)
    c_blk = din("c_blkones", [128, 128], BF16)
    c_tric = din("c_tri_c", [128, 128], BF16)
    c_triw = din("c_tri_w", [128, 128], BF16)
    c_cmask = din("c_cmask", [128, T], BF16)
    c_ovl = din("c_ovl", [128, 33], BF16)
    c_cvalid = din("c_cvalid", [128, 16, 32], F32)
    c_cbias = din("c_cbias", [128, 16, 32], F32)
    c_kaux = din("c_kaux", [4, T], BF16)
    c_dkaux = din("c_dkaux", [4, 4, T], BF16)
    c_qauxd = din("c_qaux_d", [4, 512], BF16)
    c_qauxn = din("c_qaux_n", [8, 4, 512], BF16)
    c_expand = din("c_expand", [32, T], BF16)
    if dbg:
        dbg_mix = nc.dram_tensor("dbg_mix", [128, 16, 1024], BF16, kind="ExternalOutput").ap()

    banks = [es.enter_context(nc.psum_tensor("bank%d" % i, [128, 512], F32)) for i in range(8)]
    BK = [Buf("bank%d" % i, psum=True) for i in range(8)]
    pools = {"st": [0, 1, 2], "acc": [3, 4, 5, 6], "misc": [7]}
    pptr = {"st": 0, "acc": 0, "misc": 0}

    def nxt(pool):
        ids = pools[pool]
        i = ids[pptr[pool] % len(ids)]
        pptr[pool] += 1
        return i

    def bbf(i):
        return banks[i][:, :].bitcast(BF16)

    hT = sb("hT", [128, 8, T], BF16)
    HT = Buf("hT")
    mix = sb("mix", [128, 16, 1024], BF16)
    MIX = Buf("mix")
    wbuf = [sb("wbuf%d" % i, [128, 8, 512], BF16) for i in range(2)]
    WB = [Buf("wb%d" % i) for i in range(2)]
    wst = [sb("wst%d" % i, [128, 512], F32) for i in range(4)]
    WS = [Buf("ws%d" % i) for i in range(4)]
    xs = [sb("xs%d" % i, [128, 1024], F32) for i in range(2)]
    XS = [Buf("xs%d" % i) for i in range(2)]
    xn = sb("xn", [128, 1024], BF16)
    XN = Buf("xn")
    junk = sb("junk", [128, 1024], BF16)
    KSA = sb("KSA", [128, T], BF16)
    KSB = sb("KSB", [128, T], BF16)
    KWA = sb("KWA", [128, T], BF16)
    KWB = sb("KWB", [128, T], BF16)
    B_KSA, B_KSB, B_KWA, B_KWB = Buf("KSA"), Buf("KSB"), Buf("KWA"), Buf("KWB")
    QD = [[sb("QD%d%d" % (s, c), [128, 512], BF16) for c in range(2)] for s in range(2)]
    B_QD = [[Buf("QD%d%d" % (s, c)) for c in range(2)] for s in range(2)]
    Vd = sb("Vd", [128, 16, 129], BF16)
    B_Vd = Buf("Vd")
    Zd = sb("Zd", [128, 16, 128], BF16)
    B_Zd = Buf("Zd")
    QN = [sb("QN%d" % h, [128, 512], BF16) for h in range(8)]
    B_QN = [Buf("QN%d" % h) for h in range(8)]
    Vn = sb("Vn", [128, 16, 4, 65], BF16)
    B_Vn = Buf("Vn")
    Zq = sb("Zq", [128, 4, 512], BF16)
    B_Zq = Buf("Zq")
    Gt = sb("Gt", [128, 16, 24], F32)
    B_Gt = Buf("Gt")
    KV2 = sb("KV2", [128, T], BF16)
    B_KV2 = Buf("KV2")
    kc = [sb("kc%d" % k, [128, 128], BF16) for k in range(2)]
    B_kc = [Buf("kc%d" % k) for k in range(2)]
    vc = [sb("vc%d" % k, [128, 65], BF16) for k in range(2)]
    B_vc = [Buf("vc%d" % k) for k in range(2)]
    hid = sb("hid", [128, 2, 128], BF16)
    B_hid = Buf("hid")
    ON = sb("ON", [128, 4, 512], F32)
    B_ON = Buf("ON")
    IA = [sb("IA%d" % k, [128, 4, 32], F32) for k in range(2)]
    B_IA = [Buf("IA%d" % k) for k in range(2)]
    sc32 = sb("sc32", [128, 4, 32], F32)
    B_sc32 = Buf("sc32")
    m8 = sb("m8", [128, 4, 8], F32)
    B_m8 = Buf("m8")
    SBT = sb("SBT", [128, 4, 128], BF16)
    B_SBT = Buf("SBT")
    tmp64 = sb("tmp64", [128, 4, 64], F32)
    B_tmp64 = Buf("tmp64")
    tmp32 = sb("tmp32", [128, 4, 32], F32)
    B_tmp32 = Buf("tmp32")
    pt = [sb("pt%d" % i, [128, 512], BF16) for i in range(4)]
    PTB = [Buf("pt%d" % i) for i in range(4)]
    sqb = sb("sqb", [128, 512], BF16)
    B_sqb = Buf("sqb")
    lnb = sb("lnb", [128, 512], F32)
    B_lnb = Buf("lnb")
    rsb = sb("rsb", [128, 512], F32)
    B_rsb = Buf("rsb")
    t0 = sb("t0", [128, 4, 128], F32)
    t1 = sb("t1", [128, 4, 128], F32)
    B_t0, B_t1 = Buf("t0"), Buf("t1")
    small = sb("small", [128, 64], F32)
    B_small = Buf("small")
    mT = [sb("mT%d" % i, [128, 8, 128], BF16) for i in range(2)]
    B_mT = [Buf("mT%d" % i) for i in range(2)]
    ident = sb("ident", [128, 128], BF16)
    blkones = sb("blkones", [128, 128], BF16)
    tri_c = sb("tri_c", [128, 128], BF16)
    tri_w = sb("tri_w", [128, 128], BF16)
    cmask = sb("cmask", [128, T], BF16)
    ovl = sb("ovl", [128, 33], BF16)
    cvalid = sb("cvalid", [128, 16, 32], F32)
    cbias = sb("cbias", [128, 16, 32], F32)
    normg = sb("normg", [128, 8], F32)
    gcols = sb("gcols", [128, 8], F32)
    gsub = sb("gsub", [128, 128], F32)
    lamt = sb("lamt", [128, 4, 64], F32)
    lamc = sb("lamc", [128, 8], F32)
    w2k = [sb("w2k%d" % k, [128, 2, 128], BF16) for k in range(2)]
    w2v = sb("w2v", [128, 2, 64], BF16)
    w2st = sb("w2st", [128, 2, 2, 64], F32)
    b1c = sb("b1c", [128, 2, 2], F32)
    bias_h = sb("bias_h", [128, 2, 2], F32)
    posst = sb("posst", [128, 2, 16], F32)
    pos2 = sb("pos2", [128, 2, 16], BF16)
    CONST = Buf("const")

    sems_names = ["pe", "act", "dve", "pool", "sp"] + ["dma_sp_%d" % i for i in range(8)]
    sems = {n: es.enter_context(nc.semaphore(n)) for n in sems_names}

    def cload(dst, src, **kw):
        P.dma("sp", dst, src, writes=[CONST], **kw)

    cload(ident[:, :], c_ident)
    cload(blkones[:, :], c_blk)
    cload(tri_c[:, :], c_tric)
    cload(tri_w[:, :], c_triw)
    cload(cmask[:, :], c_cmask)
    cload(ovl[:, :], c_ovl)
    cload(cvalid[:, :, :], c_cvalid)
    cload(cbias[:, :, :], c_cbias)
    cload(normg[:, :], normg_d.rearrange("(m p) -> p m", p=128), allow_slow_non_contiguous=True)

    def colload(col, src):
        for half in range(2):
            cload(gcols[half * 64:(half + 1) * 64, col:col + 1], src.rearrange("(p o) -> p o", o=1))

    colload(0, dqg_d)
    colload(1, dkg_d)
    colload(2, nqg_d)
    for i in range(3):
        colload(3 + i, nkg_d[i, :])

    def bcast_rows(src1d, n):
        return bass.AP(tensor=src1d.tensor, offset=src1d.offset, ap=[[0, 128], [1, n]])

    cload(gsub[:, :], bcast_rows(subg_d, 128))
    for i, l in enumerate([lq1_d, lk1_d, lq2_d, lk2_d]):
        cload(lamt[:, i, :], bcast_rows(l, 64))
    for kv in range(2):
        cload(b1c[:, kv, :], b1_d[kv, :].rearrange("(jt j) -> j jt", j=128), allow_slow_non_contiguous=True)
        cload(w2st[:, kv, :, :], w2_d[kv, :, :].rearrange("(jt j) d -> j jt d", j=128))
        cload(posst[:, kv, :], pos_d[kv, :, :].rearrange("(l2 two) d -> (two d) l2", two=2),
              allow_slow_non_contiguous=True)

    def mz(eng, ap):
        P.op(eng, lambda e: e.memset(ap, 0.0), [], [CONST])

    def m1(eng, ap):
        P.op(eng, lambda e: e.memset(ap, 1.0), [], [CONST])

    for tl in (KSA, KSB, KWA, KWB):
        mz("pool", tl[:, :])
    for s in range(2):
        for c in range(2):
            mz("pool", QD[s][c][:, :])
    for h in range(8):
        mz("pool", QN[h][:, :])
    mz("pool", SBT[:, :, :])
    for k in range(2):
        mz("pool", kc[k][:, :])
        mz("pool", vc[k][:, :])
        mz("pool", w2k[k][:, :, :])
        m1("pool", vc[k][:, 64:65])
    mz("pool", hid[:, :, :])
    m1("pool", Vd[:, :, 128:129])
    m1("pool", Vn[:, :, :, 64:65])
    cload(KSA[96:100, :], c_kaux)
    cload(KSB[32:36, :], c_kaux)
    cload(KSA[64:96, :], c_expand)
    cload(KSB[0:32, :], c_expand)
    for s in range(2):
        cload(QD[s][0][96:100, :], c_qauxd)
        cload(QD[s][1][32:36, :], c_qauxd)
    for h in range(8):
        if h < 4:
            cload(QN[h][96:100, :], c_qauxn[h])
        else:
            cload(QN[h][32:36, :], c_qauxn[h])
    P.op("dve", lambda e: e.tensor_scalar(out=gcols[:, 0:1], in0=gcols[:, 0:1], scalar1=0.125, scalar2=None,
                                          op0=ALU.mult), [CONST], [CONST])
    P.op("dve", lambda e: e.tensor_scalar(out=gcols[:, 2:3], in0=gcols[:, 2:3], scalar1=0.125, scalar2=None,
                                          op0=ALU.mult), [CONST], [CONST])
    P.op("dve", lambda e: e.tensor_scalar(out=gsub[:, :], in0=gsub[:, :], scalar1=0.8, scalar2=None,
                                          op0=ALU.mult), [CONST], [CONST])
    P.op("dve", lambda e: e.tensor_tensor(out=lamt[:, 0, :], in0=lamt[:, 0, :], in1=lamt[:, 1, :], op=ALU.mult),
         [CONST], [CONST])
    P.op("dve", lambda e: e.tensor_tensor(out=lamt[:, 2, :], in0=lamt[:, 2, :], in1=lamt[:, 3, :], op=ALU.mult),
         [CONST], [CONST])
    P.op("dve", lambda e: e.reduce_sum(out=lamc[:, 0:1], in_=lamt[:, 0, :], axis=AX.X), [CONST], [CONST])
    P.op("dve", lambda e: e.reduce_sum(out=lamc[:, 1:2], in_=lamt[:, 2, :], axis=AX.X), [CONST], [CONST])
    P.op("act", lambda e: e.activation(out=lamc[:, 2:4], in_=lamc[:, 0:2], func=AF.Exp), [CONST], [CONST])
    P.op("dve", lambda e: e.scalar_tensor_tensor(out=lamc[:, 4:5], in0=lamc[:, 3:4], scalar=-0.2, in1=lamc[:, 2:3],
                                                 op0=ALU.add, op1=ALU.subtract), [CONST], [CONST])
    for k in range(2):
        P.op("dve", lambda e, k=k: e.tensor_copy(out=w2k[k][:, :, k * 64:(k + 1) * 64], in_=w2st[:, 0, :, :]),
             [CONST], [CONST])
    P.op("dve", lambda e: e.tensor_copy(out=w2v[:, :, :], in_=w2st[:, 1, :, :]), [CONST], [CONST])
    P.op("dve", lambda e: e.tensor_copy(out=pos2[:, :, :], in_=posst[:, :, :]), [CONST], [CONST])

    wctr = [0]
    stc = [0]

    def stage_cast(k, dst_ap, src_ap, scale_ap):
        s = stc[0] % 4
        stc[0] += 1
        P.dma("sp", wst[s][:, :], src_ap, writes=[WS[s]])
        if scale_ap is not None:
            P.op("dve", lambda e: e.tensor_scalar(out=dst_ap, in0=wst[s][:, :], scalar1=scale_ap, scalar2=None,
                                                  op0=ALU.mult), [WS[s], CONST], [WB[k]])
        else:
            P.op("dve", lambda e: e.tensor_copy(out=dst_ap, in_=wst[s][:, :]), [WS[s]], [WB[k]])

    def load_group(g):
        k = wctr[0] % 2
        wctr[0] += 1
        for m in range(8):
            stage_cast(k, wbuf[k][:, m, :], w_in[m * 128:(m + 1) * 128, g * 512:(g + 1) * 512], normg[:, m:m + 1])
        return k

    def load_wout(n):
        k = wctr[0] % 2
        wctr[0] += 1
        for m in range(8):
            stage_cast(k, wbuf[k][:, m, :], w_out[m * 128:(m + 1) * 128, n * 512:(n + 1) * 512], None)
        return k

    def load_w1(kv):
        k = wctr[0] % 2
        wctr[0] += 1
        src = w1_d[kv, :, :].rearrange("(c p) j -> p c j", p=128)
        wv = wbuf[k][:, :, :].rearrange("p m n -> p (m n)").rearrange("p (c j) -> p c j", j=256)
        for i in range(8):
            s = stc[0] % 4
            stc[0] += 1
            P.dma("sp", wst[s][:, :].rearrange("p (c j) -> p c j", j=256), src[:, 2 * i:2 * i + 2, :], writes=[WS[s]])
            P.op("dve", lambda e, s=s, i=i: e.tensor_copy(out=wv[:, 2 * i:2 * i + 2, :],
                                                          in_=wst[s][:, :].rearrange("p (c j) -> p c j", j=256)),
                 [WS[s]], [WB[k]])
        return k, wv

    for kv in range(2):
        k, wv = load_w1(kv)
        bk = nxt("misc")
        for jt in range(2):
            for l2 in range(16):
                P.op("pe", lambda e, jt=jt, l2=l2, wv=wv, kv=kv, bk=bk: e.matmul(
                    banks[bk][:, jt:jt + 1], lhsT=wv[:, l2, jt * 128:(jt + 1) * 128], rhs=pos2[:, kv, l2:l2 + 1],
                    start=(l2 == 0 and jt == 0), stop=True, skip_group_check=True), [WB[k], CONST], [BK[bk]])
        P.op("dve", lambda e, kv=kv, bk=bk: e.tensor_tensor(out=bias_h[:, kv, :], in0=banks[bk][:, 0:2],
                                                            in1=b1c[:, kv, :], op=ALU.add), [BK[bk], CONST], [CONST])

    def qknorm(pb, n, gcol, dsts):
        P.op("act", lambda e: e.activation(out=sqb[:, :n], in_=banks[pb][:, :n], func=AF.Square), [BK[pb]], [B_sqb])
        mb = nxt("misc")
        P.op("pe", lambda e: e.matmul(banks[mb][:, :n], lhsT=blkones[:, :], rhs=sqb[:, :n], start=True, stop=True),
             [B_sqb, CONST], [BK[mb]])
        P.op("act", lambda e: e.activation(out=lnb[:, :n], in_=banks[mb][:, :n], func=AF.Ln, bias=EPS),
             [BK[mb]], [B_lnb])
        P.op("act", lambda e: e.activation(out=rsb[:, :n], in_=lnb[:, :n], func=AF.Exp, scale=-0.5),
             [B_lnb], [B_rsb])
        for (dst, lo, hi, db) in dsts:
            P.op("dve", lambda e, dst=dst, lo=lo, hi=hi: e.scalar_tensor_tensor(
                out=dst, in0=banks[pb][lo:hi, :n], scalar=gcol[lo:hi, 0:1], in1=rsb[lo:hi, :n],
                op0=ALU.mult, op1=ALU.mult), [BK[pb], B_rsb, CONST], [db])

    def proj_fm(k, c0, tq, pb, ncols=512):
        for m in range(8):
            P.op("pe", lambda e, m=m: e.matmul(banks[pb][:, :ncols], lhsT=wbuf[k][:, m, c0:c0 + 128],
                                               rhs=hT[:, m, tq * 512:tq * 512 + ncols], start=(m == 0), stop=(m == 7)),
                 [WB[k], HT], [BK[pb]])

    def proj_tm(k, c0, n, tt, out_ap, pb, first):
        for m in range(8):
            P.op("pe", lambda e, m=m: e.matmul(out_ap, lhsT=hT[:, m, tt * 128:(tt + 1) * 128],
                                               rhs=wbuf[k][:, m, c0:c0 + n], start=(first and m == 0), stop=True,
                                               skip_group_check=True), [WB[k], HT], [BK[pb]])

    pend = deque()
    npv = [0]
    DEPTH = 2
    ptc = [0]

    def drain(limit):
        while pend and npv[0] > limit:
            kind, fn = pend.popleft()
            if kind == "pv":
                npv[0] -= 1
            fn()
        while pend and pend[0][0] == "fin":
            pend.popleft()[1]()

    def flush():
        while pend:
            kind, fn = pend.popleft()
            if kind == "pv":
                npv[0] -= 1
            fn()

    def attention(tiles, Qt, QB, KB, VB, acc_ap, acc_bank, bias, fin):
        started = set()
        for d in tiles:
            sbk = nxt("st")
            c0, c1 = d["c0"], d["c1"]
            ex = d.get("extra")
            P.op("pe", lambda e, d=d, sbk=sbk, c0=c0, c1=c1, ex=ex: e.matmul(
                banks[sbk][:, c0:c1], lhsT=d["k_ap"], rhs=Qt[:, c0:c1], start=True, stop=(ex is None)),
                [KB, QB], [BK[sbk]])
            if ex is not None:
                P.op("pe", lambda e, sbk=sbk, ex=ex: e.matmul(
                    banks[sbk][:, ex[1]:ex[1] + ex[2]], lhsT=ident[:, :], rhs=ex[0], start=False, stop=True),
                    [CONST], [BK[sbk]])
            pi = ptc[0] % 4
            ptc[0] += 1
            P.op("act", lambda e, sbk=sbk, pi=pi, c0=c0, c1=c1: e.activation(
                out=pt[pi][:, c0:c1], in_=banks[sbk][:, c0:c1], func=AF.Exp, bias=float(bias)),
                [BK[sbk]], [PTB[pi]])

            def pv(d=d, pi=pi, c0=c0, c1=c1):
                for s in range(c0 // 128, c1 // 128):
                    bk = acc_bank(s)
                    first = bk not in started
                    started.add(bk)
                    P.op("pe", lambda e, s=s, first=first: e.matmul(
                        acc_ap(s), lhsT=pt[pi][:, s * 128:(s + 1) * 128], rhs=d["v_ap"], start=first, stop=True,
                        skip_group_check=True), [PTB[pi], VB], [BK[bk]])
                if d.get("more") is not None:
                    d["more"](pi)
            pend.append(("pv", pv))
            npv[0] += 1
            drain(DEPTH)
        pend.append(("fin", fin))

    def causal_tiles(Kt, qi, vfn):
        tiles = []
        for kt in range(4 * qi):
            tiles.append(dict(k_ap=Kt[:, kt * 128:(kt + 1) * 128], c0=0, c1=512, v_ap=vfn(kt)))
        for j in range(4):
            kt = 4 * qi + j
            tiles.append(dict(k_ap=Kt[:, kt * 128:(kt + 1) * 128], c0=128 * j, c1=512,
                              extra=(tri_c[:, :], 128 * j, 128), v_ap=vfn(kt)))
        return tiles

    def window_tiles(Kt, qi, vfn):
        tiles = []
        if qi > 0:
            for j in range(4):
                kt = 4 * (qi - 1) + j
                tiles.append(dict(k_ap=Kt[:, kt * 128:(kt + 1) * 128], c0=0, c1=128 * (j + 1),
                                  extra=(tri_w[:, :], 128 * j, 128), v_ap=vfn(kt)))
        for j in range(4):
            kt = 4 * qi + j
            tiles.append(dict(k_ap=Kt[:, kt * 128:(kt + 1) * 128], c0=128 * j, c1=512,
                              extra=(tri_c[:, :], 128 * j, 128), v_ap=vfn(kt)))
        return tiles

    for b in range(nb):
        k_cur = load_group(0)
        for tt in range(16):
            xi = tt % 2
            P.dma("sp", xs[xi][:, :], x[b, tt * 128:(tt + 1) * 128, :], writes=[XS[xi]])
            P.op("act", lambda e, xi=xi: e.activation(out=junk[:, :], in_=xs[xi][:, :], func=AF.Square,
                                                      accum_out=small[:, 0:1]), [XS[xi]], [B_small])
            P.op("act", lambda e: e.activation(out=small[:, 1:2], in_=small[:, 0:1], func=AF.Ln, scale=1.0 / D,
                                               bias=EPS), [B_small], [B_small])
            P.op("act", lambda e: e.activation(out=small[:, 2:3], in_=small[:, 1:2], func=AF.Exp, scale=-0.5),
                 [B_small], [B_small])
            P.op("dve", lambda e, xi=xi: e.tensor_scalar(out=xn[:, :], in0=xs[xi][:, :], scalar1=small[:, 2:3],
                                                         scalar2=None, op0=ALU.mult), [XS[xi], B_small], [XN])
            bk = nxt("st")
            for m in range(8):
                P.op("pe", lambda e, m=m, bk=bk: e.transpose(out=bbf(bk)[:, m * 128:(m + 1) * 128],
                                                             in_=xn[:, m * 128:(m + 1) * 128], identity=ident[:, :]),
                     [XN, CONST], [BK[bk]])
            P.op("dve", lambda e, tt=tt, bk=bk: e.tensor_copy(
                out=hT[:, :, tt * 128:(tt + 1) * 128],
                in_=bbf(bk)[:, 0:1024].rearrange("p (m t) -> p m t", t=128)), [BK[bk]], [HT])

        KA, KB_, B_KA, B_KB = KWA, KWB, B_KWA, B_KWB
        for h in range(4):
            flush()
            k = k_cur
            P.dma("sp", KA[96:100, :], c_dkaux[h], writes=[B_KA])
            P.dma("sp", KB_[32:36, :], c_dkaux[h], writes=[B_KB])
            for tq in range(4):
                pb = nxt("st")
                proj_fm(k, 128, tq, pb)
                qknorm(pb, 512, gcols[:, 1:2],
                       [(KA[0:64, tq * 512:(tq + 1) * 512], 0, 64, B_KA),
                        (KB_[64:128, tq * 512:(tq + 1) * 512], 64, 128, B_KB)])
            for t4 in range(4):
                pb = nxt("st")
                for i in range(4):
                    proj_tm(k, 256, 128, t4 * 4 + i, banks[pb][:, i * 128:(i + 1) * 128], pb, first=(i == 0))
                P.op("dve", lambda e, t4=t4, pb=pb: e.tensor_copy(
                    out=Vd[:, t4 * 4:(t4 + 1) * 4, 0:128],
                    in_=banks[pb][:, :].rearrange("p (i c) -> p i c", c=128)), [BK[pb]], [B_Vd])
            for t4 in range(4):
                pb = nxt("st")
                for i in range(4):
                    proj_tm(k, 384, 128, t4 * 4 + i, banks[pb][:, i * 128:(i + 1) * 128], pb, first=(i == 0))
                P.op("act", lambda e, t4=t4, pb=pb: e.activation(
                    out=Zd[:, t4 * 4:(t4 + 1) * 4, :], in_=banks[pb][:, :].rearrange("p (i c) -> p i c", c=128),
                    func=AF.Silu), [BK[pb]], [B_Zd])
                P.op("dve", lambda e, t4=t4: e.tensor_tensor(
                    out=Zd[:, t4 * 4:(t4 + 1) * 4, :], in0=Zd[:, t4 * 4:(t4 + 1) * 4, :],
                    in1=gsub[:, :].unsqueeze(1).broadcast_to([128, 4, 128]), op=ALU.mult), [B_Zd, CONST], [B_Zd])
            k_cur = load_group(h + 1 if h < 3 else 4)

            def qproj(qi_, k=k):
                qs_ = qi_ % 2
                pb_ = nxt("st")
                proj_fm(k, 0, qi_, pb_)
                qknorm(pb_, 512, gcols[:, 0:1], [(QD[qs_][0][0:64, :], 0, 64, B_QD[qs_][0]),
                                                 (QD[qs_][1][64:128, :], 64, 128, B_QD[qs_][1])])

            qproj(0)
            for qi in range(4):
                qs = qi % 2
                if qi < 3:
                    qproj(qi + 1)
                for c in range(2):
                    Kt, KBf = (KA, B_KA) if c == 0 else (KB_, B_KB)
                    bx = nxt("acc")
                    by = nxt("acc")
                    tt_ = t0 if c == 0 else t1
                    Bt_ = B_t0 if c == 0 else B_t1

                    def acc_ap(s, bx=bx, by=by):
                        return banks[bx][:, s * 129:(s + 1) * 129] if s < 3 else banks[by][:, 0:129]

                    def acc_bank(s, bx=bx, by=by):
                        return bx if s < 3 else by

                    def fin(c=c, bx=bx, by=by, tt_=tt_, Bt_=Bt_, qi=qi, h=h):
                        xv = banks[bx][:, 0:387].rearrange("p (s c) -> p s c", c=129)
                        P.op("dve", lambda e: e.tensor_copy(out=small[:, 8:11], in_=xv[:, :, 128]),
                             [BK[bx]], [B_small])
                        P.op("dve", lambda e: e.tensor_copy(out=small[:, 11:12], in_=banks[by][:, 128:129]),
                             [BK[by]], [B_small])
                        P.op("dve", lambda e: e.reciprocal(out=small[:, 12:16], in_=small[:, 8:12]),
                             [B_small], [B_small])
                        P.op("dve", lambda e: e.tensor_tensor(
                            out=tt_[:, 0:3, :], in0=xv[:, :, 0:128],
                            in1=small[:, 12:15].unsqueeze(2).broadcast_to([128, 3, 128]), op=ALU.mult),
                            [BK[bx], B_small], [Bt_])
                        P.op("dve", lambda e: e.tensor_scalar(
                            out=tt_[:, 3, :], in0=banks[by][:, 0:128], scalar1=small[:, 15:16], scalar2=None,
                            op0=ALU.mult), [BK[by], B_small], [Bt_])
                        if c == 1:
                            P.op("dve", lambda e: e.scalar_tensor_tensor(
                                out=t0[:, :, :], in0=t1[:, :, :], scalar=lamc[:, 4:5], in1=t0[:, :, :],
                                op0=ALU.mult, op1=ALU.add), [B_t0, B_t1, CONST], [B_t0])
                            P.op("dve", lambda e: e.tensor_tensor(out=t1[:, :, :], in0=t0[:, :, :], in1=t0[:, :, :],
                                                                  op=ALU.mult), [B_t0], [B_t1])
                            P.op("dve", lambda e: e.reduce_sum(out=small[:, 16:20], in_=t1[:, :, :], axis=AX.X),
                                 [B_t1], [B_small])
                            P.op("act", lambda e: e.activation(out=small[:, 20:24], in_=small[:, 16:20], func=AF.Ln,
                                                               scale=1.0 / 128, bias=EPS), [B_small], [B_small])
                            P.op("act", lambda e: e.activation(out=small[:, 24:28], in_=small[:, 20:24], func=AF.Exp,
                                                               scale=-0.5), [B_small], [B_small])
                            P.op("dve", lambda e: e.tensor_tensor(
                                out=t0[:, :, :], in0=t0[:, :, :],
                                in1=small[:, 24:28].unsqueeze(2).broadcast_to([128, 4, 128]), op=ALU.mult),
                                [B_t0, B_small], [B_t0])
                            P.op("dve", lambda e: e.tensor_tensor(
                                out=mix[:, qi * 4:(qi + 1) * 4, h * 128:(h + 1) * 128], in0=t0[:, :, :],
                                in1=Zd[:, qi * 4:(qi + 1) * 4, :], op=ALU.mult), [B_t0, B_Zd], [MIX])

                    attention(causal_tiles(Kt, qi, lambda kt: Vd[:, kt, 0:129]), QD[qs][c], B_QD[qs][c], KBf, B_Vd,
                              acc_ap, acc_bank, -DSL[h] * 512.0 * qi, fin)
        flush()

        P.dma("sp", KWA[96:100, :], c_kaux, writes=[B_KWA])
        P.dma("sp", KWB[32:36, :], c_kaux, writes=[B_KWB])
        k = k_cur
        kw1, wv1 = load_w1(0)
        for tq in range(4):
            pb = nxt("st")
            proj_fm(k, 0, tq, pb)
            qknorm(pb, 512, gcols[:, 4:5], [(KSA[0:64, tq * 512:(tq + 1) * 512], 0, 64, B_KSA),
                                            (KSB[64:128, tq * 512:(tq + 1) * 512], 64, 128, B_KSB)])
            pb = nxt("st")
            proj_fm(k, 128, tq, pb)
            qknorm(pb, 512, gcols[:, 5:6], [(KWA[0:64, tq * 512:(tq + 1) * 512], 0, 64, B_KWA),
                                            (KWB[64:128, tq * 512:(tq + 1) * 512], 64, 128, B_KWB)])

        def kv2_fill(kw, c0):
            for tq in range(4):
                pb = nxt("st")
                proj_fm(kw, c0, tq, pb)
                P.op("dve", lambda e, tq=tq, pb=pb: e.tensor_copy(out=KV2[0:64, tq * 512:(tq + 1) * 512],
                                                                  in_=banks[pb][0:64, :]), [BK[pb]], [B_KV2])
                if tq == 0:
                    P.op("dve", lambda e, pb=pb: e.tensor_copy(out=KV2[64:128, 0:511], in_=banks[pb][64:128, 1:512]),
                         [BK[pb]], [B_KV2])
                else:
                    P.op("dve", lambda e, tq=tq, pb=pb: e.tensor_copy(
                        out=KV2[64:128, tq * 512 - 1:tq * 512 + 511], in_=banks[pb][64:128, :]), [BK[pb]], [B_KV2])

        def compress_hidden(kw1, wv, kv):
            for jt in range(2):
                pb = nxt("st")
                for l2 in range(16):
                    P.op("pe", lambda e, l2=l2, jt=jt, pb=pb: e.matmul(
                        banks[pb][:, 0:127], lhsT=wv[:, l2, jt * 128:(jt + 1) * 128],
                        rhs=KV2[:, 2 * l2:2 * l2 + 16 * 126 + 1:16], start=(l2 == 0), stop=(l2 == 15)),
                        [WB[kw1], B_KV2], [BK[pb]])
                P.op("act", lambda e, jt=jt, pb=pb: e.activation(out=hid[:, jt, 0:127], in_=banks[pb][:, 0:127],
                                                                 func=AF.Silu, bias=bias_h[:, kv, jt:jt + 1]),
                     [BK[pb], CONST], [B_hid])

        for kvh in range(2):
            kv2_fill(k, 256 + kvh * 128)
            compress_hidden(kw1, wv1, 0)
            pb = nxt("st")
            for jt in range(2):
                P.op("pe", lambda e, jt=jt, pb=pb, kvh=kvh: e.matmul(
                    banks[pb][:, 0:127], lhsT=w2k[kvh][:, jt, :], rhs=hid[:, jt, 0:127], start=(jt == 0),
                    stop=(jt == 1)), [B_hid, CONST], [BK[pb]])
            qknorm(pb, 127, gcols[:, 3:4], [(kc[kvh][0:64, 0:127], 0, 64, B_kc[kvh]),
                                            (kc[kvh][64:128, 0:127], 64, 128, B_kc[kvh])])
        k = load_group(5)
        kw1, wv1 = load_w1(1)
        for kvh in range(2):
            kv2_fill(k, kvh * 128)
            compress_hidden(kw1, wv1, 1)
            pb = nxt("st")
            for jt in range(2):
                P.op("pe", lambda e, jt=jt, pb=pb: e.matmul(
                    banks[pb][0:127, 0:64], lhsT=hid[:, jt, 0:127], rhs=w2v[:, jt, :], start=(jt == 0),
                    stop=(jt == 1)), [B_hid, CONST], [BK[pb]])
            P.op("dve", lambda e, pb=pb, kvh=kvh: e.tensor_copy(out=vc[kvh][0:127, 0:64], in_=banks[pb][0:127, 0:64]),
                 [BK[pb]], [B_vc[kvh]])
        for t2 in range(8):
            pb = nxt("st")
            for i in range(2):
                proj_tm(k, 256, 256, t2 * 2 + i, banks[pb][:, i * 256:(i + 1) * 256], pb, first=(i == 0))
            P.op("dve", lambda e, t2=t2, pb=pb: e.tensor_copy(
                out=Vn[:, t2 * 2:t2 * 2 + 2, :, 0:64],
                in_=banks[pb][:, :].rearrange("p (i g c) -> p i g c", g=4, c=64)), [BK[pb]], [B_Vn])
        k = load_group(8)
        pb = nxt("st")
        for tt in range(16):
            proj_tm(k, 0, 24, tt, banks[pb][:, tt * 24:(tt + 1) * 24], pb, first=(tt == 0))
        P.op("act", lambda e, pb=pb: e.activation(out=Gt[:, :, :],
                                                  in_=banks[pb][:, 0:384].rearrange("p (t g) -> p t g", g=24),
                                                  func=AF.Sigmoid), [BK[pb]], [B_Gt])
        kq = load_group(7)
        kz = load_group(6)
        for qi in range(4):
            for i in range(4):
                pb = nxt("st")
                proj_tm(kz, 0, 512, qi * 4 + i, banks[pb][:, :], pb, first=True)
                P.op("act", lambda e, i=i, pb=pb: e.activation(out=Zq[:, i, :], in_=banks[pb][:, :], func=AF.Silu),
                     [BK[pb]], [B_Zq])
            for j in range(4):
                pb = nxt("st")
                proj_fm(kq, j * 128, qi, pb)
                qknorm(pb, 512, gcols[:, 2:3], [(QN[j][0:64, :], 0, 64, B_QN[j]),
                                                (QN[4 + j][64:128, :], 64, 128, B_QN[4 + j])])

            def nsa_fin_factory(hd, br, ab, qi=qi):
                def fin():
                    av = banks[ab][:, 0:260].rearrange("p (s c) -> p s c", c=65)
                    o = 32 + (hd % 2) * 16
                    P.op("dve", lambda e: e.tensor_scalar(out=small[:, o:o + 4], in0=av[:, :, 64], scalar1=1e-30,
                                                          scalar2=None, op0=ALU.max), [BK[ab]], [B_small])
                    P.op("dve", lambda e: e.reciprocal(out=small[:, o + 4:o + 8], in_=small[:, o:o + 4]),
                         [B_small], [B_small])
                    P.op("dve", lambda e: e.tensor_tensor(out=small[:, o + 8:o + 12], in0=small[:, o + 4:o + 8],
                                                          in1=Gt[:, qi * 4:(qi + 1) * 4, br * 8 + hd], op=ALU.mult),
                         [B_small, B_Gt], [B_small])
                    cf = small[:, o + 8:o + 12].unsqueeze(2).broadcast_to([128, 4, 64])
                    if br == 0:
                        P.op("dve", lambda e: e.tensor_tensor(out=ON[:, :, hd * 64:(hd + 1) * 64],
                                                              in0=av[:, :, 0:64], in1=cf, op=ALU.mult),
                             [BK[ab], B_small], [B_ON])
                    else:
                        P.op("dve", lambda e: e.tensor_tensor(out=tmp64[:, :, :], in0=av[:, :, 0:64], in1=cf,
                                                              op=ALU.mult), [BK[ab], B_small], [B_tmp64])
                        P.op("dve", lambda e: e.tensor_tensor(out=ON[:, :, hd * 64:(hd + 1) * 64],
                                                              in0=ON[:, :, hd * 64:(hd + 1) * 64], in1=tmp64[:, :, :],
                                                              op=ALU.add), [B_tmp64, B_ON], [B_ON])
                return fin

            for hd in range(8):
                kvh = hd // 4
                ab = nxt("acc")
                ib = nxt("acc")

                def more(pi, ib=ib):
                    for s in range(4):
                        P.op("pe", lambda e, s=s: e.matmul(
                            banks[ib][:, s * 33:(s + 1) * 33], lhsT=pt[pi][:, s * 128:(s + 1) * 128], rhs=ovl[:, :],
                            start=(s == 0), stop=True, skip_group_check=True), [PTB[pi], CONST], [BK[ib]])

                base_fin = nsa_fin_factory(hd, 0, ab)

                def fin(base_fin=base_fin, ib=ib, hd=hd, kvh=kvh):
                    base_fin()
                    iv = banks[ib][:, 0:132].rearrange("p (s c) -> p s c", c=33)
                    P.op("dve", lambda e: e.tensor_scalar(out=small[:, 28:32], in0=iv[:, :, 32], scalar1=1e-30,
                                                          scalar2=None, op0=ALU.max), [BK[ib]], [B_small])
                    P.op("dve", lambda e: e.reciprocal(out=small[:, 4:8], in_=small[:, 28:32]), [B_small], [B_small])
                    rb = small[:, 4:8].unsqueeze(2).broadcast_to([128, 4, 32])
                    if hd % 4 == 0:
                        P.op("dve", lambda e: e.tensor_tensor(out=IA[kvh][:, :, :], in0=iv[:, :, 0:32], in1=rb,
                                                              op=ALU.mult), [BK[ib], B_small], [B_IA[kvh]])
                    else:
                        P.op("dve", lambda e: e.tensor_tensor(out=tmp32[:, :, :], in0=iv[:, :, 0:32], in1=rb,
                                                              op=ALU.mult), [BK[ib], B_small], [B_tmp32])
                        P.op("dve", lambda e: e.tensor_tensor(out=IA[kvh][:, :, :], in0=IA[kvh][:, :, :],
                                                              in1=tmp32[:, :, :], op=ALU.add),
                             [B_tmp32, B_IA[kvh]], [B_IA[kvh]])

                tiles = [dict(k_ap=kc[kvh][:, :], c0=0, c1=512, extra=(cmask[:, qi * 512:(qi + 1) * 512], 0, 512),
                              v_ap=vc[kvh][:, 0:65], more=more)]
                attention(tiles, QN[hd], B_QN[hd], B_kc[kvh], B_vc[kvh],
                          lambda s, ab=ab: banks[ab][:, s * 65:(s + 1) * 65], lambda s, ab=ab: ab, 0.0, fin)
            flush()
            for kvh in range(2):
                off = 64 if kvh == 0 else 0
                P.op("dve", lambda e, kvh=kvh, qi=qi: e.tensor_tensor(out=sc32[:, :, :], in0=IA[kvh][:, :, :],
                                                                      in1=cvalid[:, qi * 4:(qi + 1) * 4, :], op=ALU.mult),
                     [B_IA[kvh], CONST], [B_sc32])
                P.op("dve", lambda e, qi=qi: e.tensor_tensor(out=sc32[:, :, :], in0=sc32[:, :, :],
                                                             in1=cbias[:, qi * 4:(qi + 1) * 4, :], op=ALU.add),
                     [B_sc32, CONST], [B_sc32])
                for s in range(4):
                    P.op("dve", lambda e, s=s: e.max(out=m8[:, s, :], in_=sc32[:, s, :]), [B_sc32], [B_m8])
                for s in range(4):
                    P.op("dve", lambda e, s=s, off=off: e.tensor_scalar(
                        out=SBT[:, s, off:off + 32], in0=sc32[:, s, :], scalar1=m8[:, s, 7:8], scalar2=NEG,
                        op0=ALU.is_lt, op1=ALU.mult), [B_sc32, B_m8], [B_SBT])
            for hd in range(8):
                kvh = hd // 4
                Kw, BKw = (KWA, B_KWA) if kvh == 0 else (KWB, B_KWB)
                ab = nxt("acc")
                attention(window_tiles(Kw, qi, lambda kt, kvh=kvh: Vn[:, kt, 2 + kvh, :]), QN[hd], B_QN[hd], BKw, B_Vn,
                          lambda s, ab=ab: banks[ab][:, s * 65:(s + 1) * 65], lambda s, ab=ab: ab,
                          -NSL[hd] * 512.0 * qi, nsa_fin_factory(hd, 2, ab))
            tb = nxt("st")
            for s in range(4):
                P.op("pe", lambda e, s=s, tb=tb: e.transpose(out=bbf(tb)[:, s * 128:(s + 1) * 128], in_=SBT[:, s, :],
                                                             identity=ident[:, :]), [B_SBT, CONST], [BK[tb]])
            for j in range(4):
                P.op("dve", lambda e, j=j, tb=tb: e.tensor_copy(out=QN[j][64:96, :], in_=bbf(tb)[64:96, 0:512]),
                     [BK[tb]], [B_QN[j]])
                P.op("dve", lambda e, j=j, tb=tb: e.tensor_copy(out=QN[4 + j][0:32, :], in_=bbf(tb)[0:32, 0:512]),
                     [BK[tb]], [B_QN[4 + j]])
            for hd in range(8):
                kvh = hd // 4
                Ks, BKs = (KSA, B_KSA) if kvh == 0 else (KSB, B_KSB)
                ab = nxt("acc")
                attention(causal_tiles(Ks, qi, lambda kt, kvh=kvh: Vn[:, kt, kvh, :]), QN[hd], B_QN[hd], BKs, B_Vn,
                          lambda s, ab=ab: banks[ab][:, s * 65:(s + 1) * 65], lambda s, ab=ab: ab,
                          -NSL[hd] * 512.0 * qi, nsa_fin_factory(hd, 1, ab))
            flush()
            P.op("dve", lambda e, qi=qi: e.tensor_tensor(out=mix[:, qi * 4:(qi + 1) * 4, 512:1024], in0=ON[:, :, :],
                                                         in1=Zq[:, :, :], op=ALU.mult), [B_ON, B_Zq], [MIX])

        if dbg and b == 0:
            P.dma("sp", dbg_mix, mix[:, :, :], reads=[MIX])

        ko = [load_wout(0), load_wout(1)]
        for tt in range(16):
            mi = tt % 2
            tb = nxt("st")
            for c in range(8):
                P.op("pe", lambda e, c=c, tb=tb, tt=tt: e.transpose(
                    out=bbf(tb)[:, c * 128:(c + 1) * 128], in_=mix[:, tt, c * 128:(c + 1) * 128],
                    identity=ident[:, :]), [MIX, CONST], [BK[tb]])
            P.op("dve", lambda e, mi=mi, tb=tb: e.tensor_copy(
                out=mT[mi][:, :, :], in_=bbf(tb)[:, 0:1024].rearrange("p (c t) -> p c t", t=128)),
                [BK[tb]], [B_mT[mi]])
            xi = tt % 2
            P.dma("sp", xs[xi][:, :], x[b, tt * 128:(tt + 1) * 128, :], writes=[XS[xi]])
            for n in range(2):
                ob = nxt("acc")
                for c in range(8):
                    P.op("pe", lambda e, c=c, n=n, ob=ob, mi=mi, ko=ko: e.matmul(
                        banks[ob][:, :], lhsT=mT[mi][:, c, :], rhs=wbuf[ko[n]][:, c, :], start=(c == 0),
                        stop=(c == 7)), [B_mT[mi], WB[ko[n]]], [BK[ob]])
                P.op("dve", lambda e, n=n, ob=ob, xi=xi: e.tensor_tensor(
                    out=xs[xi][:, n * 512:(n + 1) * 512], in0=banks[ob][:, :], in1=xs[xi][:, n * 512:(n + 1) * 512],
                    op=ALU.add), [BK[ob], XS[xi]], [XS[xi]])
            P.dma("sp", y[b, tt * 128:(tt + 1) * 128, :], xs[xi][:, :], reads=[XS[xi]])

    fin_op = P.op("sp", lambda e: e.nop(), [], [])
    for d in P.dma_ops["sp"][-16:]:
        P._add_dep(fin_op, d)
    nw = P.emit(sems)
    return dict(nops=P.nops, nwaits=nw, sbuf_bytes=total_sb[0])


_CACHE = {}


def kernel(x, norm_g, w_in, diff_q_norm_g, diff_k_norm_g, diff_lambda_q1, diff_lambda_k1, diff_lambda_q2,
           diff_lambda_k2, diff_subln_g, nsa_q_norm_g, nsa_k_norm_g, cmp_pos, cmp_w1, cmp_b1, cmp_w2, w_out,
           _dbg=False, _nb=NB, _ncores=NCORES):
    f = lambda a: np.ascontiguousarray(np.asarray(a, dtype=np.float32))
    x = f(x)
    consts = make_consts()
    shared = {
        "w_in": permute_w_in(f(w_in)[0]), "w_out": f(w_out)[0], "cmp_w1": f(cmp_w1)[0], "cmp_w2": f(cmp_w2)[0],
        "cmp_b1": f(cmp_b1)[0], "cmp_pos": f(cmp_pos)[0], "norm_g": f(norm_g)[0],
        "diff_q_norm_g": f(diff_q_norm_g)[0], "diff_k_norm_g": f(diff_k_norm_g)[0],
        "diff_lambda_q1": f(diff_lambda_q1)[0], "diff_lambda_k1": f(diff_lambda_k1)[0],
        "diff_lambda_q2": f(diff_lambda_q2)[0], "diff_lambda_k2": f(diff_lambda_k2)[0],
        "diff_subln_g": f(diff_subln_g)[0], "nsa_q_norm_g": f(nsa_q_norm_g)[0], "nsa_k_norm_g": f(nsa_k_norm_g)[0],
    }
    for k_, v in consts.items():
        shared["c_" + k_] = v
    nc = bass.Bass("TRN2", target_bir_lowering=False)
    with ExitStack() as es:
        info = build(nc, es, nb=_nb, dbg=_dbg)
    in_maps = []
    for c in range(_ncores):
        m = dict(shared)
        m["x"] = np.ascontiguousarray(x[c * _nb:(c + 1) * _nb])
        in_maps.append(m)
    res = run_bass_kernel_spmd(nc, in_maps, core_ids=list(range(_ncores)))
    out = np.concatenate([np.asarray(r["y"], dtype=np.float32) for r in res.results], axis=0)
    if _dbg:
        return out, res.results[0]["dbg_mix"], info
    return out
```

```python
import numpy as np
import ml_dtypes
import concourse.bass as bass
import concourse.mybir as mybir
from concourse.bass_utils import run_bass_kernel_spmd
from contextlib import ExitStack
from collections import deque

F32 = mybir.dt.float32
BF16 = mybir.dt.bfloat16
ALU = mybir.AluOpType
AF = mybir.ActivationFunctionType
AX = mybir.AxisListType
BF = ml_dtypes.bfloat16

NCORES = 8
NB = 4
T = 2048
D = 1024
NEG = -30000.0
EPS = 1e-6
NGRP = 9
NCOL = NGRP * 512
DSL = [2.0 ** (-2.0 * (h + 1)) for h in range(4)]
NSL = [2.0 ** (-1.0 * (h + 1)) for h in range(8)]


class Buf:
    __slots__ = ("name", "last_w", "readers", "psum")

    def __init__(self, name, psum=False):
        self.name = name
        self.last_w = None
        self.readers = {}
        self.psum = psum


class Op:
    __slots__ = ("eng", "fn", "deps", "need_sig", "sem", "val", "is_dma", "idx", "inc")

    def __init__(self, eng, fn, is_dma):
        self.eng = eng
        self.fn = fn
        self.deps = {}
        self.need_sig = is_dma
        self.sem = None
        self.val = 0
        self.is_dma = is_dma
        self.inc = 16 if is_dma else 1


class Prog:
    NDMA = 8

    def __init__(self, nc):
        self.nc = nc
        self.engs = {"pe": nc.tensor, "act": nc.scalar, "dve": nc.vector,
                     "pool": nc.gpsimd, "sp": nc.sync}
        self.ops = {k: [] for k in self.engs}
        self.dma_ops = {k: [] for k in self.engs}
        self.nops = 0

    def _key(self, op):
        return ("d", id(op)) if op.is_dma else op.eng

    def _add_dep(self, op, dep):
        if dep is None or dep is op:
            return
        if (not dep.is_dma) and dep.eng == op.eng and op.eng == "pe" and not op.is_dma:
            return
        k = self._key(dep)
        old = op.deps.get(k)
        if old is None or old.idx < dep.idx:
            op.deps[k] = dep

    def op(self, eng, fn, reads=(), writes=(), is_dma=False):
        o = Op(eng, fn, is_dma)
        o.idx = self.nops
        self.nops += 1
        for b in reads:
            self._add_dep(o, b.last_w)
            if b.psum:
                for r in b.readers.values():
                    if r.eng != eng:
                        self._add_dep(o, r)
        for b in writes:
            self._add_dep(o, b.last_w)
            for r in b.readers.values():
                self._add_dep(o, r)
        for b in reads:
            b.readers[self._key(o)] = o
        for b in writes:
            b.last_w = o
            b.readers = {}
        if is_dma:
            lst = self.dma_ops[eng]
            if len(lst) >= self.NDMA:
                self._add_dep(o, lst[len(lst) - self.NDMA])
            lst.append(o)
        self.ops[eng].append(o)
        return o

    def dma(self, eng, out, in_, reads=(), writes=(), **kw):
        return self.op(eng, lambda e: e.dma_start(out=out, in_=in_, **kw), reads, writes, is_dma=True)

    def emit(self, sems):
        for k, lst in self.ops.items():
            for o in lst:
                for d in o.deps.values():
                    d.need_sig = True
        for k, lst in self.ops.items():
            cnt = 0
            dcnt = 0
            for o in lst:
                if o.is_dma:
                    o.sem = sems["dma_%s_%d" % (k, dcnt % self.NDMA)]
                    o.val = 16 * (dcnt // self.NDMA + 1)
                    dcnt += 1
                elif o.need_sig:
                    cnt += 1
                    o.sem = sems[k]
                    o.val = cnt
        nwait = 0
        for k, lst in self.ops.items():
            e = self.engs[k]
            waited = {}
            for o in lst:
                for d in o.deps.values():
                    sid = id(d.sem)
                    if waited.get(sid, 0) >= d.val:
                        continue
                    waited[sid] = d.val
                    e.wait_ge(d.sem, d.val)
                    nwait += 1
                ins = o.fn(e)
                if o.need_sig:
                    ins.then_inc(o.sem, o.inc)
        return nwait


def make_consts():
    c = {}
    c["ident"] = np.eye(128, dtype=np.float32).astype(BF)
    blk = np.zeros((128, 128), np.float32)
    blk[:64, :64] = 1.0 / 64
    blk[64:, 64:] = 1.0 / 64
    c["blkones"] = blk.astype(BF)
    kk = np.arange(128)[:, None]
    qq = np.arange(128)[None, :]
    c["tri_c"] = np.where(qq >= kk, 0.0, NEG).astype(np.float32).astype(BF)
    c["tri_w"] = np.where(qq < kk, 0.0, NEG).astype(np.float32).astype(BF)
    cc = np.arange(128)[:, None]
    tt = np.arange(T)[None, :]
    c["cmask"] = np.where((16 * cc + 31 <= tt) & (cc < 127), 0.0, NEG).astype(np.float32).astype(BF)
    cstart = np.arange(127) * 16
    sstart = np.arange(32) * 64
    ovl = ((cstart[:, None] < sstart[None, :] + 64) & (cstart[:, None] + 32 > sstart[None, :]))
    oa = np.zeros((128, 33), np.float32)
    oa[:127, :32] = ovl
    oa[:127, 32] = 1.0
    c["ovl"] = oa.astype(BF)
    t = np.arange(T)[:, None]
    j = np.arange(32)[None, :]
    blk_t = t // 64
    valid = j <= blk_t
    forced = valid & ((j == 0) | (j >= blk_t - 1))
    cvalid = valid.astype(np.float32)
    cbias = (cvalid - 1.0) + 1000.0 * forced.astype(np.float32)
    c["cvalid"] = np.ascontiguousarray(cvalid.reshape(16, 128, 32).transpose(1, 0, 2))
    c["cbias"] = np.ascontiguousarray(cbias.reshape(16, 128, 32).transpose(1, 0, 2))
    kpos = np.arange(T)
    kaux = np.stack([128.0 * (kpos // 128), (kpos % 128).astype(np.float64),
                     np.ones(T), np.ones(T)]).astype(np.float32)
    c["kaux"] = kaux.astype(BF)
    c["dkaux"] = np.stack([DSL[h] * kaux for h in range(4)]).astype(np.float32).astype(BF)
    qrel = np.arange(512)
    qa = (qrel // 128).astype(np.float32)
    qb = (qrel % 128).astype(np.float32)
    c["qaux_d"] = np.stack([np.ones(512), np.ones(512), -128.0 * qa, -qb]).astype(np.float32).astype(BF)
    c["qaux_n"] = np.stack([np.stack([np.full(512, s), np.full(512, s), -s * 128.0 * qa, -s * qb])
                            for s in NSL]).astype(np.float32).astype(BF)
    c["expand"] = (np.arange(T)[None, :] // 64 == np.arange(32)[:, None]).astype(np.float32).astype(BF)
    return c


def permute_w_in(w):
    NQ0, NKV, NZ, NG = 2048, 2560, 3328, 3840
    cols = []
    for h in range(4):
        cols += [w[:, h * 128:(h + 1) * 128], w[:, 512 + h * 128:512 + (h + 1) * 128],
                 w[:, 1024 + h * 128:1024 + (h + 1) * 128], w[:, 1536 + h * 128:1536 + (h + 1) * 128]]
    kv = lambda s, k: w[:, NKV + s * 128 + k * 64:NKV + s * 128 + (k + 1) * 64]
    cols += [kv(2, 0), kv(2, 1), kv(4, 0), kv(4, 1), kv(0, 0), kv(0, 0), kv(0, 1), kv(0, 1)]
    cols += [kv(1, 0), kv(1, 0), kv(1, 1), kv(1, 1), kv(3, 0), kv(3, 1), kv(5, 0), kv(5, 1)]
    cols += [w[:, NZ:NZ + 512]]
    for jj in range(4):
        cols += [w[:, NQ0 + jj * 64:NQ0 + (jj + 1) * 64], w[:, NQ0 + (4 + jj) * 64:NQ0 + (5 + jj) * 64]]
    cols += [w[:, NG:NG + 24], np.zeros((w.shape[0], 512 - 24), w.dtype)]
    out = np.ascontiguousarray(np.concatenate(cols, axis=1))
    assert out.shape[1] == NCOL
    return out


def build(nc, es, nb=NB, dbg=False):
    P = Prog(nc)
    total_sb = [0]

    def sb(name, shape, dt):
        n = 1
        for s in shape[1:]:
            n *= s
        total_sb[0] += n * (4 if dt == F32 else 2)
        return es.enter_context(nc.sbuf_tensor(name, shape, dt))

    def din(name, shape, dt):
        return nc.dram_tensor(name, shape, dt, kind="ExternalInput").ap()

    x = din("x", [nb, T, D], F32)
    y = nc.dram_tensor("y", [nb, T, D], F32, kind="ExternalOutput").ap()
    w_in = din("w_in", [D, NCOL], F32)
    w_out = din("w_out", [D, D], F32)
    w1_d = din("cmp_w1", [2, 2048, 256], F32)
    w2_d = din("cmp_w2", [2, 256, 64], F32)
    b1_d = din("cmp_b1", [2, 256], F32)
    pos_d = din("cmp_pos", [2, 32, 64], F32)
    normg_d = din("norm_g", [D], F32)
    dqg_d = din("diff_q_norm_g", [64], F32)
    dkg_d = din("diff_k_norm_g", [64], F32)
    lq1_d = din("diff_lambda_q1", [64], F32)
    lk1_d = din("diff_lambda_k1", [64], F32)
    lq2_d = din("diff_lambda_q2", [64], F32)
    lk2_d = din("diff_lambda_k2", [64], F32)
    subg_d = din("diff_subln_g", [128], F32)
    nqg_d = din("nsa_q_norm_g", [64], F32)
    nkg_d = din("nsa_k_norm_g", [3, 64], F32)
    c_ident = din("c_ident", [128, 128], BF16)
    c_blk = din("c_blkones", [128, 128], BF16)
    c_tric = din("c_tri_c", [128, 128], BF16)
    c_triw = din("c_tri_w", [128, 128], BF16)
    c_cmask = din("c_cmask", [128, T], BF16)
    c_ovl = din("c_ovl", [128, 33], BF16)
    c_cvalid = din("c_cvalid", [128, 16, 32], F32)
    c_cbias = din("c_cbias", [128, 16, 32], F32)
    c_kaux = din("c_kaux", [4, T], BF16)
    c_dkaux = din("c_dkaux", [4, 4, T], BF16)
    c_qauxd = din("c_qaux_d", [4, 512], BF16)
    c_qauxn = din("c_qaux_n", [8, 4, 512], BF16)
    c_expand = din("c_expand", [32, T], BF16)
    if dbg:
        dbg_mix = nc.dram_tensor("dbg_mix", [128, 16, 1024], BF16, kind="ExternalOutput").ap()

    banks = [es.enter_context(nc.psum_tensor("bank%d" % i, [128, 512], F32)) for i in range(8)]
    BK = [Buf("bank%d" % i, psum=True) for i in range(8)]
    pools = {"st": [0, 1, 2], "acc": [3, 4, 5, 6], "misc": [7]}
    pptr = {"st": 0, "acc": 0, "misc": 0}

    def nxt(pool):
        ids = pools[pool]
        i = ids[pptr[pool] % len(ids)]
        pptr[pool] += 1
        return i

    def bbf(i):
        return banks[i][:, :].bitcast(BF16)

    hT = sb("hT", [128, 8, T], BF16)
    HT = Buf("hT")
    mix = sb("mix", [128, 16, 1024], BF16)
    MIX = Buf("mix")
    wbuf = [sb("wbuf%d" % i, [128, 8, 512], BF16) for i in range(2)]
    WB = [Buf("wb%d" % i) for i in range(2)]
    wst = [sb("wst%d" % i, [128, 512], F32) for i in range(4)]
    WS = [Buf("ws%d" % i) for i in range(4)]
    xs = [sb("xs%d" % i, [128, 1024], F32) for i in range(2)]
    XS = [Buf("xs%d" % i) for i in range(2)]
    xn = sb("xn", [128, 1024], BF16)
    XN = Buf("xn")
    junk = sb("junk", [128, 1024], BF16)
    KSA = sb("KSA", [128, T], BF16)
    KSB = sb("KSB", [128, T], BF16)
    KWA = sb("KWA", [128, T], BF16)
    KWB = sb("KWB", [128, T], BF16)
    B_KSA, B_KSB, B_KWA, B_KWB = Buf("KSA"), Buf("KSB"), Buf("KWA"), Buf("KWB")
    QD = [[sb("QD%d%d" % (s, c), [128, 512], BF16) for c in range(2)] for s in range(2)]
    B_QD = [[Buf("QD%d%d" % (s, c)) for c in range(2)] for s in range(2)]
    Vd = sb("Vd", [128, 16, 129], BF16)
    B_Vd = Buf("Vd")
    Zd = sb("Zd", [128, 16, 128], BF16)
    B_Zd = Buf("Zd")
    QN = [sb("QN%d" % h, [128, 512], BF16) for h in range(8)]
    B_QN = [Buf("QN%d" % h) for h in range(8)]
    Vn = sb("Vn", [128, 16, 4, 65], BF16)
    B_Vn = Buf("Vn")
    Zq = sb("Zq", [128, 4, 512], BF16)
    B_Zq = Buf("Zq")
    Gt = sb("Gt", [128, 16, 24], F32)
    B_Gt = Buf("Gt")
    KV2 = sb("KV2", [128, T], BF16)
    B_KV2 = Buf("KV2")
    kc = [sb("kc%d" % k, [128, 128], BF16) for k in range(2)]
    B_kc = [Buf("kc%d" % k) for k in range(2)]
    vc = [sb("vc%d" % k, [128, 65], BF16) for k in range(2)]
    B_vc = [Buf("vc%d" % k) for k in range(2)]
    hid = sb("hid", [128, 2, 128], BF16)
    B_hid = Buf("hid")
    ON = sb("ON", [128, 4, 512], F32)
    B_ON = Buf("ON")
    IA = [sb("IA%d" % k, [128, 4, 32], F32) for k in range(2)]
    B_IA = [Buf("IA%d" % k) for k in range(2)]
    sc32 = sb("sc32", [128, 4, 32], F32)
    B_sc32 = Buf("sc32")
    m8 = sb("m8", [128, 4, 8], F32)
    B_m8 = Buf("m8")
    SBT = sb("SBT", [128, 4, 128], BF16)
    B_SBT = Buf("SBT")
    tmp64 = sb("tmp64", [128, 4, 64], F32)
    B_tmp64 = Buf("tmp64")
    tmp32 = sb("tmp32", [128, 4, 32], F32)
    B_tmp32 = Buf("tmp32")
    pt = [sb("pt%d" % i, [128, 512], BF16) for i in range(4)]
    PTB = [Buf("pt%d" % i) for i in range(4)]
    sqb = sb("sqb", [128, 512], BF16)
    B_sqb = Buf("sqb")
    lnb = sb("lnb", [128, 512], F32)
    B_lnb = Buf("lnb")
    rsb = sb("rsb", [128, 512], F32)
    B_rsb = Buf("rsb")
    t0 = sb("t0", [128, 4, 128], F32)
    t1 = sb("t1", [128, 4, 128], F32)
    B_t0, B_t1 = Buf("t0"), Buf("t1")
    small = sb("small", [128, 64], F32)
    B_small = Buf("small")
    mT = [sb("mT%d" % i, [128, 8, 128], BF16) for i in range(2)]
    B_mT = [Buf("mT%d" % i) for i in range(2)]
    ident = sb("ident", [128, 128], BF16)
    blkones = sb("blkones", [128, 128], BF16)
    tri_c = sb("tri_c", [128, 128], BF16)
    tri_w = sb("tri_w", [128, 128], BF16)
    cmask = sb("cmask", [128, T], BF16)
    ovl = sb("ovl", [128, 33], BF16)
    cvalid = sb("cvalid", [128, 16, 32], F32)
    cbias = sb("cbias", [128, 16, 32], F32)
    normg = sb("normg", [128, 8], F32)
    gcols = sb("gcols", [128, 8], F32)
    gsub = sb("gsub", [128, 128], F32)
    lamt = sb("lamt", [128, 4, 64], F32)
    lamc = sb("lamc", [128, 8], F32)
    w2k = [sb("w2k%d" % k, [128, 2, 128], BF16) for k in range(2)]
    w2v = sb("w2v", [128, 2, 64], BF16)
    w2st = sb("w2st", [128, 2, 2, 64], F32)
    b1c = sb("b1c", [128, 2, 2], F32)
    bias_h = sb("bias_h", [128, 2, 2], F32)
    posst = sb("posst", [128, 2, 16], F32)
    pos2 = sb("pos2", [128, 2, 16], BF16)
    CONST = Buf("const")

    sems_names = ["pe", "act", "dve", "pool", "sp"] + ["dma_sp_%d" % i for i in range(8)]
    sems = {n: es.enter_context(nc.semaphore(n)) for n in sems_names}

    def cload(dst, src, **kw):
        P.dma("sp", dst, src, writes=[CONST], **kw)

    cload(ident[:, :], c_ident)
    cload(blkones[:, :], c_blk)
    cload(tri_c[:, :], c_tric)
    cload(tri_w[:, :], c_triw)
    cload(cmask[:, :], c_cmask)
    cload(ovl[:, :], c_ovl)
    cload(cvalid[:, :, :], c_cvalid)
    cload(cbias[:, :, :], c_cbias)
    cload(normg[:, :], normg_d.rearrange("(m p) -> p m", p=128), allow_slow_non_contiguous=True)

    def colload(col, src):
        for half in range(2):
            cload(gcols[half * 64:(half + 1) * 64, col:col + 1], src.rearrange("(p o) -> p o", o=1))

    colload(0, dqg_d)
    colload(1, dkg_d)
    colload(2, nqg_d)
    for i in range(3):
        colload(3 + i, nkg_d[i, :])

    def bcast_rows(src1d, n):
        return bass.AP(tensor=src1d.tensor, offset=src1d.offset, ap=[[0, 128], [1, n]])

    cload(gsub[:, :], bcast_rows(subg_d, 128))
    for i, l in enumerate([lq1_d, lk1_d, lq2_d, lk2_d]):
        cload(lamt[:, i, :], bcast_rows(l, 64))
    for kv in range(2):
        cload(b1c[:, kv, :], b1_d[kv, :].rearrange("(jt j) -> j jt", j=128), allow_slow_non_contiguous=True)
        cload(w2st[:, kv, :, :], w2_d[kv, :, :].rearrange("(jt j) d -> j jt d", j=128))
        cload(posst[:, kv, :], pos_d[kv, :, :].rearrange("(l2 two) d -> (two d) l2", two=2),
              allow_slow_non_contiguous=True)

    def mz(eng, ap):
        P.op(eng, lambda e: e.memset(ap, 0.0), [], [CONST])

    def m1(eng, ap):
        P.op(eng, lambda e: e.memset(ap, 1.0), [], [CONST])

    for tl in (KSA, KSB, KWA, KWB):
        mz("pool", tl[:, :])
    for s in range(2):
        for c in range(2):
            mz("pool", QD[s][c][:, :])
    for h in range(8):
        mz("pool", QN[h][:, :])
    mz("pool", SBT[:, :, :])
    for k in range(2):
        mz("pool", kc[k][:, :])
        mz("pool", vc[k][:, :])
        mz("pool", w2k[k][:, :, :])
        m1("pool", vc[k][:, 64:65])
    mz("pool", hid[:, :, :])
    m1("pool", Vd[:, :, 128:129])
    m1("pool", Vn[:, :, :, 64:65])
    cload(KSA[96:100, :], c_kaux)
    cload(KSB[32:36, :], c_kaux)
    cload(KSA[64:96, :], c_expand)
    cload(KSB[0:32, :], c_expand)
    for s in range(2):
        cload(QD[s][0][96:100, :], c_qauxd)
        cload(QD[s][1][32:36, :], c_qauxd)
    for h in range(8):
        if h < 4:
            cload(QN[h][96:100, :], c_qauxn[h])
        else:
            cload(QN[h][32:36, :], c_qauxn[h])
    P.op("dve", lambda e: e.tensor_scalar(out=gcols[:, 0:1], in0=gcols[:, 0:1], scalar1=0.125, scalar2=None,
                                          op0=ALU.mult), [CONST], [CONST])
    P.op("dve", lambda e: e.tensor_scalar(out=gcols[:, 2:3], in0=gcols[:, 2:3], scalar1=0.125, scalar2=None,
                                          op0=ALU.mult), [CONST], [CONST])
    P.op("dve", lambda e: e.tensor_scalar(out=gsub[:, :], in0=gsub[:, :], scalar1=0.8, scalar2=None,
                                          op0=ALU.mult), [CONST], [CONST])
    P.op("dve", lambda e: e.tensor_tensor(out=lamt[:, 0, :], in0=lamt[:, 0, :], in1=lamt[:, 1, :], op=ALU.mult),
         [CONST], [CONST])
    P.op("dve", lambda e: e.tensor_tensor(out=lamt[:, 2, :], in0=lamt[:, 2, :], in1=lamt[:, 3, :], op=ALU.mult),
         [CONST], [CONST])
    P.op("dve", lambda e: e.reduce_sum(out=lamc[:, 0:1], in_=lamt[:, 0, :], axis=AX.X), [CONST], [CONST])
    P.op("dve", lambda e: e.reduce_sum(out=lamc[:, 1:2], in_=lamt[:, 2, :], axis=AX.X), [CONST], [CONST])
    P.op("act", lambda e: e.activation(out=lamc[:, 2:4], in_=lamc[:, 0:2], func=AF.Exp), [CONST], [CONST])
    P.op("dve", lambda e: e.scalar_tensor_tensor(out=lamc[:, 4:5], in0=lamc[:, 3:4], scalar=-0.2, in1=lamc[:, 2:3],
                                                 op0=ALU.add, op1=ALU.subtract), [CONST], [CONST])
    for k in range(2):
        P.op("dve", lambda e, k=k: e.tensor_copy(out=w2k[k][:, :, k * 64:(k + 1) * 64], in_=w2st[:, 0, :, :]),
             [CONST], [CONST])
    P.op("dve", lambda e: e.tensor_copy(out=w2v[:, :, :], in_=w2st[:, 1, :, :]), [CONST], [CONST])
    P.op("dve", lambda e: e.tensor_copy(out=pos2[:, :, :], in_=posst[:, :, :]), [CONST], [CONST])

    wctr = [0]
    stc = [0]

    def stage_cast(k, dst_ap, src_ap, scale_ap):
        s = stc[0] % 4
        stc[0] += 1
        P.dma("sp", wst[s][:, :], src_ap, writes=[WS[s]])
        if scale_ap is not None:
            P.op("dve", lambda e: e.tensor_scalar(out=dst_ap, in0=wst[s][:, :], scalar1=scale_ap, scalar2=None,
                                                  op0=ALU.mult), [WS[s], CONST], [WB[k]])
        else:
            P.op("dve", lambda e: e.tensor_copy(out=dst_ap, in_=wst[s][:, :]), [WS[s]], [WB[k]])

    def load_group(g):
        k = wctr[0] % 2
        wctr[0] += 1
        for m in range(8):
            stage_cast(k, wbuf[k][:, m, :], w_in[m * 128:(m + 1) * 128, g * 512:(g + 1) * 512], normg[:, m:m + 1])
        return k

    def load_wout(n):
        k = wctr[0] % 2
        wctr[0] += 1
        for m in range(8):
            stage_cast(k, wbuf[k][:, m, :], w_out[m * 128:(m + 1) * 128, n * 512:(n + 1) * 512], None)
        return k

    def load_w1(kv):
        k = wctr[0] % 2
        wctr[0] += 1
        src = w1_d[kv, :, :].rearrange("(c p) j -> p c j", p=128)
        wv = wbuf[k][:, :, :].rearrange("p m n -> p (m n)").rearrange("p (c j) -> p c j", j=256)
        for i in range(8):
            s = stc[0] % 4
            stc[0] += 1
            P.dma("sp", wst[s][:, :].rearrange("p (c j) -> p c j", j=256), src[:, 2 * i:2 * i + 2, :], writes=[WS[s]])
            P.op("dve", lambda e, s=s, i=i: e.tensor_copy(out=wv[:, 2 * i:2 * i + 2, :],
                                                          in_=wst[s][:, :].rearrange("p (c j) -> p c j", j=256)),
                 [WS[s]], [WB[k]])
        return k, wv

    for kv in range(2):
        k, wv = load_w1(kv)
        bk = nxt("misc")
        for jt in range(2):
            for l2 in range(16):
                P.op("pe", lambda e, jt=jt, l2=l2, wv=wv, kv=kv, bk=bk: e.matmul(
                    banks[bk][:, jt:jt + 1], lhsT=wv[:, l2, jt * 128:(jt + 1) * 128], rhs=pos2[:, kv, l2:l2 + 1],
                    start=(l2 == 0 and jt == 0), stop=True, skip_group_check=True), [WB[k], CONST], [BK[bk]])
        P.op("dve", lambda e, kv=kv, bk=bk: e.tensor_tensor(out=bias_h[:, kv, :], in0=banks[bk][:, 0:2],
                                                            in1=b1c[:, kv, :], op=ALU.add), [BK[bk], CONST], [CONST])

    def qknorm(pb, n, gcol, dsts):
        P.op("act", lambda e: e.activation(out=sqb[:, :n], in_=banks[pb][:, :n], func=AF.Square), [BK[pb]], [B_sqb])
        mb = nxt("misc")
        P.op("pe", lambda e: e.matmul(banks[mb][:, :n], lhsT=blkones[:, :], rhs=sqb[:, :n], start=True, stop=True),
             [B_sqb, CONST], [BK[mb]])
        P.op("act", lambda e: e.activation(out=lnb[:, :n], in_=banks[mb][:, :n], func=AF.Ln, bias=EPS),
             [BK[mb]], [B_lnb])
        P.op("act", lambda e: e.activation(out=rsb[:, :n], in_=lnb[:, :n], func=AF.Exp, scale=-0.5),
             [B_lnb], [B_rsb])
        for (dst, lo, hi, db) in dsts:
            P.op("dve", lambda e, dst=dst, lo=lo, hi=hi: e.scalar_tensor_tensor(
                out=dst, in0=banks[pb][lo:hi, :n], scalar=gcol[lo:hi, 0:1], in1=rsb[lo:hi, :n],
                op0=ALU.mult, op1=ALU.mult), [BK[pb], B_rsb, CONST], [db])

    def proj_fm(k, c0, tq, pb, ncols=512):
        for m in range(8):
            P.op("pe", lambda e, m=m: e.matmul(banks[pb][:, :ncols], lhsT=wbuf[k][:, m, c0:c0 + 128],
                                               rhs=hT[:, m, tq * 512:tq * 512 + ncols], start=(m == 0), stop=(m == 7)),
                 [WB[k], HT], [BK[pb]])

    def proj_tm(k, c0, n, tt, out_ap, pb, first):
        for m in range(8):
            P.op("pe", lambda e, m=m: e.matmul(out_ap, lhsT=hT[:, m, tt * 128:(tt + 1) * 128],
                                               rhs=wbuf[k][:, m, c0:c0 + n], start=(first and m == 0), stop=True,
                                               skip_group_check=True), [WB[k], HT], [BK[pb]])

    pend = deque()
    npv = [0]
    DEPTH = 2
    ptc = [0]

    def drain(limit):
        while pend and npv[0] > limit:
            kind, fn = pend.popleft()
            if kind == "pv":
                npv[0] -= 1
            fn()
        while pend and pend[0][0] == "fin":
            pend.popleft()[1]()

    def flush():
        while pend:
            kind, fn = pend.popleft()
            if kind == "pv":
                npv[0] -= 1
            fn()

    def attention(tiles, Qt, QB, KB, VB, acc_ap, acc_bank, bias, fin):
        started = set()
        for d in tiles:
            sbk = nxt("st")
            c0, c1 = d["c0"], d["c1"]
            ex = d.get("extra")
            P.op("pe", lambda e, d=d, sbk=sbk, c0=c0, c1=c1, ex=ex: e.matmul(
                banks[sbk][:, c0:c1], lhsT=d["k_ap"], rhs=Qt[:, c0:c1], start=True, stop=(ex is None)),
                [KB, QB], [BK[sbk]])
            if ex is not None:
                P.op("pe", lambda e, sbk=sbk, ex=ex: e.matmul(
                    banks[sbk][:, ex[1]:ex[1] + ex[2]], lhsT=ident[:, :], rhs=ex[0], start=False, stop=True),
                    [CONST], [BK[sbk]])
            pi = ptc[0] % 4
            ptc[0] += 1
            P.op("act", lambda e, sbk=sbk, pi=pi, c0=c0, c1=c1: e.activation(
                out=pt[pi][:, c0:c1], in_=banks[sbk][:, c0:c1], func=AF.Exp, bias=float(bias)),
                [BK[sbk]], [PTB[pi]])

            def pv(d=d, pi=pi, c0=c0, c1=c1):
                for s in range(c0 // 128, c1 // 128):
                    bk = acc_bank(s)
                    first = bk not in started
                    started.add(bk)
                    P.op("pe", lambda e, s=s, first=first: e.matmul(
                        acc_ap(s), lhsT=pt[pi][:, s * 128:(s + 1) * 128], rhs=d["v_ap"], start=first, stop=True,
                        skip_group_check=True), [PTB[pi], VB], [BK[bk]])
                if d.get("more") is not None:
                    d["more"](pi)
            pend.append(("pv", pv))
            npv[0] += 1
            drain(DEPTH)
        pend.append(("fin", fin))

    def causal_tiles(Kt, qi, vfn):
        tiles = []
        for kt in range(4 * qi):
            tiles.append(dict(k_ap=Kt[:, kt * 128:(kt + 1) * 128], c0=0, c1=512, v_ap=vfn(kt)))
        for j in range(4):
            kt = 4 * qi + j
            tiles.append(dict(k_ap=Kt[:, kt * 128:(kt + 1) * 128], c0=128 * j, c1=512,
                              extra=(tri_c[:, :], 128 * j, 128), v_ap=vfn(kt)))
        return tiles

    def window_tiles(Kt, qi, vfn):
        tiles = []
        if qi > 0:
            for j in range(4):
                kt = 4 * (qi - 1) + j
                tiles.append(dict(k_ap=Kt[:, kt * 128:(kt + 1) * 128], c0=0, c1=128 * (j + 1),
                                  extra=(tri_w[:, :], 128 * j, 128), v_ap=vfn(kt)))
        for j in range(4):
            kt = 4 * qi + j
            tiles.append(dict(k_ap=Kt[:, kt * 128:(kt + 1) * 128], c0=128 * j, c1=512,
                              extra=(tri_c[:, :], 128 * j, 128), v_ap=vfn(kt)))
        return tiles

    for b in range(nb):
        k_cur = load_group(0)
        for tt in range(16):
            xi = tt % 2
            P.dma("sp", xs[xi][:, :], x[b, tt * 128:(tt + 1) * 128, :], writes=[XS[xi]])
            P.op("act", lambda e, xi=xi: e.activation(out=junk[:, :], in_=xs[xi][:, :], func=AF.Square,
                                                      accum_out=small[:, 0:1]), [XS[xi]], [B_small])
            P.op("act", lambda e: e.activation(out=small[:, 1:2], in_=small[:, 0:1], func=AF.Ln, scale=1.0 / D,
                                               bias=EPS), [B_small], [B_small])
            P.op("act", lambda e: e.activation(out=small[:, 2:3], in_=small[:, 1:2], func=AF.Exp, scale=-0.5),
                 [B_small], [B_small])
            P.op("dve", lambda e, xi=xi: e.tensor_scalar(out=xn[:, :], in0=xs[xi][:, :], scalar1=small[:, 2:3],
                                                         scalar2=None, op0=ALU.mult), [XS[xi], B_small], [XN])
            bk = nxt("st")
            for m in range(8):
                P.op("pe", lambda e, m=m, bk=bk: e.transpose(out=bbf(bk)[:, m * 128:(m + 1) * 128],
                                                             in_=xn[:, m * 128:(m + 1) * 128], identity=ident[:, :]),
                     [XN, CONST], [BK[bk]])
            P.op("dve", lambda e, tt=tt, bk=bk: e.tensor_copy(
                out=hT[:, :, tt * 128:(tt + 1) * 128],
                in_=bbf(bk)[:, 0:1024].rearrange("p (m t) -> p m t", t=128)), [BK[bk]], [HT])

        KA, KB_, B_KA, B_KB = KWA, KWB, B_KWA, B_KWB
        for h in range(4):
            flush()
            k = k_cur
            P.dma("sp", KA[96:100, :], c_dkaux[h], writes=[B_KA])
            P.dma("sp", KB_[32:36, :], c_dkaux[h], writes=[B_KB])
            for tq in range(4):
                pb = nxt("st")
                proj_fm(k, 128, tq, pb)
                qknorm(pb, 512, gcols[:, 1:2],
                       [(KA[0:64, tq * 512:(tq + 1) * 512], 0, 64, B_KA),
                        (KB_[64:128, tq * 512:(tq + 1) * 512], 64, 128, B_KB)])
            for t4 in range(4):
                pb = nxt("st")
                for i in range(4):
                    proj_tm(k, 256, 128, t4 * 4 + i, banks[pb][:, i * 128:(i + 1) * 128], pb, first=(i == 0))
                P.op("dve", lambda e, t4=t4, pb=pb: e.tensor_copy(
                    out=Vd[:, t4 * 4:(t4 + 1) * 4, 0:128],
                    in_=banks[pb][:, :].rearrange("p (i c) -> p i c", c=128)), [BK[pb]], [B_Vd])
            for t4 in range(4):
                pb = nxt("st")
                for i in range(4):
                    proj_tm(k, 384, 128, t4 * 4 + i, banks[pb][:, i * 128:(i + 1) * 128], pb, first=(i == 0))
                P.op("act", lambda e, t4=t4, pb=pb: e.activation(
                    out=Zd[:, t4 * 4:(t4 + 1) * 4, :], in_=banks[pb][:, :].rearrange("p (i c) -> p i c", c=128),
                    func=AF.Silu), [BK[pb]], [B_Zd])
                P.op("dve", lambda e, t4=t4: e.tensor_tensor(
                    out=Zd[:, t4 * 4:(t4 + 1) * 4, :], in0=Zd[:, t4 * 4:(t4 + 1) * 4, :],
                    in1=gsub[:, :].unsqueeze(1).broadcast_to([128, 4, 128]), op=ALU.mult), [B_Zd, CONST], [B_Zd])
            k_cur = load_group(h + 1 if h < 3 else 4)

            def qproj(qi_, k=k):
                qs_ = qi_ % 2
                pb_ = nxt("st")
                proj_fm(k, 0, qi_, pb_)
                qknorm(pb_, 512, gcols[:, 0:1], [(QD[qs_][0][0:64, :], 0, 64, B_QD[qs_][0]),
                                                 (QD[qs_][1][64:128, :], 64, 128, B_QD[qs_][1])])

            qproj(0)
            for qi in range(4):
                qs = qi % 2
                if qi < 3:
                    qproj(qi + 1)
                for c in range(2):
                    Kt, KBf = (KA, B_KA) if c == 0 else (KB_, B_KB)
                    bx = nxt("acc")
                    by = nxt("acc")
                    tt_ = t0 if c == 0 else t1
                    Bt_ = B_t0 if c == 0 else B_t1

                    def acc_ap(s, bx=bx, by=by):
                        return banks[bx][:, s * 129:(s + 1) * 129] if s < 3 else banks[by][:, 0:129]

                    def acc_bank(s, bx=bx, by=by):
                        return bx if s < 3 else by

                    def fin(c=c, bx=bx, by=by, tt_=tt_, Bt_=Bt_, qi=qi, h=h):
                        xv = banks[bx][:, 0:387].rearrange("p (s c) -> p s c", c=129)
                        P.op("dve", lambda e: e.tensor_copy(out=small[:, 8:11], in_=xv[:, :, 128]),
                             [BK[bx]], [B_small])
                        P.op("dve", lambda e: e.tensor_copy(out=small[:, 11:12], in_=banks[by][:, 128:129]),
                             [BK[by]], [B_small])
                        P.op("dve", lambda e: e.reciprocal(out=small[:, 12:16], in_=small[:, 8:12]),
                             [B_small], [B_small])
                        P.op("dve", lambda e: e.tensor_tensor(
                            out=tt_[:, 0:3, :], in0=xv[:, :, 0:128],
                            in1=small[:, 12:15].unsqueeze(2).broadcast_to([128, 3, 128]), op=ALU.mult),
                            [BK[bx], B_small], [Bt_])
                        P.op("dve", lambda e: e.tensor_scalar(
                            out=tt_[:, 3, :], in0=banks[by][:, 0:128], scalar1=small[:, 15:16], scalar2=None,
                            op0=ALU.mult), [BK[by], B_small], [Bt_])
                        if c == 1:
                            P.op("dve", lambda e: e.scalar_tensor_tensor(
                                out=t0[:, :, :], in0=t1[:, :, :], scalar=lamc[:, 4:5], in1=t0[:, :, :],
                                op0=ALU.mult, op1=ALU.add), [B_t0, B_t1, CONST], [B_t0])
                            P.op("dve", lambda e: e.tensor_tensor(out=t1[:, :, :], in0=t0[:, :, :], in1=t0[:, :, :],
                                                                  op=ALU.mult), [B_t0], [B_t1])
                            P.op("dve", lambda e: e.reduce_sum(out=small[:, 16:20], in_=t1[:, :, :], axis=AX.X),
                                 [B_t1], [B_small])
                            P.op("act", lambda e: e.activation(out=small[:, 20:24], in_=small[:, 16:20], func=AF.Ln,
                                                               scale=1.0 / 128, bias=EPS), [B_small], [B_small])
                            P.op("act", lambda e: e.activation(out=small[:, 24:28], in_=small[:, 20:24], func=AF.Exp,
                                                               scale=-0.5), [B_small], [B_small])
                            P.op("dve", lambda e: e.tensor_tensor(
                                out=t0[:, :, :], in0=t0[:, :, :],
                                in1=small[:, 24:28].unsqueeze(2).broadcast_to([128, 4, 128]), op=ALU.mult),
                                [B_t0, B_small], [B_t0])
                            P.op("dve", lambda e: e.tensor_tensor(
                                out=mix[:, qi * 4:(qi + 1) * 4, h * 128:(h + 1) * 128], in0=t0[:, :, :],
                                in1=Zd[:, qi * 4:(qi + 1) * 4, :], op=ALU.mult), [B_t0, B_Zd], [MIX])

                    attention(causal_tiles(Kt, qi, lambda kt: Vd[:, kt, 0:129]), QD[qs][c], B_QD[qs][c], KBf, B_Vd,
                              acc_ap, acc_bank, -DSL[h] * 512.0 * qi, fin)
        flush()

        P.dma("sp", KWA[96:100, :], c_kaux, writes=[B_KWA])
        P.dma("sp", KWB[32:36, :], c_kaux, writes=[B_KWB])
        k = k_cur
        kw1, wv1 = load_w1(0)
        for tq in range(4):
            pb = nxt("st")
            proj_fm(k, 0, tq, pb)
            qknorm(pb, 512, gcols[:, 4:5], [(KSA[0:64, tq * 512:(tq + 1) * 512], 0, 64, B_KSA),
                                            (KSB[64:128, tq * 512:(tq + 1) * 512], 64, 128, B_KSB)])
            pb = nxt("st")
            proj_fm(k, 128, tq, pb)
            qknorm(pb, 512, gcols[:, 5:6], [(KWA[0:64, tq * 512:(tq + 1) * 512], 0, 64, B_KWA),
                                            (KWB[64:128, tq * 512:(tq + 1) * 512], 64, 128, B_KWB)])

        def kv2_fill(kw, c0):
            for tq in range(4):
                pb = nxt("st")
                proj_fm(kw, c0, tq, pb)
                P.op("dve", lambda e, tq=tq, pb=pb: e.tensor_copy(out=KV2[0:64, tq * 512:(tq + 1) * 512],
                                                                  in_=banks[pb][0:64, :]), [BK[pb]], [B_KV2])
                if tq == 0:
                    P.op("dve", lambda e, pb=pb: e.tensor_copy(out=KV2[64:128, 0:511], in_=banks[pb][64:128, 1:512]),
                         [BK[pb]], [B_KV2])
                else:
                    P.op("dve", lambda e, tq=tq, pb=pb: e.tensor_copy(
                        out=KV2[64:128, tq * 512 - 1:tq * 512 + 511], in_=banks[pb][64:128, :]), [BK[pb]], [B_KV2])

        def compress_hidden(kw1, wv, kv):
            for jt in range(2):
                pb = nxt("st")
                for l2 in range(16):
                    P.op("pe", lambda e, l2=l2, jt=jt, pb=pb: e.matmul(
                        banks[pb][:, 0:127], lhsT=wv[:, l2, jt * 128:(jt + 1) * 128],
                        rhs=KV2[:, 2 * l2:2 * l2 + 16 * 126 + 1:16], start=(l2 == 0), stop=(l2 == 15)),
                        [WB[kw1], B_KV2], [BK[pb]])
                P.op("act", lambda e, jt=jt, pb=pb: e.activation(out=hid[:, jt, 0:127], in_=banks[pb][:, 0:127],
                                                                 func=AF.Silu, bias=bias_h[:, kv, jt:jt + 1]),
                     [BK[pb], CONST], [B_hid])

        for kvh in range(2):
            kv2_fill(k, 256 + kvh * 128)
            if kvh == 1:
                k5 = load_group(5)
            compress_hidden(kw1, wv1, 0)
            pb = nxt("st")
            for jt in range(2):
                P.op("pe", lambda e, jt=jt, pb=pb, kvh=kvh: e.matmul(
                    banks[pb][:, 0:127], lhsT=w2k[kvh][:, jt, :], rhs=hid[:, jt, 0:127], start=(jt == 0),
                    stop=(jt == 1)), [B_hid, CONST], [BK[pb]])
            qknorm(pb, 127, gcols[:, 3:4], [(kc[kvh][0:64, 0:127], 0, 64, B_kc[kvh]),
                                            (kc[kvh][64:128, 0:127], 64, 128, B_kc[kvh])])
        k = k5
        kw1, wv1 = load_w1(1)
        for kvh in range(2):
            kv2_fill(k, kvh * 128)
            compress_hidden(kw1, wv1, 1)
            pb = nxt("st")
            for jt in range(2):
                P.op("pe", lambda e, jt=jt, pb=pb: e.matmul(
                    banks[pb][0:127, 0:64], lhsT=hid[:, jt, 0:127], rhs=w2v[:, jt, :], start=(jt == 0),
                    stop=(jt == 1)), [B_hid, CONST], [BK[pb]])
            P.op("dve", lambda e, pb=pb, kvh=kvh: e.tensor_copy(out=vc[kvh][0:127, 0:64], in_=banks[pb][0:127, 0:64]),
                 [BK[pb]], [B_vc[kvh]])
        for t2 in range(8):
            pb = nxt("st")
            for i in range(2):
                proj_tm(k, 256, 256, t2 * 2 + i, banks[pb][:, i * 256:(i + 1) * 256], pb, first=(i == 0))
            P.op("dve", lambda e, t2=t2, pb=pb: e.tensor_copy(
                out=Vn[:, t2 * 2:t2 * 2 + 2, :, 0:64],
                in_=banks[pb][:, :].rearrange("p (i g c) -> p i g c", g=4, c=64)), [BK[pb]], [B_Vn])
        k = load_group(8)
        pb = nxt("st")
        for tt in range(16):
            proj_tm(k, 0, 24, tt, banks[pb][:, tt * 24:(tt + 1) * 24], pb, first=(tt == 0))
        P.op("act", lambda e, pb=pb: e.activation(out=Gt[:, :, :],
                                                  in_=banks[pb][:, 0:384].rearrange("p (t g) -> p t g", g=24),
                                                  func=AF.Sigmoid), [BK[pb]], [B_Gt])
        kq = load_group(7)
        kz = load_group(6)
        for qi in range(4):
            for i in range(4):
                pb = nxt("st")
                proj_tm(kz, 0, 512, qi * 4 + i, banks[pb][:, :], pb, first=True)
                P.op("act", lambda e, i=i, pb=pb: e.activation(out=Zq[:, i, :], in_=banks[pb][:, :], func=AF.Silu),
                     [BK[pb]], [B_Zq])
            for j in range(4):
                pb = nxt("st")
                proj_fm(kq, j * 128, qi, pb)
                qknorm(pb, 512, gcols[:, 2:3], [(QN[j][0:64, :], 0, 64, B_QN[j]),
                                                (QN[4 + j][64:128, :], 64, 128, B_QN[4 + j])])

            if qi == 3:
                ko = [load_wout(0), load_wout(1)]

            def nsa_fin_factory(hd, br, ab, qi=qi):
                def fin():
                    av = banks[ab][:, 0:260].rearrange("p (s c) -> p s c", c=65)
                    o = 32 + (hd % 2) * 16
                    P.op("dve", lambda e: e.tensor_scalar(out=small[:, o:o + 4], in0=av[:, :, 64], scalar1=1e-30,
                                                          scalar2=None, op0=ALU.max), [BK[ab]], [B_small])
                    P.op("dve", lambda e: e.reciprocal(out=small[:, o + 4:o + 8], in_=small[:, o:o + 4]),
                         [B_small], [B_small])
                    P.op("dve", lambda e: e.tensor_tensor(out=small[:, o + 8:o + 12], in0=small[:, o + 4:o + 8],
                                                          in1=Gt[:, qi * 4:(qi + 1) * 4, br * 8 + hd], op=ALU.mult),
                         [B_small, B_Gt], [B_small])
                    cf = small[:, o + 8:o + 12].unsqueeze(2).broadcast_to([128, 4, 64])
                    if br == 0:
                        P.op("dve", lambda e: e.tensor_tensor(out=ON[:, :, hd * 64:(hd + 1) * 64],
                                                              in0=av[:, :, 0:64], in1=cf, op=ALU.mult),
                             [BK[ab], B_small], [B_ON])
                    else:
                        P.op("dve", lambda e: e.tensor_tensor(out=tmp64[:, :, :], in0=av[:, :, 0:64], in1=cf,
                                                              op=ALU.mult), [BK[ab], B_small], [B_tmp64])
                        P.op("dve", lambda e: e.tensor_tensor(out=ON[:, :, hd * 64:(hd + 1) * 64],
                                                              in0=ON[:, :, hd * 64:(hd + 1) * 64], in1=tmp64[:, :, :],
                                                              op=ALU.add), [B_tmp64, B_ON], [B_ON])
                return fin

            for hd in range(8):
                kvh = hd // 4
                ab = nxt("acc")
                ib = nxt("acc")

                def more(pi, ib=ib):
                    for s in range(4):
                        P.op("pe", lambda e, s=s: e.matmul(
                            banks[ib][:, s * 33:(s + 1) * 33], lhsT=pt[pi][:, s * 128:(s + 1) * 128], rhs=ovl[:, :],
                            start=(s == 0), stop=True, skip_group_check=True), [PTB[pi], CONST], [BK[ib]])

                base_fin = nsa_fin_factory(hd, 0, ab)

                def fin(base_fin=base_fin, ib=ib, hd=hd, kvh=kvh):
                    base_fin()
                    iv = banks[ib][:, 0:132].rearrange("p (s c) -> p s c", c=33)
                    P.op("dve", lambda e: e.tensor_scalar(out=small[:, 28:32], in0=iv[:, :, 32], scalar1=1e-30,
                                                          scalar2=None, op0=ALU.max), [BK[ib]], [B_small])
                    P.op("dve", lambda e: e.reciprocal(out=small[:, 4:8], in_=small[:, 28:32]), [B_small], [B_small])
                    rb = small[:, 4:8].unsqueeze(2).broadcast_to([128, 4, 32])
                    if hd % 4 == 0:
                        P.op("dve", lambda e: e.tensor_tensor(out=IA[kvh][:, :, :], in0=iv[:, :, 0:32], in1=rb,
                                                              op=ALU.mult), [BK[ib], B_small], [B_IA[kvh]])
                    else:
                        P.op("dve", lambda e: e.tensor_tensor(out=tmp32[:, :, :], in0=iv[:, :, 0:32], in1=rb,
                                                              op=ALU.mult), [BK[ib], B_small], [B_tmp32])
                        P.op("dve", lambda e: e.tensor_tensor(out=IA[kvh][:, :, :], in0=IA[kvh][:, :, :],
                                                              in1=tmp32[:, :, :], op=ALU.add),
                             [B_tmp32, B_IA[kvh]], [B_IA[kvh]])

                tiles = [dict(k_ap=kc[kvh][:, :], c0=0, c1=512, extra=(cmask[:, qi * 512:(qi + 1) * 512], 0, 512),
                              v_ap=vc[kvh][:, 0:65], more=more)]
                attention(tiles, QN[hd], B_QN[hd], B_kc[kvh], B_vc[kvh],
                          lambda s, ab=ab: banks[ab][:, s * 65:(s + 1) * 65], lambda s, ab=ab: ab, 0.0, fin)
            flush()
            for kvh in range(2):
                off = 64 if kvh == 0 else 0
                P.op("dve", lambda e, kvh=kvh, qi=qi: e.tensor_tensor(out=sc32[:, :, :], in0=IA[kvh][:, :, :],
                                                                      in1=cvalid[:, qi * 4:(qi + 1) * 4, :], op=ALU.mult),
                     [B_IA[kvh], CONST], [B_sc32])
                P.op("dve", lambda e, qi=qi: e.tensor_tensor(out=sc32[:, :, :], in0=sc32[:, :, :],
                                                             in1=cbias[:, qi * 4:(qi + 1) * 4, :], op=ALU.add),
                     [B_sc32, CONST], [B_sc32])
                for s in range(4):
                    P.op("dve", lambda e, s=s: e.max(out=m8[:, s, :], in_=sc32[:, s, :]), [B_sc32], [B_m8])
                for s in range(4):
                    P.op("dve", lambda e, s=s, off=off: e.tensor_scalar(
                        out=SBT[:, s, off:off + 32], in0=sc32[:, s, :], scalar1=m8[:, s, 7:8], scalar2=NEG,
                        op0=ALU.is_lt, op1=ALU.mult), [B_sc32, B_m8], [B_SBT])
            for hd in range(8):
                kvh = hd // 4
                Kw, BKw = (KWA, B_KWA) if kvh == 0 else (KWB, B_KWB)
                ab = nxt("acc")
                attention(window_tiles(Kw, qi, lambda kt, kvh=kvh: Vn[:, kt, 2 + kvh, :]), QN[hd], B_QN[hd], BKw, B_Vn,
                          lambda s, ab=ab: banks[ab][:, s * 65:(s + 1) * 65], lambda s, ab=ab: ab,
                          -NSL[hd] * 512.0 * qi, nsa_fin_factory(hd, 2, ab))
            tb = nxt("st")
            for s in range(4):
                P.op("pe", lambda e, s=s, tb=tb: e.transpose(out=bbf(tb)[:, s * 128:(s + 1) * 128], in_=SBT[:, s, :],
                                                             identity=ident[:, :]), [B_SBT, CONST], [BK[tb]])
            for j in range(4):
                P.op("dve", lambda e, j=j, tb=tb: e.tensor_copy(out=QN[j][64:96, :], in_=bbf(tb)[64:96, 0:512]),
                     [BK[tb]], [B_QN[j]])
                P.op("dve", lambda e, j=j, tb=tb: e.tensor_copy(out=QN[4 + j][0:32, :], in_=bbf(tb)[0:32, 0:512]),
                     [BK[tb]], [B_QN[4 + j]])
            for hd in range(8):
                kvh = hd // 4
                Ks, BKs = (KSA, B_KSA) if kvh == 0 else (KSB, B_KSB)
                ab = nxt("acc")
                attention(causal_tiles(Ks, qi, lambda kt, kvh=kvh: Vn[:, kt, kvh, :]), QN[hd], B_QN[hd], BKs, B_Vn,
                          lambda s, ab=ab: banks[ab][:, s * 65:(s + 1) * 65], lambda s, ab=ab: ab,
                          -NSL[hd] * 512.0 * qi, nsa_fin_factory(hd, 1, ab))
            flush()
            P.op("dve", lambda e, qi=qi: e.tensor_tensor(out=mix[:, qi * 4:(qi + 1) * 4, 512:1024], in0=ON[:, :, :],
                                                         in1=Zq[:, :, :], op=ALU.mult), [B_ON, B_Zq], [MIX])

        if dbg and b == 0:
            P.dma("sp", dbg_mix, mix[:, :, :], reads=[MIX])

        def prep_mT(tt_):
            mi_ = tt_ % 2
            tb_ = nxt("st")
            for c_ in range(8):
                P.op("pe", lambda e, c_=c_, tb_=tb_, tt_=tt_: e.transpose(
                    out=bbf(tb_)[:, c_ * 128:(c_ + 1) * 128], in_=mix[:, tt_, c_ * 128:(c_ + 1) * 128],
                    identity=ident[:, :]), [MIX, CONST], [BK[tb_]])
            P.op("dve", lambda e, mi_=mi_, tb_=tb_: e.tensor_copy(
                out=mT[mi_][:, :, :], in_=bbf(tb_)[:, 0:1024].rearrange("p (c t) -> p c t", t=128)),
                [BK[tb_]], [B_mT[mi_]])

        prep_mT(0)
        for tt in range(16):
            mi = tt % 2
            if tt < 15:
                prep_mT(tt + 1)
            xi = tt % 2
            P.dma("sp", xs[xi][:, :], x[b, tt * 128:(tt + 1) * 128, :], writes=[XS[xi]])
            for n in range(2):
                ob = nxt("acc")
                for c in range(8):
                    P.op("pe", lambda e, c=c, n=n, ob=ob, mi=mi, ko=ko: e.matmul(
                        banks[ob][:, :], lhsT=mT[mi][:, c, :], rhs=wbuf[ko[n]][:, c, :], start=(c == 0),
                        stop=(c == 7)), [B_mT[mi], WB[ko[n]]], [BK[ob]])
                P.op("dve", lambda e, n=n, ob=ob, xi=xi: e.tensor_tensor(
                    out=xs[xi][:, n * 512:(n + 1) * 512], in0=banks[ob][:, :], in1=xs[xi][:, n * 512:(n + 1) * 512],
                    op=ALU.add), [BK[ob], XS[xi]], [XS[xi]])
            P.dma("sp", y[b, tt * 128:(tt + 1) * 128, :], xs[xi][:, :], reads=[XS[xi]])

    fin_op = P.op("sp", lambda e: e.nop(), [], [])
    for d in P.dma_ops["sp"][-16:]:
        P._add_dep(fin_op, d)
    nw = P.emit(sems)
    return dict(nops=P.nops, nwaits=nw, sbuf_bytes=total_sb[0])


_CACHE = {}


def kernel(x, norm_g, w_in, diff_q_norm_g, diff_k_norm_g, diff_lambda_q1, diff_lambda_k1, diff_lambda_q2,
           diff_lambda_k2, diff_subln_g, nsa_q_norm_g, nsa_k_norm_g, cmp_pos, cmp_w1, cmp_b1, cmp_w2, w_out,
           _dbg=False, _nb=NB, _ncores=NCORES):
    f = lambda a: np.ascontiguousarray(np.asarray(a, dtype=np.float32))
    x = f(x)
    consts = make_consts()
    shared = {
        "w_in": permute_w_in(f(w_in)[0]), "w_out": f(w_out)[0], "cmp_w1": f(cmp_w1)[0], "cmp_w2": f(cmp_w2)[0],
        "cmp_b1": f(cmp_b1)[0], "cmp_pos": f(cmp_pos)[0], "norm_g": f(norm_g)[0],
        "diff_q_norm_g": f(diff_q_norm_g)[0], "diff_k_norm_g": f(diff_k_norm_g)[0],
        "diff_lambda_q1": f(diff_lambda_q1)[0], "diff_lambda_k1": f(diff_lambda_k1)[0],
        "diff_lambda_q2": f(diff_lambda_q2)[0], "diff_lambda_k2": f(diff_lambda_k2)[0],
        "diff_subln_g": f(diff_subln_g)[0], "nsa_q_norm_g": f(nsa_q_norm_g)[0], "nsa_k_norm_g": f(nsa_k_norm_g)[0],
    }
    for k_, v in consts.items():
        shared["c_" + k_] = v
    nc = bass.Bass("TRN2", target_bir_lowering=False)
    with ExitStack() as es:
        info = build(nc, es, nb=_nb, dbg=_dbg)
    in_maps = []
    for c in range(_ncores):
        m = dict(shared)
        m["x"] = np.ascontiguousarray(x[c * _nb:(c + 1) * _nb])
        in_maps.append(m)
    res = run_bass_kernel_spmd(nc, in_maps, core_ids=list(range(_ncores)))
    out = np.concatenate([np.asarray(r["y"], dtype=np.float32) for r in res.results], axis=0)
    if _dbg:
        return out, res.results[0]["dbg_mix"], info
    return out
```

```python
import numpy as np
import ml_dtypes
import concourse.bass as bass
import concourse.mybir as mybir
from concourse.bass_utils import run_bass_kernel_spmd
from contextlib import ExitStack
from collections import deque

F32 = mybir.dt.float32
BF16 = mybir.dt.bfloat16
ALU = mybir.AluOpType
AF = mybir.ActivationFunctionType
AX = mybir.AxisListType
BF = ml_dtypes.bfloat16

NCORES = 8
NB = 4
T = 2048
D = 1024
NEG = -30000.0
EPS = 1e-6
NGRP = 9
NCOL = NGRP * 512
DSL = [2.0 ** (-2.0 * (h + 1)) for h in range(4)]
NSL = [2.0 ** (-1.0 * (h + 1)) for h in range(8)]


class Buf:
    __slots__ = ("name", "last_w", "readers", "psum")

    def __init__(self, name, psum=False):
        self.name = name
        self.last_w = None
        self.readers = {}
        self.psum = psum


class Op:
    __slots__ = ("eng", "fn", "deps", "need_sig", "sem", "val", "is_dma", "idx", "inc")

    def __init__(self, eng, fn, is_dma):
        self.eng = eng
        self.fn = fn
        self.deps = {}
        self.need_sig = is_dma
        self.sem = None
        self.val = 0
        self.is_dma = is_dma
        self.inc = 16 if is_dma else 1


class Prog:
    NDMA = 8

    def __init__(self, nc):
        self.nc = nc
        self.engs = {"pe": nc.tensor, "act": nc.scalar, "dve": nc.vector,
                     "pool": nc.gpsimd, "sp": nc.sync}
        self.ops = {k: [] for k in self.engs}
        self.dma_ops = {k: [] for k in self.engs}
        self.nops = 0

    def _key(self, op):
        return ("d", id(op)) if op.is_dma else op.eng

    def _add_dep(self, op, dep):
        if dep is None or dep is op:
            return
        if (not dep.is_dma) and dep.eng == op.eng and op.eng == "pe" and not op.is_dma:
            return
        k = self._key(dep)
        old = op.deps.get(k)
        if old is None or old.idx < dep.idx:
            op.deps[k] = dep

    def op(self, eng, fn, reads=(), writes=(), is_dma=False):
        o = Op(eng, fn, is_dma)
        o.idx = self.nops
        self.nops += 1
        for b in reads:
            self._add_dep(o, b.last_w)
            if b.psum:
                for r in b.readers.values():
                    if r.eng != eng:
                        self._add_dep(o, r)
        for b in writes:
            self._add_dep(o, b.last_w)
            for r in b.readers.values():
                self._add_dep(o, r)
        for b in reads:
            b.readers[self._key(o)] = o
        for b in writes:
            b.last_w = o
            b.readers = {}
        if is_dma:
            lst = self.dma_ops[eng]
            if len(lst) >= self.NDMA:
                self._add_dep(o, lst[len(lst) - self.NDMA])
            lst.append(o)
        self.ops[eng].append(o)
        return o

    def dma(self, eng, out, in_, reads=(), writes=(), **kw):
        return self.op(eng, lambda e: e.dma_start(out=out, in_=in_, **kw), reads, writes, is_dma=True)

    def emit(self, sems):
        for k, lst in self.ops.items():
            for o in lst:
                for d in o.deps.values():
                    d.need_sig = True
        for k, lst in self.ops.items():
            cnt = 0
            dcnt = 0
            for o in lst:
                if o.is_dma:
                    o.sem = sems["dma_%s_%d" % (k, dcnt % self.NDMA)]
                    o.val = 16 * (dcnt // self.NDMA + 1)
                    dcnt += 1
                elif o.need_sig:
                    cnt += 1
                    o.sem = sems[k]
                    o.val = cnt
        nwait = 0
        for k, lst in self.ops.items():
            e = self.engs[k]
            waited = {}
            for o in lst:
                for d in o.deps.values():
                    sid = id(d.sem)
                    if waited.get(sid, 0) >= d.val:
                        continue
                    waited[sid] = d.val
                    e.wait_ge(d.sem, d.val)
                    nwait += 1
                ins = o.fn(e)
                if o.need_sig:
                    ins.then_inc(o.sem, o.inc)
        return nwait


def make_consts():
    c = {}
    c["ident"] = np.eye(128, dtype=np.float32).astype(BF)
    blk = np.zeros((128, 128), np.float32)
    blk[:64, :64] = 1.0 / 64
    blk[64:, 64:] = 1.0 / 64
    c["blkones"] = blk.astype(BF)
    kk = np.arange(128)[:, None]
    qq = np.arange(128)[None, :]
    c["tri_c"] = np.where(qq >= kk, 0.0, NEG).astype(np.float32).astype(BF)
    c["tri_w"] = np.where(qq < kk, 0.0, NEG).astype(np.float32).astype(BF)
    cc = np.arange(128)[:, None]
    tt = np.arange(T)[None, :]
    c["cmask"] = np.where((16 * cc + 31 <= tt) & (cc < 127), 0.0, NEG).astype(np.float32).astype(BF)
    cstart = np.arange(127) * 16
    sstart = np.arange(32) * 64
    ovl = ((cstart[:, None] < sstart[None, :] + 64) & (cstart[:, None] + 32 > sstart[None, :]))
    oa = np.zeros((128, 33), np.float32)
    oa[:127, :32] = ovl
    oa[:127, 32] = 1.0
    c["ovl"] = oa.astype(BF)
    t = np.arange(T)[:, None]
    j = np.arange(32)[None, :]
    blk_t = t // 64
    valid = j <= blk_t
    forced = valid & ((j == 0) | (j >= blk_t - 1))
    cvalid = valid.astype(np.float32)
    cbias = (cvalid - 1.0) + 1000.0 * forced.astype(np.float32)
    c["cvalid"] = np.ascontiguousarray(cvalid.reshape(16, 128, 32).transpose(1, 0, 2))
    c["cbias"] = np.ascontiguousarray(cbias.reshape(16, 128, 32).transpose(1, 0, 2))
    kpos = np.arange(T)
    kaux = np.stack([128.0 * (kpos // 128), (kpos % 128).astype(np.float64),
                     np.ones(T), np.ones(T)]).astype(np.float32)
    c["kaux"] = kaux.astype(BF)
    c["dkaux"] = np.stack([DSL[h] * kaux for h in range(4)]).astype(np.float32).astype(BF)
    qrel = np.arange(512)
    qa = (qrel // 128).astype(np.float32)
    qb = (qrel % 128).astype(np.float32)
    c["qaux_d"] = np.stack([np.ones(512), np.ones(512), -128.0 * qa, -qb]).astype(np.float32).astype(BF)
    c["qaux_n"] = np.stack([np.stack([np.full(512, s), np.full(512, s), -s * 128.0 * qa, -s * qb])
                            for s in NSL]).astype(np.float32).astype(BF)
    c["expand"] = (np.arange(T)[None, :] // 64 == np.arange(32)[:, None]).astype(np.float32).astype(BF)
    return c


def permute_w_in(w):
    NQ0, NKV, NZ, NG = 2048, 2560, 3328, 3840
    cols = []
    for h in range(4):
        cols += [w[:, h * 128:(h + 1) * 128], w[:, 512 + h * 128:512 + (h + 1) * 128],
                 w[:, 1024 + h * 128:1024 + (h + 1) * 128], w[:, 1536 + h * 128:1536 + (h + 1) * 128]]
    kv = lambda s, k: w[:, NKV + s * 128 + k * 64:NKV + s * 128 + (k + 1) * 64]
    cols += [kv(2, 0), kv(2, 1), kv(4, 0), kv(4, 1), kv(0, 0), kv(0, 0), kv(0, 1), kv(0, 1)]
    cols += [kv(1, 0), kv(1, 0), kv(1, 1), kv(1, 1), kv(3, 0), kv(3, 1), kv(5, 0), kv(5, 1)]
    cols += [w[:, NZ:NZ + 512]]
    for jj in range(4):
        cols += [w[:, NQ0 + jj * 64:NQ0 + (jj + 1) * 64], w[:, NQ0 + (4 + jj) * 64:NQ0 + (5 + jj) * 64]]
    cols += [w[:, NG:NG + 24], np.zeros((w.shape[0], 512 - 24), w.dtype)]
    out = np.ascontiguousarray(np.concatenate(cols, axis=1))
    assert out.shape[1] == NCOL
    return out


def build(nc, es, nb=NB, dbg=False):
    P = Prog(nc)
    total_sb = [0]

    def sb(name, shape, dt):
        n = 1
        for s in shape[1:]:
            n *= s
        total_sb[0] += n * (4 if dt == F32 else 2)
        return es.enter_context(nc.sbuf_tensor(name, shape, dt))

    def din(name, shape, dt):
        return nc.dram_tensor(name, shape, dt, kind="ExternalInput").ap()

    x = din("x", [nb, T, D], F32)
    y = nc.dram_tensor("y", [nb, T, D], F32, kind="ExternalOutput").ap()
    w_in = din("w_in", [D, NCOL], F32)
    w_out = din("w_out", [D, D], F32)
    w1_d = din("cmp_w1", [2, 2048, 256], F32)
    w2_d = din("cmp_w2", [2, 256, 64], F32)
    b1_d = din("cmp_b1", [2, 256], F32)
    pos_d = din("cmp_pos", [2, 32, 64], F32)
    normg_d = din("norm_g", [D], F32)
    dqg_d = din("diff_q_norm_g", [64], F32)
    dkg_d = din("diff_k_norm_g", [64], F32)
    lq1_d = din("diff_lambda_q1", [64], F32)
    lk1_d = din("diff_lambda_k1", [64], F32)
    lq2_d = din("diff_lambda_q2", [64], F32)
    lk2_d = din("diff_lambda_k2", [64], F32)
    subg_d = din("diff_subln_g", [128], F32)
    nqg_d = din("nsa_q_norm_g", [64], F32)
    nkg_d = din("nsa_k_norm_g", [3, 64], F32)
    c_ident = din("c_ident", [128, 128], BF16)
    c_blk = din("c_blkones", [128, 128], BF16)
    c_tric = din("c_tri_c", [128, 128], BF16)
    c_triw = din("c_tri_w", [128, 128], BF16)
    c_cmask = din("c_cmask", [128, T], BF16)
    c_ovl = din("c_ovl", [128, 33], BF16)
    c_cvalid = din("c_cvalid", [128, 16, 32], F32)
    c_cbias = din("c_cbias", [128, 16, 32], F32)
    c_kaux = din("c_kaux", [4, T], BF16)
    c_dkaux = din("c_dkaux", [4, 4, T], BF16)
    c_qauxd = din("c_qaux_d", [4, 512], BF16)
    c_qauxn = din("c_qaux_n", [8, 4, 512], BF16)
    c_expand = din("c_expand", [32, T], BF16)
    if dbg:
        dbg_mix = nc.dram_tensor("dbg_mix", [128, 16, 1024], BF16, kind="ExternalOutput").ap()

    banks = [es.enter_context(nc.psum_tensor("bank%d" % i, [128, 512], F32)) for i in range(8)]
    BK = [Buf("bank%d" % i, psum=True) for i in range(8)]
    pools = {"st": [0, 1, 2, 7], "acc": [3, 4, 5, 6], "big": [0, 1, 2, 7, 3, 4, 5, 6]}
    pptr = {"st": 0, "acc": 0, "big": 0}

    def nxt(pool):
        ids = pools[pool]
        i = ids[pptr[pool] % len(ids)]
        pptr[pool] += 1
        return i

    def bbf(i):
        return banks[i][:, :].bitcast(BF16)

    hT = sb("hT", [128, 8, T], BF16)
    HT = Buf("hT")
    mix = sb("mix", [128, 16, 1024], BF16)
    MIX = Buf("mix")
    wbuf = [sb("wbuf%d" % i, [128, 8, 512], BF16) for i in range(2)]
    WB = [Buf("wb%d" % i) for i in range(2)]
    wst = [sb("wst%d" % i, [128, 512], F32) for i in range(4)]
    WS = [Buf("ws%d" % i) for i in range(4)]
    xs = [sb("xs%d" % i, [128, 1024], F32) for i in range(2)]
    XS = [Buf("xs%d" % i) for i in range(2)]
    xn = sb("xn", [128, 1024], BF16)
    XN = Buf("xn")
    junk = sb("junk", [128, 1024], BF16)
    KSA = sb("KSA", [128, T], BF16)
    KSB = sb("KSB", [128, T], BF16)
    KWA = sb("KWA", [128, T], BF16)
    KWB = sb("KWB", [128, T], BF16)
    B_KSA, B_KSB, B_KWA, B_KWB = Buf("KSA"), Buf("KSB"), Buf("KWA"), Buf("KWB")
    QD = [[sb("QD%d%d" % (s, c), [128, 512], BF16) for c in range(2)] for s in range(2)]
    B_QD = [[Buf("QD%d%d" % (s, c)) for c in range(2)] for s in range(2)]
    Vd = sb("Vd", [128, 16, 129], BF16)
    B_Vd = Buf("Vd")
    Zd = sb("Zd", [128, 16, 128], BF16)
    B_Zd = Buf("Zd")
    QN = [sb("QN%d" % h, [128, 512], BF16) for h in range(8)]
    B_QN = [Buf("QN%d" % h) for h in range(8)]
    Vn = sb("Vn", [128, 16, 4, 65], BF16)
    B_Vn = Buf("Vn")
    Zq = sb("Zq", [128, 4, 512], BF16)
    B_Zq = Buf("Zq")
    Gt = sb("Gt", [128, 16, 24], F32)
    B_Gt = Buf("Gt")
    KV2 = sb("KV2", [128, T], BF16)
    B_KV2 = Buf("KV2")
    kc = [sb("kc%d" % k, [128, 128], BF16) for k in range(2)]
    B_kc = [Buf("kc%d" % k) for k in range(2)]
    vc = [sb("vc%d" % k, [128, 65], BF16) for k in range(2)]
    B_vc = [Buf("vc%d" % k) for k in range(2)]
    hid = sb("hid", [128, 2, 128], BF16)
    B_hid = Buf("hid")
    ON = sb("ON", [128, 4, 512], F32)
    B_ON = Buf("ON")
    IA = [sb("IA%d" % k, [128, 4, 32], F32) for k in range(2)]
    B_IA = [Buf("IA%d" % k) for k in range(2)]
    sc32 = sb("sc32", [128, 4, 32], F32)
    B_sc32 = Buf("sc32")
    m8 = sb("m8", [128, 4, 8], F32)
    B_m8 = Buf("m8")
    SBT = sb("SBT", [128, 4, 128], BF16)
    B_SBT = Buf("SBT")
    tmp64 = sb("tmp64", [128, 4, 64], F32)
    B_tmp64 = Buf("tmp64")
    tmp32 = sb("tmp32", [128, 4, 32], F32)
    B_tmp32 = Buf("tmp32")
    pt = [sb("pt%d" % i, [128, 512], BF16) for i in range(4)]
    PTB = [Buf("pt%d" % i) for i in range(4)]
    sqb = sb("sqb", [128, 512], BF16)
    B_sqb = Buf("sqb")
    lnb = sb("lnb", [128, 512], F32)
    B_lnb = Buf("lnb")
    rsb = sb("rsb", [128, 512], F32)
    B_rsb = Buf("rsb")
    t0 = sb("t0", [128, 4, 128], F32)
    t1 = sb("t1", [128, 4, 128], F32)
    B_t0, B_t1 = Buf("t0"), Buf("t1")
    small = sb("small", [128, 64], F32)
    B_small = Buf("small")
    mT = [sb("mT%d" % i, [128, 8, 128], BF16) for i in range(2)]
    B_mT = [Buf("mT%d" % i) for i in range(2)]
    ident = sb("ident", [128, 128], BF16)
    blkones = sb("blkones", [128, 128], BF16)
    tri_c = sb("tri_c", [128, 128], BF16)
    tri_w = sb("tri_w", [128, 128], BF16)
    cmask = sb("cmask", [128, T], BF16)
    ovl = sb("ovl", [128, 33], BF16)
    cvalid = sb("cvalid", [128, 16, 32], F32)
    cbias = sb("cbias", [128, 16, 32], F32)
    normg = sb("normg", [128, 8], F32)
    gcols = sb("gcols", [128, 8], F32)
    gsub = sb("gsub", [128, 128], F32)
    lamt = sb("lamt", [128, 4, 64], F32)
    lamc = sb("lamc", [128, 8], F32)
    w2k = [sb("w2k%d" % k, [128, 2, 128], BF16) for k in range(2)]
    w2v = sb("w2v", [128, 2, 64], BF16)
    w2st = sb("w2st", [128, 2, 2, 64], F32)
    b1c = sb("b1c", [128, 2, 2], F32)
    bias_h = sb("bias_h", [128, 2, 2], F32)
    posst = sb("posst", [128, 2, 16], F32)
    pos2 = sb("pos2", [128, 2, 16], BF16)
    CONST = Buf("const")

    sems_names = ["pe", "act", "dve", "pool", "sp"] + ["dma_sp_%d" % i for i in range(8)]
    sems = {n: es.enter_context(nc.semaphore(n)) for n in sems_names}

    def cload(dst, src, **kw):
        P.dma("sp", dst, src, writes=[CONST], **kw)

    cload(ident[:, :], c_ident)
    cload(blkones[:, :], c_blk)
    cload(tri_c[:, :], c_tric)
    cload(tri_w[:, :], c_triw)
    cload(cmask[:, :], c_cmask)
    cload(ovl[:, :], c_ovl)
    cload(cvalid[:, :, :], c_cvalid)
    cload(cbias[:, :, :], c_cbias)
    cload(normg[:, :], normg_d.rearrange("(m p) -> p m", p=128), allow_slow_non_contiguous=True)

    def colload(col, src):
        for half in range(2):
            cload(gcols[half * 64:(half + 1) * 64, col:col + 1], src.rearrange("(p o) -> p o", o=1))

    colload(0, dqg_d)
    colload(1, dkg_d)
    colload(2, nqg_d)
    for i in range(3):
        colload(3 + i, nkg_d[i, :])

    def bcast_rows(src1d, n):
        return bass.AP(tensor=src1d.tensor, offset=src1d.offset, ap=[[0, 128], [1, n]])

    cload(gsub[:, :], bcast_rows(subg_d, 128))
    for i, l in enumerate([lq1_d, lk1_d, lq2_d, lk2_d]):
        cload(lamt[:, i, :], bcast_rows(l, 64))
    for kv in range(2):
        cload(b1c[:, kv, :], b1_d[kv, :].rearrange("(jt j) -> j jt", j=128), allow_slow_non_contiguous=True)
        cload(w2st[:, kv, :, :], w2_d[kv, :, :].rearrange("(jt j) d -> j jt d", j=128))
        cload(posst[:, kv, :], pos_d[kv, :, :].rearrange("(l2 two) d -> (two d) l2", two=2),
              allow_slow_non_contiguous=True)

    def mz(eng, ap):
        P.op(eng, lambda e: e.memset(ap, 0.0), [], [CONST])

    def m1(eng, ap):
        P.op(eng, lambda e: e.memset(ap, 1.0), [], [CONST])

    for tl in (KSA, KSB, KWA, KWB):
        mz("pool", tl[:, :])
    for s in range(2):
        for c in range(2):
            mz("pool", QD[s][c][:, :])
    for h in range(8):
        mz("pool", QN[h][:, :])
    mz("pool", SBT[:, :, :])
    for k in range(2):
        mz("pool", kc[k][:, :])
        mz("pool", vc[k][:, :])
        mz("pool", w2k[k][:, :, :])
        m1("pool", vc[k][:, 64:65])
    mz("pool", hid[:, :, :])
    m1("pool", Vd[:, :, 128:129])
    m1("pool", Vn[:, :, :, 64:65])
    cload(KSA[96:100, :], c_kaux)
    cload(KSB[32:36, :], c_kaux)
    cload(KSA[64:96, :], c_expand)
    cload(KSB[0:32, :], c_expand)
    for s in range(2):
        cload(QD[s][0][96:100, :], c_qauxd)
        cload(QD[s][1][32:36, :], c_qauxd)
    for h in range(8):
        if h < 4:
            cload(QN[h][96:100, :], c_qauxn[h])
        else:
            cload(QN[h][32:36, :], c_qauxn[h])
    P.op("dve", lambda e: e.tensor_scalar(out=gcols[:, 0:1], in0=gcols[:, 0:1], scalar1=0.125, scalar2=None,
                                          op0=ALU.mult), [CONST], [CONST])
    P.op("dve", lambda e: e.tensor_scalar(out=gcols[:, 2:3], in0=gcols[:, 2:3], scalar1=0.125, scalar2=None,
                                          op0=ALU.mult), [CONST], [CONST])
    P.op("dve", lambda e: e.tensor_scalar(out=gsub[:, :], in0=gsub[:, :], scalar1=0.8, scalar2=None,
                                          op0=ALU.mult), [CONST], [CONST])
    P.op("dve", lambda e: e.tensor_tensor(out=lamt[:, 0, :], in0=lamt[:, 0, :], in1=lamt[:, 1, :], op=ALU.mult),
         [CONST], [CONST])
    P.op("dve", lambda e: e.tensor_tensor(out=lamt[:, 2, :], in0=lamt[:, 2, :], in1=lamt[:, 3, :], op=ALU.mult),
         [CONST], [CONST])
    P.op("dve", lambda e: e.reduce_sum(out=lamc[:, 0:1], in_=lamt[:, 0, :], axis=AX.X), [CONST], [CONST])
    P.op("dve", lambda e: e.reduce_sum(out=lamc[:, 1:2], in_=lamt[:, 2, :], axis=AX.X), [CONST], [CONST])
    P.op("act", lambda e: e.activation(out=lamc[:, 2:4], in_=lamc[:, 0:2], func=AF.Exp), [CONST], [CONST])
    P.op("dve", lambda e: e.scalar_tensor_tensor(out=lamc[:, 4:5], in0=lamc[:, 3:4], scalar=-0.2, in1=lamc[:, 2:3],
                                                 op0=ALU.add, op1=ALU.subtract), [CONST], [CONST])
    for k in range(2):
        P.op("dve", lambda e, k=k: e.tensor_copy(out=w2k[k][:, :, k * 64:(k + 1) * 64], in_=w2st[:, 0, :, :]),
             [CONST], [CONST])
    P.op("dve", lambda e: e.tensor_copy(out=w2v[:, :, :], in_=w2st[:, 1, :, :]), [CONST], [CONST])
    P.op("dve", lambda e: e.tensor_copy(out=pos2[:, :, :], in_=posst[:, :, :]), [CONST], [CONST])

    wctr = [0]
    stc = [0]

    def stage_cast(k, dst_ap, src_ap, scale_ap):
        s = stc[0] % 4
        stc[0] += 1
        P.dma("sp", wst[s][:, :], src_ap, writes=[WS[s]])
        if scale_ap is not None:
            P.op("dve", lambda e: e.tensor_scalar(out=dst_ap, in0=wst[s][:, :], scalar1=scale_ap, scalar2=None,
                                                  op0=ALU.mult), [WS[s], CONST], [WB[k]])
        else:
            P.op("dve", lambda e: e.tensor_copy(out=dst_ap, in_=wst[s][:, :]), [WS[s]], [WB[k]])

    def load_group(g):
        k = wctr[0] % 2
        wctr[0] += 1
        for m in range(8):
            stage_cast(k, wbuf[k][:, m, :], w_in[m * 128:(m + 1) * 128, g * 512:(g + 1) * 512], normg[:, m:m + 1])
        return k

    def load_wout(n):
        k = wctr[0] % 2
        wctr[0] += 1
        for m in range(8):
            stage_cast(k, wbuf[k][:, m, :], w_out[m * 128:(m + 1) * 128, n * 512:(n + 1) * 512], None)
        return k

    def load_w1(kv):
        k = wctr[0] % 2
        wctr[0] += 1
        src = w1_d[kv, :, :].rearrange("(c p) j -> p c j", p=128)
        wv = wbuf[k][:, :, :].rearrange("p m n -> p (m n)").rearrange("p (c j) -> p c j", j=256)
        for i in range(8):
            s = stc[0] % 4
            stc[0] += 1
            P.dma("sp", wst[s][:, :].rearrange("p (c j) -> p c j", j=256), src[:, 2 * i:2 * i + 2, :], writes=[WS[s]])
            P.op("dve", lambda e, s=s, i=i: e.tensor_copy(out=wv[:, 2 * i:2 * i + 2, :],
                                                          in_=wst[s][:, :].rearrange("p (c j) -> p c j", j=256)),
                 [WS[s]], [WB[k]])
        return k, wv

    for kv in range(2):
        k, wv = load_w1(kv)
        bk = nxt("st")
        for jt in range(2):
            for l2 in range(16):
                P.op("pe", lambda e, jt=jt, l2=l2, wv=wv, kv=kv, bk=bk: e.matmul(
                    banks[bk][:, jt:jt + 1], lhsT=wv[:, l2, jt * 128:(jt + 1) * 128], rhs=pos2[:, kv, l2:l2 + 1],
                    start=(l2 == 0 and jt == 0), stop=True, skip_group_check=True), [WB[k], CONST], [BK[bk]])
        P.op("dve", lambda e, kv=kv, bk=bk: e.tensor_tensor(out=bias_h[:, kv, :], in0=banks[bk][:, 0:2],
                                                            in1=b1c[:, kv, :], op=ALU.add), [BK[bk], CONST], [CONST])

    def qknorm(pb, n, gcol, dsts, pool="st"):
        P.op("act", lambda e: e.activation(out=sqb[:, :n], in_=banks[pb][:, :n], func=AF.Square), [BK[pb]], [B_sqb])
        mb = nxt(pool)
        P.op("pe", lambda e: e.matmul(banks[mb][:, :n], lhsT=blkones[:, :], rhs=sqb[:, :n], start=True, stop=True),
             [B_sqb, CONST], [BK[mb]])
        P.op("act", lambda e: e.activation(out=lnb[:, :n], in_=banks[mb][:, :n], func=AF.Ln, bias=EPS),
             [BK[mb]], [B_lnb])
        P.op("act", lambda e: e.activation(out=rsb[:, :n], in_=lnb[:, :n], func=AF.Exp, scale=-0.5),
             [B_lnb], [B_rsb])
        for (dst, lo, hi, db) in dsts:
            P.op("dve", lambda e, dst=dst, lo=lo, hi=hi: e.scalar_tensor_tensor(
                out=dst, in0=banks[pb][lo:hi, :n], scalar=gcol[lo:hi, 0:1], in1=rsb[lo:hi, :n],
                op0=ALU.mult, op1=ALU.mult), [BK[pb], B_rsb, CONST], [db])

    def proj_fm(k, c0, tq, pb, ncols=512):
        for m in range(8):
            P.op("pe", lambda e, m=m: e.matmul(banks[pb][:, :ncols], lhsT=wbuf[k][:, m, c0:c0 + 128],
                                               rhs=hT[:, m, tq * 512:tq * 512 + ncols], start=(m == 0), stop=(m == 7)),
                 [WB[k], HT], [BK[pb]])

    def proj_tm(k, c0, n, tt, out_ap, pb, first):
        for m in range(8):
            P.op("pe", lambda e, m=m: e.matmul(out_ap, lhsT=hT[:, m, tt * 128:(tt + 1) * 128],
                                               rhs=wbuf[k][:, m, c0:c0 + n], start=(first and m == 0), stop=True,
                                               skip_group_check=True), [WB[k], HT], [BK[pb]])

    pend = deque()
    npv = [0]
    DEPTH = 3
    ptc = [0]

    def drain(limit):
        while pend and npv[0] > limit:
            kind, fn = pend.popleft()
            if kind == "pv":
                npv[0] -= 1
            fn()
        while pend and pend[0][0] == "fin":
            pend.popleft()[1]()

    def flush():
        while pend:
            kind, fn = pend.popleft()
            if kind == "pv":
                npv[0] -= 1
            fn()

    def attention(tiles, Qt, QB, KB, VB, acc_ap, acc_bank, bias, fin):
        started = set()
        for d in tiles:
            sbk = nxt("st")
            c0, c1 = d["c0"], d["c1"]
            ex = d.get("extra")
            P.op("pe", lambda e, d=d, sbk=sbk, c0=c0, c1=c1, ex=ex: e.matmul(
                banks[sbk][:, c0:c1], lhsT=d["k_ap"], rhs=Qt[:, c0:c1], start=True, stop=(ex is None)),
                [KB, QB], [BK[sbk]])
            if ex is not None:
                P.op("pe", lambda e, sbk=sbk, ex=ex: e.matmul(
                    banks[sbk][:, ex[1]:ex[1] + ex[2]], lhsT=ident[:, :], rhs=ex[0], start=False, stop=True),
                    [CONST], [BK[sbk]])
            pi = ptc[0] % 4
            ptc[0] += 1
            P.op("act", lambda e, sbk=sbk, pi=pi, c0=c0, c1=c1: e.activation(
                out=pt[pi][:, c0:c1], in_=banks[sbk][:, c0:c1], func=AF.Exp, bias=float(bias)),
                [BK[sbk]], [PTB[pi]])

            def pv(d=d, pi=pi, c0=c0, c1=c1):
                for s in range(c0 // 128, c1 // 128):
                    bk = acc_bank(s)
                    first = bk not in started
                    started.add(bk)
                    P.op("pe", lambda e, s=s, first=first: e.matmul(
                        acc_ap(s), lhsT=pt[pi][:, s * 128:(s + 1) * 128], rhs=d["v_ap"], start=first, stop=True,
                        skip_group_check=True), [PTB[pi], VB], [BK[bk]])
                if d.get("more") is not None:
                    d["more"](pi)
            pend.append(("pv", pv))
            npv[0] += 1
            drain(DEPTH)
        pend.append(("fin", fin))

    def causal_tiles(Kt, qi, vfn):
        tiles = []
        for kt in range(4 * qi):
            tiles.append(dict(k_ap=Kt[:, kt * 128:(kt + 1) * 128], c0=0, c1=512, v_ap=vfn(kt)))
        for j in range(4):
            kt = 4 * qi + j
            tiles.append(dict(k_ap=Kt[:, kt * 128:(kt + 1) * 128], c0=128 * j, c1=512,
                              extra=(tri_c[:, :], 128 * j, 128), v_ap=vfn(kt)))
        return tiles

    def window_tiles(Kt, qi, vfn):
        tiles = []
        if qi > 0:
            for j in range(4):
                kt = 4 * (qi - 1) + j
                tiles.append(dict(k_ap=Kt[:, kt * 128:(kt + 1) * 128], c0=0, c1=128 * (j + 1),
                                  extra=(tri_w[:, :], 128 * j, 128), v_ap=vfn(kt)))
        for j in range(4):
            kt = 4 * qi + j
            tiles.append(dict(k_ap=Kt[:, kt * 128:(kt + 1) * 128], c0=128 * j, c1=512,
                              extra=(tri_c[:, :], 128 * j, 128), v_ap=vfn(kt)))
        return tiles

    def phaseA(b):
        for tt in range(16):
            xi = tt % 2
            P.dma("sp", xs[xi][:, :], x[b, tt * 128:(tt + 1) * 128, :], writes=[XS[xi]])
            P.op("act", lambda e, xi=xi: e.activation(out=junk[:, :], in_=xs[xi][:, :], func=AF.Square,
                                                      accum_out=small[:, 0:1]), [XS[xi]], [B_small])
            P.op("act", lambda e: e.activation(out=small[:, 1:2], in_=small[:, 0:1], func=AF.Ln, scale=1.0 / D,
                                               bias=EPS), [B_small], [B_small])
            P.op("act", lambda e: e.activation(out=small[:, 2:3], in_=small[:, 1:2], func=AF.Exp, scale=-0.5),
                 [B_small], [B_small])
            P.op("dve", lambda e, xi=xi: e.tensor_scalar(out=xn[:, :], in0=xs[xi][:, :], scalar1=small[:, 2:3],
                                                         scalar2=None, op0=ALU.mult), [XS[xi], B_small], [XN])
            bk = nxt("st")
            for m in range(8):
                P.op("pe", lambda e, m=m, bk=bk: e.transpose(out=bbf(bk)[:, m * 128:(m + 1) * 128],
                                                             in_=xn[:, m * 128:(m + 1) * 128], identity=ident[:, :]),
                     [XN, CONST], [BK[bk]])
            P.op("dve", lambda e, tt=tt, bk=bk: e.tensor_copy(
                out=hT[:, :, tt * 128:(tt + 1) * 128],
                in_=bbf(bk)[:, 0:1024].rearrange("p (m t) -> p m t", t=128)), [BK[bk]], [HT])


    for b in range(nb):
        k_cur = load_group(0)
        if b == 0:
            phaseA(0)
        KA, KB_, B_KA, B_KB = KWA, KWB, B_KWA, B_KWB
        for h in range(4):
            flush()
            k = k_cur
            P.dma("sp", KA[96:100, :], c_dkaux[h], writes=[B_KA])
            P.dma("sp", KB_[32:36, :], c_dkaux[h], writes=[B_KB])
            for tq in range(4):
                pb = nxt("big")
                proj_fm(k, 128, tq, pb)
                qknorm(pb, 512, gcols[:, 1:2],
                       [(KA[0:64, tq * 512:(tq + 1) * 512], 0, 64, B_KA),
                        (KB_[64:128, tq * 512:(tq + 1) * 512], 64, 128, B_KB)], pool="big")
            for t4 in range(4):
                pb = nxt("big")
                for i in range(4):
                    proj_tm(k, 256, 128, t4 * 4 + i, banks[pb][:, i * 128:(i + 1) * 128], pb, first=(i == 0))
                P.op("dve", lambda e, t4=t4, pb=pb: e.tensor_copy(
                    out=Vd[:, t4 * 4:(t4 + 1) * 4, 0:128],
                    in_=banks[pb][:, :].rearrange("p (i c) -> p i c", c=128)), [BK[pb]], [B_Vd])
            for t4 in range(4):
                pb = nxt("big")
                for i in range(4):
                    proj_tm(k, 384, 128, t4 * 4 + i, banks[pb][:, i * 128:(i + 1) * 128], pb, first=(i == 0))
                P.op("act", lambda e, t4=t4, pb=pb: e.activation(
                    out=Zd[:, t4 * 4:(t4 + 1) * 4, :], in_=banks[pb][:, :].rearrange("p (i c) -> p i c", c=128),
                    func=AF.Silu), [BK[pb]], [B_Zd])
                P.op("dve", lambda e, t4=t4: e.tensor_tensor(
                    out=Zd[:, t4 * 4:(t4 + 1) * 4, :], in0=Zd[:, t4 * 4:(t4 + 1) * 4, :],
                    in1=gsub[:, :].unsqueeze(1).broadcast_to([128, 4, 128]), op=ALU.mult), [B_Zd, CONST], [B_Zd])
            k_cur = load_group(h + 1 if h < 3 else 4)

            def qproj(qi_, k=k):
                qs_ = qi_ % 2
                pb_ = nxt("st")
                proj_fm(k, 0, qi_, pb_)
                qknorm(pb_, 512, gcols[:, 0:1], [(QD[qs_][0][0:64, :], 0, 64, B_QD[qs_][0]),
                                                 (QD[qs_][1][64:128, :], 64, 128, B_QD[qs_][1])])

            qproj(0)
            for qi in range(4):
                qs = qi % 2
                if qi < 3:
                    qproj(qi + 1)
                for c in range(2):
                    Kt, KBf = (KA, B_KA) if c == 0 else (KB_, B_KB)
                    bx = nxt("acc")
                    by = nxt("acc")
                    tt_ = t0 if c == 0 else t1
                    Bt_ = B_t0 if c == 0 else B_t1

                    def acc_ap(s, bx=bx, by=by):
                        return banks[bx][:, s * 129:(s + 1) * 129] if s < 3 else banks[by][:, 0:129]

                    def acc_bank(s, bx=bx, by=by):
                        return bx if s < 3 else by

                    def fin(c=c, bx=bx, by=by, tt_=tt_, Bt_=Bt_, qi=qi, h=h):
                        xv = banks[bx][:, 0:387].rearrange("p (s c) -> p s c", c=129)
                        P.op("dve", lambda e: e.tensor_copy(out=small[:, 8:11], in_=xv[:, :, 128]),
                             [BK[bx]], [B_small])
                        P.op("dve", lambda e: e.tensor_copy(out=small[:, 11:12], in_=banks[by][:, 128:129]),
                             [BK[by]], [B_small])
                        P.op("dve", lambda e: e.reciprocal(out=small[:, 12:16], in_=small[:, 8:12]),
                             [B_small], [B_small])
                        P.op("dve", lambda e: e.tensor_tensor(
                            out=tt_[:, 0:3, :], in0=xv[:, :, 0:128],
                            in1=small[:, 12:15].unsqueeze(2).broadcast_to([128, 3, 128]), op=ALU.mult),
                            [BK[bx], B_small], [Bt_])
                        P.op("dve", lambda e: e.tensor_scalar(
                            out=tt_[:, 3, :], in0=banks[by][:, 0:128], scalar1=small[:, 15:16], scalar2=None,
                            op0=ALU.mult), [BK[by], B_small], [Bt_])
                        if c == 1:
                            P.op("dve", lambda e: e.scalar_tensor_tensor(
                                out=t0[:, :, :], in0=t1[:, :, :], scalar=lamc[:, 4:5], in1=t0[:, :, :],
                                op0=ALU.mult, op1=ALU.add), [B_t0, B_t1, CONST], [B_t0])
                            P.op("dve", lambda e: e.tensor_tensor(out=t1[:, :, :], in0=t0[:, :, :], in1=t0[:, :, :],
                                                                  op=ALU.mult), [B_t0], [B_t1])
                            P.op("dve", lambda e: e.reduce_sum(out=small[:, 16:20], in_=t1[:, :, :], axis=AX.X),
                                 [B_t1], [B_small])
                            P.op("act", lambda e: e.activation(out=small[:, 20:24], in_=small[:, 16:20], func=AF.Ln,
                                                               scale=1.0 / 128, bias=EPS), [B_small], [B_small])
                            P.op("act", lambda e: e.activation(out=small[:, 24:28], in_=small[:, 20:24], func=AF.Exp,
                                                               scale=-0.5), [B_small], [B_small])
                            P.op("dve", lambda e: e.tensor_tensor(
                                out=t0[:, :, :], in0=t0[:, :, :],
                                in1=small[:, 24:28].unsqueeze(2).broadcast_to([128, 4, 128]), op=ALU.mult),
                                [B_t0, B_small], [B_t0])
                            P.op("dve", lambda e: e.tensor_tensor(
                                out=mix[:, qi * 4:(qi + 1) * 4, h * 128:(h + 1) * 128], in0=t0[:, :, :],
                                in1=Zd[:, qi * 4:(qi + 1) * 4, :], op=ALU.mult), [B_t0, B_Zd], [MIX])

                    attention(causal_tiles(Kt, qi, lambda kt: Vd[:, kt, 0:129]), QD[qs][c], B_QD[qs][c], KBf, B_Vd,
                              acc_ap, acc_bank, -DSL[h] * 512.0 * qi, fin)
        flush()

        P.dma("sp", KWA[96:100, :], c_kaux, writes=[B_KWA])
        P.dma("sp", KWB[32:36, :], c_kaux, writes=[B_KWB])
        k = k_cur
        kw1, wv1 = load_w1(0)
        for tq in range(4):
            pb = nxt("big")
            proj_fm(k, 0, tq, pb)
            qknorm(pb, 512, gcols[:, 4:5], [(KSA[0:64, tq * 512:(tq + 1) * 512], 0, 64, B_KSA),
                                            (KSB[64:128, tq * 512:(tq + 1) * 512], 64, 128, B_KSB)], pool="big")
            pb = nxt("big")
            proj_fm(k, 128, tq, pb)
            qknorm(pb, 512, gcols[:, 5:6], [(KWA[0:64, tq * 512:(tq + 1) * 512], 0, 64, B_KWA),
                                            (KWB[64:128, tq * 512:(tq + 1) * 512], 64, 128, B_KWB)], pool="big")

        def kv2_fill(kw, c0):
            for tq in range(4):
                pb = nxt("big")
                proj_fm(kw, c0, tq, pb)
                P.op("dve", lambda e, tq=tq, pb=pb: e.tensor_copy(out=KV2[0:64, tq * 512:(tq + 1) * 512],
                                                                  in_=banks[pb][0:64, :]), [BK[pb]], [B_KV2])
                if tq == 0:
                    P.op("dve", lambda e, pb=pb: e.tensor_copy(out=KV2[64:128, 0:511], in_=banks[pb][64:128, 1:512]),
                         [BK[pb]], [B_KV2])
                else:
                    P.op("dve", lambda e, tq=tq, pb=pb: e.tensor_copy(
                        out=KV2[64:128, tq * 512 - 1:tq * 512 + 511], in_=banks[pb][64:128, :]), [BK[pb]], [B_KV2])

        def compress_hidden(kw1, wv, kv):
            for jt in range(2):
                pb = nxt("big")
                for l2 in range(16):
                    P.op("pe", lambda e, l2=l2, jt=jt, pb=pb: e.matmul(
                        banks[pb][:, 0:127], lhsT=wv[:, l2, jt * 128:(jt + 1) * 128],
                        rhs=KV2[:, 2 * l2:2 * l2 + 16 * 126 + 1:16], start=(l2 == 0), stop=(l2 == 15)),
                        [WB[kw1], B_KV2], [BK[pb]])
                P.op("act", lambda e, jt=jt, pb=pb: e.activation(out=hid[:, jt, 0:127], in_=banks[pb][:, 0:127],
                                                                 func=AF.Silu, bias=bias_h[:, kv, jt:jt + 1]),
                     [BK[pb], CONST], [B_hid])

        for kvh in range(2):
            kv2_fill(k, 256 + kvh * 128)
            if kvh == 1:
                k5 = load_group(5)
            compress_hidden(kw1, wv1, 0)
            pb = nxt("big")
            for jt in range(2):
                P.op("pe", lambda e, jt=jt, pb=pb, kvh=kvh: e.matmul(
                    banks[pb][:, 0:127], lhsT=w2k[kvh][:, jt, :], rhs=hid[:, jt, 0:127], start=(jt == 0),
                    stop=(jt == 1)), [B_hid, CONST], [BK[pb]])
            qknorm(pb, 127, gcols[:, 3:4], [(kc[kvh][0:64, 0:127], 0, 64, B_kc[kvh]),
                                            (kc[kvh][64:128, 0:127], 64, 128, B_kc[kvh])], pool="big")
        k = k5
        kw1, wv1 = load_w1(1)
        for kvh in range(2):
            kv2_fill(k, kvh * 128)
            compress_hidden(kw1, wv1, 1)
            pb = nxt("big")
            for jt in range(2):
                P.op("pe", lambda e, jt=jt, pb=pb: e.matmul(
                    banks[pb][0:127, 0:64], lhsT=hid[:, jt, 0:127], rhs=w2v[:, jt, :], start=(jt == 0),
                    stop=(jt == 1)), [B_hid, CONST], [BK[pb]])
            P.op("dve", lambda e, pb=pb, kvh=kvh: e.tensor_copy(out=vc[kvh][0:127, 0:64], in_=banks[pb][0:127, 0:64]),
                 [BK[pb]], [B_vc[kvh]])
        for t2 in range(8):
            pb = nxt("big")
            for i in range(2):
                proj_tm(k, 256, 256, t2 * 2 + i, banks[pb][:, i * 256:(i + 1) * 256], pb, first=(i == 0))
            P.op("dve", lambda e, t2=t2, pb=pb: e.tensor_copy(
                out=Vn[:, t2 * 2:t2 * 2 + 2, :, 0:64],
                in_=banks[pb][:, :].rearrange("p (i g c) -> p i g c", g=4, c=64)), [BK[pb]], [B_Vn])
        k = load_group(8)
        pb = nxt("big")
        for tt in range(16):
            proj_tm(k, 0, 24, tt, banks[pb][:, tt * 24:(tt + 1) * 24], pb, first=(tt == 0))
        P.op("act", lambda e, pb=pb: e.activation(out=Gt[:, :, :],
                                                  in_=banks[pb][:, 0:384].rearrange("p (t g) -> p t g", g=24),
                                                  func=AF.Sigmoid), [BK[pb]], [B_Gt])
        kq = load_group(7)
        kz = load_group(6)
        for qi in range(4):
            for i in range(4):
                pb = nxt("st")
                proj_tm(kz, 0, 512, qi * 4 + i, banks[pb][:, :], pb, first=True)
                P.op("act", lambda e, i=i, pb=pb: e.activation(out=Zq[:, i, :], in_=banks[pb][:, :], func=AF.Silu),
                     [BK[pb]], [B_Zq])
            for j in range(4):
                pb = nxt("st")
                proj_fm(kq, j * 128, qi, pb)
                qknorm(pb, 512, gcols[:, 2:3], [(QN[j][0:64, :], 0, 64, B_QN[j]),
                                                (QN[4 + j][64:128, :], 64, 128, B_QN[4 + j])])

            if qi == 3:
                ko = [load_wout(0), load_wout(1)]
                if b + 1 < nb:
                    phaseA(b + 1)

            def nsa_fin_factory(hd, br, ab, qi=qi):
                def fin():
                    av = banks[ab][:, 0:260].rearrange("p (s c) -> p s c", c=65)
                    o = 32 + (hd % 2) * 16
                    P.op("dve", lambda e: e.tensor_scalar(out=small[:, o:o + 4], in0=av[:, :, 64], scalar1=1e-30,
                                                          scalar2=None, op0=ALU.max), [BK[ab]], [B_small])
                    P.op("dve", lambda e: e.reciprocal(out=small[:, o + 4:o + 8], in_=small[:, o:o + 4]),
                         [B_small], [B_small])
                    P.op("dve", lambda e: e.tensor_tensor(out=small[:, o + 8:o + 12], in0=small[:, o + 4:o + 8],
                                                          in1=Gt[:, qi * 4:(qi + 1) * 4, br * 8 + hd], op=ALU.mult),
                         [B_small, B_Gt], [B_small])
                    cf = small[:, o + 8:o + 12].unsqueeze(2).broadcast_to([128, 4, 64])
                    if br == 0:
                        P.op("dve", lambda e: e.tensor_tensor(out=ON[:, :, hd * 64:(hd + 1) * 64],
                                                              in0=av[:, :, 0:64], in1=cf, op=ALU.mult),
                             [BK[ab], B_small], [B_ON])
                    else:
                        P.op("dve", lambda e: e.tensor_tensor(out=tmp64[:, :, :], in0=av[:, :, 0:64], in1=cf,
                                                              op=ALU.mult), [BK[ab], B_small], [B_tmp64])
                        P.op("dve", lambda e: e.tensor_tensor(out=ON[:, :, hd * 64:(hd + 1) * 64],
                                                              in0=ON[:, :, hd * 64:(hd + 1) * 64], in1=tmp64[:, :, :],
                                                              op=ALU.add), [B_tmp64, B_ON], [B_ON])
                return fin

            for hd in range(8):
                kvh = hd // 4
                ab = nxt("acc")
                ib = nxt("acc")

                def more(pi, ib=ib):
                    for s in range(4):
                        P.op("pe", lambda e, s=s: e.matmul(
                            banks[ib][:, s * 33:(s + 1) * 33], lhsT=pt[pi][:, s * 128:(s + 1) * 128], rhs=ovl[:, :],
                            start=(s == 0), stop=True, skip_group_check=True), [PTB[pi], CONST], [BK[ib]])

                base_fin = nsa_fin_factory(hd, 0, ab)

                def fin(base_fin=base_fin, ib=ib, hd=hd, kvh=kvh):
                    base_fin()
                    iv = banks[ib][:, 0:132].rearrange("p (s c) -> p s c", c=33)
                    P.op("dve", lambda e: e.tensor_scalar(out=small[:, 28:32], in0=iv[:, :, 32], scalar1=1e-30,
                                                          scalar2=None, op0=ALU.max), [BK[ib]], [B_small])
                    P.op("dve", lambda e: e.reciprocal(out=small[:, 4:8], in_=small[:, 28:32]), [B_small], [B_small])
                    rb = small[:, 4:8].unsqueeze(2).broadcast_to([128, 4, 32])
                    if hd % 4 == 0:
                        P.op("dve", lambda e: e.tensor_tensor(out=IA[kvh][:, :, :], in0=iv[:, :, 0:32], in1=rb,
                                                              op=ALU.mult), [BK[ib], B_small], [B_IA[kvh]])
                    else:
                        P.op("dve", lambda e: e.tensor_tensor(out=tmp32[:, :, :], in0=iv[:, :, 0:32], in1=rb,
                                                              op=ALU.mult), [BK[ib], B_small], [B_tmp32])
                        P.op("dve", lambda e: e.tensor_tensor(out=IA[kvh][:, :, :], in0=IA[kvh][:, :, :],
                                                              in1=tmp32[:, :, :], op=ALU.add),
                             [B_tmp32, B_IA[kvh]], [B_IA[kvh]])

                tiles = [dict(k_ap=kc[kvh][:, :], c0=0, c1=512, extra=(cmask[:, qi * 512:(qi + 1) * 512], 0, 512),
                              v_ap=vc[kvh][:, 0:65], more=more)]
                attention(tiles, QN[hd], B_QN[hd], B_kc[kvh], B_vc[kvh],
                          lambda s, ab=ab: banks[ab][:, s * 65:(s + 1) * 65], lambda s, ab=ab: ab, 0.0, fin)
            flush()
            for kvh in range(2):
                off = 64 if kvh == 0 else 0
                P.op("dve", lambda e, kvh=kvh, qi=qi: e.tensor_tensor(out=sc32[:, :, :], in0=IA[kvh][:, :, :],
                                                                      in1=cvalid[:, qi * 4:(qi + 1) * 4, :], op=ALU.mult),
                     [B_IA[kvh], CONST], [B_sc32])
                P.op("dve", lambda e, qi=qi: e.tensor_tensor(out=sc32[:, :, :], in0=sc32[:, :, :],
                                                             in1=cbias[:, qi * 4:(qi + 1) * 4, :], op=ALU.add),
                     [B_sc32, CONST], [B_sc32])
                for s in range(4):
                    P.op("dve", lambda e, s=s: e.max(out=m8[:, s, :], in_=sc32[:, s, :]), [B_sc32], [B_m8])
                for s in range(4):
                    P.op("dve", lambda e, s=s, off=off: e.tensor_scalar(
                        out=SBT[:, s, off:off + 32], in0=sc32[:, s, :], scalar1=m8[:, s, 7:8], scalar2=NEG,
                        op0=ALU.is_lt, op1=ALU.mult), [B_sc32, B_m8], [B_SBT])
            for hd in range(8):
                kvh = hd // 4
                Kw, BKw = (KWA, B_KWA) if kvh == 0 else (KWB, B_KWB)
                ab = nxt("acc")
                attention(window_tiles(Kw, qi, lambda kt, kvh=kvh: Vn[:, kt, 2 + kvh, :]), QN[hd], B_QN[hd], BKw, B_Vn,
                          lambda s, ab=ab: banks[ab][:, s * 65:(s + 1) * 65], lambda s, ab=ab: ab,
                          -NSL[hd] * 512.0 * qi, nsa_fin_factory(hd, 2, ab))
            tb = nxt("st")
            for s in range(4):
                P.op("pe", lambda e, s=s, tb=tb: e.transpose(out=bbf(tb)[:, s * 128:(s + 1) * 128], in_=SBT[:, s, :],
                                                             identity=ident[:, :]), [B_SBT, CONST], [BK[tb]])
            for j in range(4):
                P.op("dve", lambda e, j=j, tb=tb: e.tensor_copy(out=QN[j][64:96, :], in_=bbf(tb)[64:96, 0:512]),
                     [BK[tb]], [B_QN[j]])
                P.op("dve", lambda e, j=j, tb=tb: e.tensor_copy(out=QN[4 + j][0:32, :], in_=bbf(tb)[0:32, 0:512]),
                     [BK[tb]], [B_QN[4 + j]])
            for hd in range(8):
                kvh = hd // 4
                Ks, BKs = (KSA, B_KSA) if kvh == 0 else (KSB, B_KSB)
                ab = nxt("acc")
                attention(causal_tiles(Ks, qi, lambda kt, kvh=kvh: Vn[:, kt, kvh, :]), QN[hd], B_QN[hd], BKs, B_Vn,
                          lambda s, ab=ab: banks[ab][:, s * 65:(s + 1) * 65], lambda s, ab=ab: ab,
                          -NSL[hd] * 512.0 * qi, nsa_fin_factory(hd, 1, ab))
            flush()
            P.op("dve", lambda e, qi=qi: e.tensor_tensor(out=mix[:, qi * 4:(qi + 1) * 4, 512:1024], in0=ON[:, :, :],
                                                         in1=Zq[:, :, :], op=ALU.mult), [B_ON, B_Zq], [MIX])

        if dbg and b == 0:
            P.dma("sp", dbg_mix, mix[:, :, :], reads=[MIX])

        def prep_mT(tt_):
            mi_ = tt_ % 2
            tb_ = nxt("st")
            for c_ in range(8):
                P.op("pe", lambda e, c_=c_, tb_=tb_, tt_=tt_: e.transpose(
                    out=bbf(tb_)[:, c_ * 128:(c_ + 1) * 128], in_=mix[:, tt_, c_ * 128:(c_ + 1) * 128],
                    identity=ident[:, :]), [MIX, CONST], [BK[tb_]])
            P.op("dve", lambda e, mi_=mi_, tb_=tb_: e.tensor_copy(
                out=mT[mi_][:, :, :], in_=bbf(tb_)[:, 0:1024].rearrange("p (c t) -> p c t", t=128)),
                [BK[tb_]], [B_mT[mi_]])

        prep_mT(0)
        for tt in range(16):
            mi = tt % 2
            if tt < 15:
                prep_mT(tt + 1)
            xi = tt % 2
            P.dma("sp", xs[xi][:, :], x[b, tt * 128:(tt + 1) * 128, :], writes=[XS[xi]])
            for n in range(2):
                ob = nxt("acc")
                for c in range(8):
                    P.op("pe", lambda e, c=c, n=n, ob=ob, mi=mi, ko=ko: e.matmul(
                        banks[ob][:, :], lhsT=mT[mi][:, c, :], rhs=wbuf[ko[n]][:, c, :], start=(c == 0),
                        stop=(c == 7)), [B_mT[mi], WB[ko[n]]], [BK[ob]])
                P.op("dve", lambda e, n=n, ob=ob, xi=xi: e.tensor_tensor(
                    out=xs[xi][:, n * 512:(n + 1) * 512], in0=banks[ob][:, :], in1=xs[xi][:, n * 512:(n + 1) * 512],
                    op=ALU.add), [BK[ob], XS[xi]], [XS[xi]])
            P.dma("sp", y[b, tt * 128:(tt + 1) * 128, :], xs[xi][:, :], reads=[XS[xi]])

    fin_op = P.op("sp", lambda e: e.nop(), [], [])
    for d in P.dma_ops["sp"][-16:]:
        P._add_dep(fin_op, d)
    nw = P.emit(sems)
    return dict(nops=P.nops, nwaits=nw, sbuf_bytes=total_sb[0])


_CACHE = {}


def kernel(x, norm_g, w_in, diff_q_norm_g, diff_k_norm_g, diff_lambda_q1, diff_lambda_k1, diff_lambda_q2,
           diff_lambda_k2, diff_subln_g, nsa_q_norm_g, nsa_k_norm_g, cmp_pos, cmp_w1, cmp_b1, cmp_w2, w_out,
           _dbg=False, _nb=NB, _ncores=NCORES):
    f = lambda a: np.ascontiguousarray(np.asarray(a, dtype=np.float32))
    x = f(x)
    consts = make_consts()
    shared = {
        "w_in": permute_w_in(f(w_in)[0]), "w_out": f(w_out)[0], "cmp_w1": f(cmp_w1)[0], "cmp_w2": f(cmp_w2)[0],
        "cmp_b1": f(cmp_b1)[0], "cmp_pos": f(cmp_pos)[0], "norm_g": f(norm_g)[0],
        "diff_q_norm_g": f(diff_q_norm_g)[0], "diff_k_norm_g": f(diff_k_norm_g)[0],
        "diff_lambda_q1": f(diff_lambda_q1)[0], "diff_lambda_k1": f(diff_lambda_k1)[0],
        "diff_lambda_q2": f(diff_lambda_q2)[0], "diff_lambda_k2": f(diff_lambda_k2)[0],
        "diff_subln_g": f(diff_subln_g)[0], "nsa_q_norm_g": f(nsa_q_norm_g)[0], "nsa_k_norm_g": f(nsa_k_norm_g)[0],
    }
    for k_, v in consts.items():
        shared["c_" + k_] = v
    nc = bass.Bass("TRN2", target_bir_lowering=False)
    with ExitStack() as es:
        info = build(nc, es, nb=_nb, dbg=_dbg)
    in_maps = []
    for c in range(_ncores):
        m = dict(shared)
        m["x"] = np.ascontiguousarray(x[c * _nb:(c + 1) * _nb])
        in_maps.append(m)
    res = run_bass_kernel_spmd(nc, in_maps, core_ids=list(range(_ncores)))
    out = np.concatenate([np.asarray(r["y"], dtype=np.float32) for r in res.results], axis=0)
    if _dbg:
        return out, res.results[0]["dbg_mix"], info
    return out
```
